# Optimizing a Trainium2 kernel written in Bass

```python
import jax
import jax.numpy as jnp
from jax import lax
import numpy as np

D_MODEL = 1024
BATCH = 16
SEQ = 2048
DEPTH = 4

CTX_LEN = 256
GRID_W = 64
HEAD_DIM = 64
BRANCH_WIDTH = 512
N_BRANCH = 3
A_HEADS = 8
A_KV_HEADS = 2
WINDOW = 128
B_HEADS = 8
B_KV_HEADS = 2
C_HEADS = 4
C_DK = 128
C_DV = 128
CHUNK = 64
BLOCK = 128
D_FF = 2816
CONV_W = 3
ROPE_THETA = 10000.0
AXIS_ROPE_DIM = HEAD_DIM // 2
N_MOD = 6
EPS = 1e-6
NEG_BIG = -1e30
PROJ_SIZES = (A_HEADS * HEAD_DIM, A_KV_HEADS * HEAD_DIM, A_KV_HEADS * HEAD_DIM,
              B_HEADS * HEAD_DIM, B_KV_HEADS * HEAD_DIM, B_KV_HEADS * HEAD_DIM,
              C_HEADS * C_DK, C_HEADS * C_DK, C_HEADS * C_DK, C_HEADS * C_DV, C_HEADS * C_DV,
              N_BRANCH * D_MODEL)
PROJ_WIDTH = sum(PROJ_SIZES)

kernel_name = 'hybrid_gated_swa_axial_hgrn2_dit'


def rms_norm(t, g):
    tf = t.astype(jnp.float32)
    y = tf * lax.rsqrt(jnp.mean(tf * tf, axis=-1, keepdims=True) + EPS)
    return (y * g.astype(jnp.float32)).astype(t.dtype)


def axial_rope(n):
    rows = n // GRID_W
    row = jnp.repeat(jnp.arange(rows), GRID_W).astype(jnp.float32)
    col = jnp.tile(jnp.arange(GRID_W), rows).astype(jnp.float32)
    inv = ROPE_THETA ** (-jnp.arange(0, AXIS_ROPE_DIM, 2, dtype=jnp.float32) / AXIS_ROPE_DIM)
    ang = jnp.concatenate([row[:, None] * inv, col[:, None] * inv], axis=-1)
    return jnp.cos(ang), jnp.sin(ang)


def apply_rope(t, cos, sin):
    t1, t2 = t[..., 0::2], t[..., 1::2]
    c = cos[None, :, None, :].astype(t.dtype)
    s = sin[None, :, None, :].astype(t.dtype)
    return jnp.stack([t1 * c - t2 * s, t1 * s + t2 * c], axis=-1).reshape(t.shape)


def features(h, w_in, qn, kn, rope):
    B_, L, _ = h.shape
    split_at = np.cumsum(PROJ_SIZES)[:-1].tolist()
    aq, ak, av, bq, bk, bv, cq, cf, cb, ci, cg, gates = jnp.split(h @ w_in, split_at, axis=-1)
    heads = lambda t, n: t.reshape(B_, L, n, HEAD_DIM)
    aq, ak, av = heads(aq, A_HEADS), heads(ak, A_KV_HEADS), heads(av, A_KV_HEADS)
    bq = rms_norm(heads(bq, B_HEADS), qn)
    bk = rms_norm(heads(bk, B_KV_HEADS), kn)
    bv = heads(bv, B_KV_HEADS)
    if rope is not None:
        cos, sin = rope
        aq, ak, bq, bk = [apply_rope(t, cos, sin) for t in (aq, ak, bq, bk)]
    return aq, ak, av, bq, bk, bv, cq, cf, cb, ci, cg, gates


def dense_attn(q, k, v, sink=None):
    B_, Lq, HQ, hd = q.shape
    KV = k.shape[2]
    G = HQ // KV
    s = jnp.einsum('bqkgd,bjkd->bkgqj', q.reshape(B_, Lq, KV, G, hd), k).astype(jnp.float32) * hd ** -0.5
    if sink is not None:
        s_sink = jnp.broadcast_to(sink.astype(jnp.float32).reshape(1, KV, G, 1, 1), s.shape[:-1] + (1,))
        s = jnp.concatenate([s, s_sink], axis=-1)
    p = jax.nn.softmax(s, axis=-1)[..., :k.shape[1]].astype(v.dtype)
    return jnp.einsum('bkgqj,bjkd->bqkgd', p, v).reshape(B_, Lq, HQ * hd)


def window_attn(q, k, v, ck, cv, sink):
    B_, S, HQ, hd = q.shape
    KV = k.shape[2]
    G = HQ // KV
    nb = S // BLOCK
    n_ctx = ck.shape[1]
    qb = q.reshape(B_, nb, BLOCK, KV, G, hd)

    def band(t):
        tp = jnp.pad(t, ((0, 0), (BLOCK, BLOCK), (0, 0), (0, 0))).reshape(B_, nb + 2, BLOCK, KV, hd)
        return jnp.concatenate([tp[:, :-2], tp[:, 1:-1], tp[:, 2:]], axis=2)

    kb, vb = band(k), band(v)
    scale = hd ** -0.5
    s_loc = jnp.einsum('bnqkgd,bnjkd->bkgnqj', qb, kb).astype(jnp.float32) * scale
    blk = jnp.arange(nb)[:, None, None]
    qpos = blk * BLOCK + jnp.arange(BLOCK)[None, :, None]
    kpos = (blk - 1) * BLOCK + jnp.arange(3 * BLOCK)[None, None, :]
    valid = (jnp.abs(kpos - qpos) <= WINDOW) & (kpos >= 0) & (kpos < S)
    s_loc = jnp.where(valid, s_loc, NEG_BIG)
    s_ctx = jnp.einsum('bnqkgd,bjkd->bkgnqj', qb, ck).astype(jnp.float32) * scale
    s_sink = jnp.broadcast_to(sink.astype(jnp.float32).reshape(1, KV, G, 1, 1, 1), s_ctx.shape[:-1] + (1,))
    p = jax.nn.softmax(jnp.concatenate([s_ctx, s_loc, s_sink], axis=-1), axis=-1).astype(v.dtype)
    o = (jnp.einsum('bkgnqj,bjkd->bnqkgd', p[..., :n_ctx], cv)
         + jnp.einsum('bkgnqj,bnjkd->bnqkgd', p[..., n_ctx:n_ctx + 3 * BLOCK], vb))
    return o.reshape(B_, S, HQ * hd)


def global_attn(q, k, v, ck, cv):
    B_, S, HQ, hd = q.shape
    nb = S // BLOCK
    k_all = jnp.concatenate([ck, k], axis=1)
    v_all = jnp.concatenate([cv, v], axis=1)
    qb = jnp.moveaxis(q.reshape(B_, nb, BLOCK, HQ, hd), 1, 0)
    o = lax.map(lambda qblk: dense_attn(qblk, k_all, v_all), qb)
    return jnp.moveaxis(o, 0, 1).reshape(B_, S, HQ * hd)


def layer_lower_bounds(raw):
    p = jax.nn.softmax(raw.astype(jnp.float32), axis=0)
    return jnp.clip(jnp.cumsum(p, axis=0) - p[0], 0.0, 1.0)


def hgrn_forget(z, lb):
    log_f = jax.nn.log_sigmoid(z) + jnp.log1p(lb * jnp.exp(-z))
    k = (1.0 - lb) * jax.nn.sigmoid(-z)
    return k, log_f


def gla_chunk_scan(q, k, v, log_f, s0):
    B_, L, H, _ = q.shape
    nc = L // CHUNK
    to_chunks = lambda t: t.reshape(B_, nc, CHUNK, H, t.shape[-1]).transpose(1, 0, 3, 2, 4)
    tri = jnp.tril(jnp.ones((CHUNK, CHUNK), dtype=bool))[:, :, None]

    def step(s, inp):
        qi, ki, vi, fi = inp
        a = jnp.cumsum(fi, axis=2)
        o_inter = jnp.einsum('bhtd,bhde->bhte', qi * jnp.exp(a), s)
        diff = a[:, :, :, None, :] - a[:, :, None, :, :]
        decay = jnp.exp(jnp.where(tri, diff, NEG_BIG))
        scores = jnp.einsum('bhtd,bhjd,bhtjd->bhtj', qi, ki, decay)
        o_intra = jnp.einsum('bhtj,bhje->bhte', scores, vi)
        a_last = a[:, :, -1, :]
        s_new = (jnp.exp(a_last)[..., None] * s
                 + jnp.einsum('bhjd,bhje->bhde', ki * jnp.exp(a_last[:, :, None, :] - a), vi))
        return s_new, o_inter + o_intra

    s_fin, o = lax.scan(step, s0, (to_chunks(q), to_chunks(k), to_chunks(v), to_chunks(log_f)))
    return o.transpose(1, 0, 3, 2, 4).reshape(B_, L, H, v.shape[-1]), s_fin


def hgrn2_bidir(cq, cf, cb, ci, cg, lb_f, lb_b, gn, s0_f, s0_b):
    B_, L = cq.shape[:2]
    q = jax.nn.silu(cq.astype(jnp.float32)).reshape(B_, L, C_HEADS, C_DK)
    v = ci.astype(jnp.float32).reshape(B_, L, C_HEADS, C_DV)
    k_f, lf_f = hgrn_forget(cf.astype(jnp.float32).reshape(B_, L, C_HEADS, C_DK), lb_f.reshape(C_HEADS, C_DK))
    k_b, lf_b = hgrn_forget(cb.astype(jnp.float32).reshape(B_, L, C_HEADS, C_DK), lb_b.reshape(C_HEADS, C_DK))
    rev = lambda t: jnp.flip(t, axis=1)
    o_f, s_f = gla_chunk_scan(q, k_f, v, lf_f, s0_f)
    o_b, s_b = gla_chunk_scan(rev(q), rev(k_b), rev(v), rev(lf_b), s0_b)
    o = rms_norm(o_f + rev(o_b), gn) * jax.nn.silu(cg.astype(jnp.float32)).reshape(B_, L, C_HEADS, C_DV)
    return o.reshape(B_, L, C_HEADS * C_DV).astype(cq.dtype), s_f, s_b


def merge_branches(ya, yb, yc, gates, w_branch, w_out):
    B_, L = ya.shape[:2]
    ys = jnp.stack([ya, yb, yc], axis=2)
    up = jnp.einsum('blnw,nwd->blnd', ys, w_branch)
    g = jax.nn.sigmoid(gates.reshape(B_, L, N_BRANCH, -1))
    return jnp.sum(g * up, axis=2) @ w_out


def dwconv(u, w, b):
    y = lax.conv_general_dilated(u, w[:, None, :], window_strides=(1,),
                                 padding=((CONV_W // 2, CONV_W // 2),),
                                 dimension_numbers=('NWC', 'WIO', 'NWC'),
                                 feature_group_count=u.shape[-1])
    return y + b


def conv_ffn(h, w_up, w_dw, b_dw, w_down):
    a, g = jnp.split(dwconv(h @ w_up, w_dw, b_dw), 2, axis=-1)
    return (a * jax.nn.silu(g)) @ w_down


def setup_inputs(seed: int = 0) -> dict:
    key = jax.random.key(seed)
    ks = jax.random.split(key, 22)
    nrm = lambda k, shape, s: jax.random.normal(k, shape, jnp.float32) * s
    L = DEPTH
    return {
        'x': nrm(ks[0], (BATCH, SEQ, D_MODEL), 1.0),
        'c': nrm(ks[1], (BATCH, D_MODEL), 1.0),
        'ctx': nrm(ks[2], (BATCH, CTX_LEN, D_MODEL), 1.0),
        'c_ctx': nrm(ks[3], (D_MODEL,), 1.0),
        'w_mod': nrm(ks[4], (L, D_MODEL, N_MOD * D_MODEL), 0.5 * D_MODEL ** -0.5),
        'b_mod': nrm(ks[5], (L, N_MOD * D_MODEL), 0.02),
        'norm_mix': 1.0 + nrm(ks[6], (L, D_MODEL), 0.05),
        'norm_ffn': 1.0 + nrm(ks[7], (L, D_MODEL), 0.05),
        'w_in': nrm(ks[8], (L, D_MODEL, PROJ_WIDTH), D_MODEL ** -0.5),
        'sink_a': nrm(ks[9], (L, A_HEADS), 0.5),
        'qn_b': 1.0 + nrm(ks[10], (L, HEAD_DIM), 0.05),
        'kn_b': 1.0 + nrm(ks[11], (L, HEAD_DIM), 0.05),
        'lb_fwd': nrm(ks[12], (L, C_HEADS * C_DK), 1.0),
        'lb_bwd': nrm(ks[13], (L, C_HEADS * C_DK), 1.0),
        'gn_c': 1.0 + nrm(ks[14], (L, C_DV), 0.05),
        'w_branch': nrm(ks[15], (L, N_BRANCH, BRANCH_WIDTH, D_MODEL), BRANCH_WIDTH ** -0.5),
        'w_out': nrm(ks[16], (L, D_MODEL, D_MODEL), D_MODEL ** -0.5),
        'w_up': nrm(ks[17], (L, D_MODEL, 2 * D_FF), D_MODEL ** -0.5),
        'w_dw': nrm(ks[18], (L, CONV_W, 2 * D_FF), CONV_W ** -0.5),
        'b_dw': nrm(ks[19], (L, 2 * D_FF), 0.02),
        'w_down': nrm(ks[20], (L, D_FF, D_MODEL), D_FF ** -0.5),
        'norm_final': 1.0 + nrm(ks[21], (D_MODEL,), 0.05),
    }


def reference(x, c, ctx, c_ctx, w_mod, b_mod, norm_mix, norm_ffn, w_in, sink_a, qn_b, kn_b,
              lb_fwd, lb_bwd, gn_c, w_branch, w_out, w_up, w_dw, b_dw, w_down, norm_final):
    B_, S, D = x.shape
    rope = axial_rope(S)
    lbf = layer_lower_bounds(lb_fwd)
    lbb = layer_lower_bounds(lb_bwd)
    s0 = jnp.zeros((B_, C_HEADS, C_DK, C_DV), jnp.float32)
    for l in range(DEPTH):
        last = l == DEPTH - 1
        m_lat = (jax.nn.silu(c) @ w_mod[l] + b_mod[l]).reshape(B_, 1, N_MOD, D)
        m_ctx = (jax.nn.silu(c_ctx) @ w_mod[l] + b_mod[l]).reshape(N_MOD, D)

        h_lat = rms_norm(x, norm_mix[l]) * (1.0 + m_lat[:, :, 1]) + m_lat[:, :, 0]
        h_ctx = rms_norm(ctx, norm_mix[l]) * (1.0 + m_ctx[1]) + m_ctx[0]
        aq, ak, av, bq, bk, bv, cq, cf, cb, ci, cg, gates = features(h_lat, w_in[l], qn_b[l], kn_b[l], rope)
        caq, cak, cav, cbq, cbk, cbv, ccq, ccf, ccb, cci, ccg, cgates = features(h_ctx, w_in[l], qn_b[l], kn_b[l], None)

        yc_ctx, s_f, s_b = hgrn2_bidir(ccq, ccf, ccb, cci, ccg, lbf[l], lbb[l], gn_c[l], s0, s0)
        yc = hgrn2_bidir(cq, cf, cb, ci, cg, lbf[l], lbb[l], gn_c[l], s_f, s_b)[0]
        ya = window_attn(aq, ak, av, cak, cav, sink_a[l])
        yb = global_attn(bq, bk, bv, cbk, cbv)
        x_mix = x + m_lat[:, :, 2] * merge_branches(ya, yb, yc, gates, w_branch[l], w_out[l])

        h = rms_norm(x_mix, norm_ffn[l]) * (1.0 + m_lat[:, :, 4]) + m_lat[:, :, 3]
        x = x_mix + m_lat[:, :, 5] * conv_ffn(h, w_up[l], w_dw[l], b_dw[l], w_down[l])

        if not last:
            ya_c = dense_attn(caq, cak, cav, sink_a[l])
            yb_c = dense_attn(cbq, cbk, cbv)
            ctx_mix = ctx + m_ctx[2] * merge_branches(ya_c, yb_c, yc_ctx, cgates, w_branch[l], w_out[l])
            hc = rms_norm(ctx_mix, norm_ffn[l]) * (1.0 + m_ctx[4]) + m_ctx[3]
            ctx = ctx_mix + m_ctx[5] * conv_ffn(hc, w_up[l], w_dw[l], b_dw[l], w_down[l])
    return rms_norm(x, norm_final)
```

```python
import os
import numpy as np
import concourse.bass as bass
import concourse.mybir as mybir
from concourse.bass_utils import run_bass_kernel_spmd
from contextlib import ExitStack

F32 = mybir.dt.float32
BF16 = mybir.dt.bfloat16
AF = mybir.ActivationFunctionType
ALU = mybir.AluOpType
AX = mybir.AxisListType

D = 1024
SL = 2048
NCX = 256
T = 2304
LYR = 4
NCH = 72
BLK = [(0, 256), (256, 512), (768, 512), (1280, 512), (1792, 512)]
EPS = 1e-6
HSTAGE = int(os.environ.get("HSTAGE", "0"))
AQ, AK, AV, BQ, BK, BV, CQ, CF, CB, CI, CG, GT = 0, 512, 640, 768, 1280, 1408, 1536, 2048, 2560, 3072, 3584, 4096


def blk_of(t):
    return 0 if t < 256 else 1 + (t - 256) // 512


ENGS = ("pe", "dve", "act", "pool", "sp")
NSLOT = {"sp": 8, "act": 4, "pool": 8}


class Op:
    __slots__ = ("eng", "fn", "cs", "deps", "needs_inc", "value", "is_dma", "idx")


class Sched:
    def __init__(self):
        self.ops = []
        self.lastw = {}
        self.readers = {}
        self.dma_rr = {"sp": 0, "act": 0, "pool": 0}
        self.slot_last = {}
        self.pending_bar = {}
        self.last_of_cs = {}
        self.genbase = {}
        self.epoch = 0

    def barrier(self):
        deps = list(self.last_of_cs.values())
        for e in ENGS:
            self.pending_bar[e] = set(deps) | self.pending_bar.get(e, set())

    def add(self, eng, fn, reads=(), writes=(), dma=False, pwrites=()):
        op = Op()
        op.eng = eng
        op.fn = fn
        op.is_dma = dma
        op.idx = len(self.ops)
        op.needs_inc = dma
        op.value = None
        if dma:
            s = self.dma_rr[eng]
            self.dma_rr[eng] = (s + 1) % NSLOT[eng]
            op.cs = ("dma", eng, s)
        else:
            op.cs = (eng, self.epoch)
        deps = set()
        raww = set()
        for k in reads:
            for w in self.lastw.get(k, ()):
                deps.add(w)
                raww.add(w)
            if isinstance(k, str) and k.startswith("ps") and len(k) == 3:
                for cs, i in self.readers.get(k, {}).items():
                    if cs != op.cs:
                        deps.add(i)
        for k in writes:
            for w in self.lastw.get(k, ()):
                deps.add(w)
                raww.add(w)
            for cs, i in self.readers.get(k, {}).items():
                deps.add(i)
        for k in pwrites:
            rd = self.readers.get(k)
            if rd:
                self.genbase[k] = set(rd.values()) | set(self.lastw.get(k, ()))
            for w in self.genbase.get(k, ()):
                deps.add(w)
                raww.add(w)
        if dma:
            pl = self.slot_last.get(op.cs)
            if pl is not None:
                deps.add(pl)
            self.slot_last[op.cs] = op.idx
        bar = self.pending_bar.pop(eng, None)
        if bar:
            deps |= bar
        final = []
        for d in deps:
            dop = self.ops[d]
            if dop.cs == op.cs and not dma:
                if op.eng == "pe":
                    continue
            final.append(d)
        op.deps = final
        for k in reads:
            self.readers.setdefault(k, {})[op.cs] = op.idx
        for k in writes:
            self.lastw[k] = [op.idx]
            self.readers[k] = {}
            self.genbase[k] = {op.idx}
        for k in pwrites:
            if self.readers.get(k):
                self.lastw[k] = [op.idx]
                self.readers[k] = {}
            else:
                self.lastw.setdefault(k, []).append(op.idx)
        self.last_of_cs[op.cs] = op.idx
        self.ops.append(op)
        return op

    def emit(self, nc, final_wait_eng="sp"):
        ops = self.ops
        for op in ops:
            for d in op.deps:
                ops[d].needs_inc = True
        cnt = {}
        for op in ops:
            if op.needs_inc:
                inc = 16 if op.is_dma else 1
                cnt[op.cs] = cnt.get(op.cs, 0) + inc
                op.value = cnt[op.cs]
        cs_list = sorted(cnt.keys(), key=str)
        with ExitStack() as es:
            sems = {}
            for cs in cs_list:
                nm = "s_" + ("_".join(str(c) for c in cs) if isinstance(cs, tuple) else cs)
                nm = nm.replace(" ", "")
                sems[cs] = es.enter_context(nc.semaphore(nm))
            block = es.enter_context(nc.Block())
            streams = {e: [] for e in ENGS}
            for op in ops:
                streams[op.eng].append(op)

            def run(eng_name, eng):
                seen = {}
                for op in streams[eng_name]:
                    need = {}
                    for d in op.deps:
                        dop = ops[d]
                        v = dop.value
                        if seen.get(dop.cs, 0) < v and need.get(dop.cs, 0) < v:
                            need[dop.cs] = v
                    for cs, v in need.items():
                        eng.wait_ge(sems[cs], v)
                        seen[cs] = v
                    ins = op.fn(eng)
                    if op.needs_inc:
                        ins.then_inc(sems[op.cs], 16 if op.is_dma else 1)
                if eng_name == final_wait_eng:
                    for cs, v in cnt.items():
                        if seen.get(cs, 0) < v:
                            eng.wait_ge(sems[cs], v)

            @block.tensor
            def _(e):
                run("pe", e)

            @block.vector
            def _(e):
                run("dve", e)

            @block.scalar
            def _(e):
                run("act", e)

            @block.gpsimd
            def _(e):
                run("pool", e)

            @block.sync
            def _(e):
                run("sp", e)
        return cnt


def make_consts():
    c = {}
    c["k_ident"] = np.eye(128, dtype=np.float32)
    bo = np.zeros((128, 128), np.float32)
    bo[:64, :64] = 1.0
    bo[64:, 64:] = 1.0
    c["k_bones"] = bo
    ii = np.arange(128)[:, None]
    jj = np.arange(128)[None, :]
    band = np.ones((128, 384), np.float32)
    band[:, 0:128] = (jj >= ii)
    band[:, 256:384] = (jj <= ii)
    c["k_band"] = band
    j6 = (np.arange(64) % 32)[:, None]
    t6 = np.arange(32)[None, :]
    tri = np.zeros((64, 64), np.float32)
    tri[:, 0:32] = (j6 <= t6)
    tri[:, 32:64] = (j6 >= t6)
    c["k_tri"] = tri
    pos = np.arange(SL)
    row = (pos // 64).astype(np.float32)
    col = (pos % 64).astype(np.float32)
    inv = (np.float32(10000.0) ** (-np.arange(0, 32, 2, dtype=np.float32) / np.float32(32))).astype(np.float32)
    ang = np.concatenate([row[:, None] * inv, col[:, None] * inv], axis=-1).astype(np.float32)
    cs_ = np.cos(ang).astype(np.float32)
    sn_ = np.sin(ang).astype(np.float32)
    C = np.ones((128, T), np.float32)
    Sg = np.zeros((128, T), np.float32)
    for p in range(128):
        r = p % 64
        pair = r % 32
        C[p, NCX:] = cs_[:, pair]
        Sg[p, NCX:] = sn_[:, pair] * (-1.0 if r < 32 else 1.0)
    def blocked(a):
        return np.concatenate([np.ascontiguousarray(a[:, t0:t0 + n]).reshape(-1) for (t0, n) in BLK]).astype(np.float32)
    c["k_ropec"] = blocked(C)
    c["k_ropes"] = blocked(Sg)
    return c


WNAMES = [("w_mod", (4, 1024, 6144)), ("b_mod", (4, 6144)), ("norm_mix", (4, 1024)), ("norm_ffn", (4, 1024)),
          ("w_in", (4, 1024, 7168)), ("sink_a", (4, 8)), ("qn_b", (4, 64)), ("kn_b", (4, 64)),
          ("lb_fwd", (4, 512)), ("lb_bwd", (4, 512)), ("gn_c", (4, 128)), ("w_branch", (4, 3, 512, 1024)),
          ("w_out", (4, 1024, 1024)), ("w_up", (4, 1024, 5632)), ("w_dw", (4, 3, 5632)), ("b_dw", (4, 5632)),
          ("w_down", (4, 2816, 1024)), ("norm_final", (1024,))]
CNAMES = [("k_ident", (128, 128)), ("k_bones", (128, 128)), ("k_band", (128, 384)), ("k_tri", (64, 64)),
          ("k_ropec", (128 * T,)), ("k_ropes", (128 * T,))]


def build(nb=2, depth=4, dump=(), phases=("hgrn", "attnA", "attnB", "merge", "ffn")):
    nc = bass.Bass("TRN2", target_bir_lowering=False)
    S = Sched()
    dr = {}
    for name, shape in [("x", (nb, SL, D)), ("c", (nb, D)), ("ctx", (nb, NCX, D)), ("c_ctx", (D,))] + WNAMES + CNAMES:
        dr[name] = nc.dram_tensor(name, list(shape), F32, kind="ExternalInput").ap()
    y_d = nc.dram_tensor("y", [nb, SL, D], F32, kind="ExternalOutput").ap()
    ysc = nc.dram_tensor("ysc", [3 * 4 * 128 * (T // 2)], F32).ap()

    def ysc_blk(n_, k_, bi_):
        t0_, bn_ = BLK[bi_]
        off = (n_ * 4 + k_) * 128 * (T // 2) + 128 * (t0_ // 2)
        return bass.AP(ysc.tensor, off, [[bn_ // 2, 128], [1, bn_ // 2]])

    def ysc_blk4(n_, bi_):
        t0_, bn_ = BLK[bi_]
        off = (n_ * 4) * 128 * (T // 2) + 128 * (t0_ // 2)
        return bass.AP(ysc.tensor, off, [[bn_ // 2, 128], [128 * (T // 2), 4], [1, bn_ // 2]])
    ob_d = nc.dram_tensor("ob_d", [NCH, 32, 128], F32).ap()
    dump_d = {}
    dump_names = set(dump)
    w_in = dr["w_in"]

    def DAP(base, offset, pat):
        return bass.AP(base.tensor, offset, pat)

    es = ExitStack()
    with es:
        uid = [0]

        def sbt(name, shape, dt):
            uid[0] += 1
            return nc.sbuf_tensor("%s_%d" % (name, uid[0]), list(shape), dt)

        def sb(name, shape, dt):
            return es.enter_context(sbt(name, list(shape), dt))

        def A(eng, fn, r=(), w=(), dma=False, pw=()):
            return S.add(eng, fn, r, w, dma, pw)

        def dma(eng, out, in_, r=(), w=(), pw=(), slow=False):
            if slow:
                A(eng, lambda e: e.dma_start(out=out, in_=in_, allow_slow_non_contiguous=True), r, w, True, pw)
            else:
                A(eng, lambda e: e.dma_start(out=out, in_=in_), r, w, True, pw)

        def mm(out, lhsT, rhs, st, sp, r, w):
            A("pe", lambda e: e.matmul(out, lhsT=lhsT, rhs=rhs, start=st, stop=sp), r, w)

        def tr(out, in_, ident, r, w):
            A("pe", lambda e: e.transpose(out=out, in_=in_, identity=ident), r, w)

        def act(out, in_, func, r=(), w=(), pw=(), **kw):
            A("act", lambda e: e.activation(out=out, in_=in_, func=func, **kw), r, w, False, pw)

        def tt(eng, out, in0, in1, op, r=(), w=(), pw=()):
            A(eng, lambda e: e.tensor_tensor(out=out, in0=in0, in1=in1, op=op), r, w, False, pw)

        def ts(eng, out, in0, s1, s2, op0, op1, r=(), w=(), pw=()):
            A(eng, lambda e: e.tensor_scalar(out=out, in0=in0, scalar1=s1, scalar2=s2, op0=op0, op1=op1), r, w, False, pw)

        def ts1(eng, out, in0, s1, op0, r=(), w=(), pw=()):
            np_ = out.shape[0]
            if isinstance(s1, (int, float)):
                A(eng, lambda e: e.tensor_scalar(out=out, in0=in0, scalar1=float(s1), scalar2=0.0, op0=op0, op1=ALU.add), r, w, False, pw)
            else:
                A(eng, lambda e: e.tensor_scalar(out=out, in0=in0, scalar1=s1, scalar2=zeroc[0:np_, 0:1], op0=op0, op1=ALU.add),
                  list(r) + ["zeroc"], w, False, pw)

        def stt(eng, out, in0, sc, in1, op0, op1, r=(), w=(), pw=()):
            A(eng, lambda e: e.scalar_tensor_tensor(out=out, in0=in0, scalar=sc, in1=in1, op0=op0, op1=op1), r, w, False, pw)

        def cp(eng, out, in_, r=(), w=(), pw=()):
            if eng == "act":
                A(eng, lambda e: e.activation(out=out, in_=in_, func=AF.Copy), r, w, False, pw)
            else:
                A(eng, lambda e: e.tensor_copy(out=out, in_=in_), r, w, False, pw)

        def recip(out, in_, r=(), w=()):
            A("dve", lambda e: e.reciprocal(out=out, in_=in_), r, w)

        def memset(eng, ap, val, w=()):
            A(eng, lambda e: e.memset(ap, val), (), w)

        def dumpx(name, ap, shape, keys):
            if name in dump_names:
                d_ = nc.dram_tensor("dbg_" + name, list(shape), F32 if ap.dtype == F32 else BF16, kind="ExternalOutput").ap()
                dump_d[name] = d_
                dma("sp", d_, ap, r=keys)

        ps = [es.enter_context(nc.psum_tensor("ps%d" % i, [128, 512], F32)) for i in range(8)]
        pctr = [0]

        def psn():
            i = pctr[0] % 6
            pctr[0] += 1
            return i

        pactr = [0]

        def psacc():
            i = 6 + pactr[0] % 2
            pactr[0] += 1
            return i

        def pk(i):
            return "ps%d" % i

        def psb(i):
            return ps[i][:].bitcast(BF16)

        NWS = 4
        wsl = [sb("ws%d" % i, [128, 2048], BF16) for i in range(NWS)]
        wctr = [0]

        def wslot():
            i = wctr[0] % NWS
            wctr[0] += 1
            return i

        def wk(i):
            return "ws%d" % i

        def wload(si, off, src2d, kt, n):
            dst = wsl[si][:, off:off + kt * n].rearrange("p (k n) -> p k n", n=n)
            src = src2d.rearrange("(k p) n -> p k n", p=128)
            dma("pool", dst, src, pw=[wk(si)])
            return dst

        x_sb = sb("x_sb", [128, 8, T], F32)
        h_sb = sb("h_sb", [128, 8, T], BF16)
        gen = sb("gen", [128, 3072], F32)
        g = [gen[:, i * 512:(i + 1) * 512] for i in range(6)]
        gk = ["g%d" % i for i in range(6)]
        class Carver:
            def __init__(self, tile, n):
                self.t, self.n, self.off = tile, n, 0

            def __call__(self, shape):
                m = 1
                for d_ in shape[1:]:
                    m *= d_
                assert self.off + m <= self.n, (self.off, m, self.n)
                ap = self.t[0:shape[0], self.off:self.off + m]
                self.off += m
                if len(shape) == 3:
                    ap = ap.rearrange("p (a b) -> p a b", b=shape[2])
                elif len(shape) == 4:
                    ap = ap.rearrange("p (a b c) -> p a b c", b=shape[2], c=shape[3])
                return ap

        cf = Carver(sb("miscf", [128, 2600], F32), 2600)
        cb = Carver(sb("miscb", [128, 560], BF16), 560)
        identf = cf([128, 128])
        identb = cb([128, 128])
        onesf = cf([128, 128])
        bones = cf([128, 128])
        band01 = cb([128, 384])
        tri = cf([64, 64])
        m_all = cf([128, LYR, 48, 4])
        bmt = cf([128, LYR, 48])
        nmix = cf([128, LYR, 8])
        nffn = cf([128, LYR, 8])
        nfin = cf([128, 8])
        cT = cf([128, 4, 8])
        csb = cb([128, 4, 8])
        s1t = cf([128, 4, 8])
        lraw = cf([128, 2, 4, LYR])
        lbt = cf([128, 2, 4, LYR])
        omlt = cf([128, 2, 4, LYR])
        lsum = cf([128, 8])
        sinkrep = cf([128, LYR * 8])
        negsink = cf([128, LYR * 8])
        gq = cf([128, 4, LYR])
        gnrep = cf([64, LYR * 128])
        wdw = cf([128, 3, 44])
        bdw = cf([128, 44])
        epsc = cf([128, 1])
        onec = cf([128, 1])
        zeroc = cf([128, 1])
        m0125 = cf([128, 1])
        dma("sp", identf[:], dr["k_ident"], w=["identf"])
        dma("pool", identb[:], dr["k_ident"], w=["identb"])
        dma("sp", bones[:], dr["k_bones"], w=["bones"])
        dma("pool", band01[:], dr["k_band"], w=["band01"])
        dma("sp", tri[:], dr["k_tri"], w=["tri"])
        memset("dve", onesf[:], 1.0, w=["onesf"])
        memset("dve", epsc[:], EPS, w=["epsc"])
        memset("dve", onec[:], 1.0, w=["onec"])
        memset("dve", zeroc[:], 0.0, w=["zeroc"])
        memset("dve", m0125[:], -0.125, w=["m0125"])
        def loadT(dst_ap, src_rows, R, dkey, in_view=None):
            dma("sp", g[5][0:R, 0:128], src_rows, w=[gk[5]])
            pi = psn()
            tr(ps[pi][:, 0:R], g[5][0:R, 0:128], identf[0:R, 0:R], r=[gk[5], "identf"], w=[pk(pi)])
            src = ps[pi][:, 0:R] if in_view is None else in_view(ps[pi][:, 0:R])
            cp("dve", dst_ap, src, r=[pk(pi)], pw=[dkey])

        loadT(nmix.rearrange("p l d -> p (l d)"), dr["norm_mix"].rearrange("l (dt p) -> (l dt) p", p=128), 32, "nmix")
        loadT(nffn.rearrange("p l d -> p (l d)"), dr["norm_ffn"].rearrange("l (dt p) -> (l dt) p", p=128), 32, "nffn")
        loadT(nfin, dr["norm_final"].rearrange("(dt p) -> dt p", p=128), 8, "nfin")
        bmf = bmt.rearrange("p l c -> p (l c)")
        bsrc = dr["b_mod"].rearrange("l (ct p) -> (l ct) p", p=128)
        for hf_ in range(2):
            loadT(bmf[:, hf_ * 96:(hf_ + 1) * 96], bsrc[hf_ * 96:(hf_ + 1) * 96, :], 96, "bmt")
        for a_, nm_ in enumerate(["lb_fwd", "lb_bwd"]):
            loadT(lraw[:, a_].rearrange("p h l -> p l h"), dr[nm_].rearrange("l (h p) -> (l h) p", p=128), 16, "lraw%d" % a_,
                  in_view=lambda ap: ap.rearrange("p (l h) -> p l h", h=4))
        dma("sp", sinkrep[:], DAP(dr["sink_a"], 0, [[0, 128], [1, LYR * 8]]), w=["sinkrep"], slow=True)
        dma("sp", gnrep[:], DAP(dr["gn_c"], 0, [[0, 64], [1, LYR * 128]]), w=["gnrep"], slow=True)
        for gi, (nm, flip) in enumerate([("qn_b", 0), ("qn_b", 1), ("kn_b", 0), ("kn_b", 1)]):
            dma("sp", g[4][0:4, 0:64], dr[nm], w=[gk[4]])
            srcv = g[4][0:4, 0:64].rearrange("l (r two) -> l two r", two=2)
            for hp in range(2):
                for half in range(2):
                    c0 = 128 + hp * 64 + half * 32
                    cp("dve", g[4][0:4, c0:c0 + 32], srcv[:, half ^ flip, :], r=[gk[4]], pw=["g4perm"])
            pi = psn()
            tr(ps[pi][:, 0:4], g[4][0:4, 128:256], identf[0:4, 0:4], r=["g4perm", "identf"], w=[pk(pi)])
            cp("dve", gq[:, gi, :], ps[pi][:, 0:4], r=[pk(pi)], pw=["gq"])
        ts1("dve", negsink[:], sinkrep[:], -1.0, ALU.mult, r=["sinkrep"], w=["negsink"])

        lr = lraw[:].rearrange("p a h l -> p (a h) l")
        lb3 = lbt[:].rearrange("p a h l -> p (a h) l")
        om3 = omlt[:].rearrange("p a h l -> p (a h) l")
        act(lr, lr, AF.Exp, r=["lraw0", "lraw1"], w=["lraw"])
        A("dve", lambda e: e.reduce_sum(out=lsum[:], in_=lr, axis=AX.X), ["lraw"], ["lsum"])
        recip(lsum[:], lsum[:], r=["lsum"], w=["lsum"])
        tt("dve", lr, lr, lsum[:].unsqueeze(2).to_broadcast([128, 8, LYR]), ALU.mult, r=["lraw", "lsum"], w=["lraw"])
        memset("dve", lb3[:, :, 0:1], 0.0, w=["lb0"])
        cp("dve", lb3[:, :, 1:2], lr[:, :, 1:2], r=["lraw", "lb0"], w=["lb1"])
        tt("dve", lb3[:, :, 2:3], lb3[:, :, 1:2], lr[:, :, 2:3], ALU.add, r=["lraw", "lb1"], w=["lb2"])
        tt("dve", lb3[:, :, 3:4], lb3[:, :, 2:3], lr[:, :, 3:4], ALU.add, r=["lraw", "lb2"], w=["lb3"])
        ts("dve", lb3, lb3, 0.0, 1.0, ALU.max, ALU.min, r=["lb0", "lb1", "lb2", "lb3"], w=["lb"])
        ts("dve", om3, lb3, -1.0, 1.0, ALU.mult, ALU.add, r=["lb"], w=["oml"])

        NS = nb + 1
        memset("dve", cT[:], 0.0, w=["cT"])
        loadT(cT.rearrange("p j k -> p (j k)")[:, 0:nb * 8], dr["c"].rearrange("j (kt p) -> (j kt) p", p=128), nb * 8, "cT")
        loadT(cT[:, nb, :], dr["c_ctx"].rearrange("(kt p) -> kt p", p=128), 8, "cT")
        act(csb[:], cT[:], AF.Silu, r=["cT"], w=["csb"])
        for l in range(depth):
            for gi in range(24):
                si = wslot()
                wv = wload(si, 0, dr["w_mod"][l][:, gi * 256:(gi + 1) * 256], 8, 256)
                pi = psn()
                for ci in range(2):
                    for kt in range(8):
                        mm(ps[pi][:, ci * 4:ci * 4 + 4], wv[:, kt, ci * 128:(ci + 1) * 128], csb[:, :, kt], kt == 0, kt == 7,
                           r=[wk(si), "csb"], w=[pk(pi)])
                tt("dve", m_all[:, l, gi * 2:gi * 2 + 2, :], ps[pi][:, 0:8].rearrange("p (c j) -> p c j", j=4),
                   bmt[:, l, gi * 2:gi * 2 + 2].unsqueeze(2).to_broadcast([128, 2, 4]), ALU.add,
                   r=[pk(pi), "bmt"], pw=["m_all"])

        def mod(l, i, dt, j):
            return m_all[:, l, i * 8 + dt, j:j + 1]

        def load_x(b):
            with ExitStack() as ph:
                stage = [ph.enter_context(sbt("stg%d" % i, [128, 1024], F32)) for i in range(2)]
                for tt_ in range(18):
                    src = dr["ctx"][b, tt_ * 128:(tt_ + 1) * 128, :] if tt_ < 2 else dr["x"][b, (tt_ - 2) * 128:(tt_ - 1) * 128, :]
                    st = stage[tt_ % 2]
                    sk = "stg%d" % (tt_ % 2)
                    dma("sp", st[:], src, w=[sk])
                    bi = blk_of(tt_ * 128)
                    for half in range(2):
                        pi = psn()
                        for q in range(4):
                            dt = half * 4 + q
                            tr(ps[pi][:, q * 128:(q + 1) * 128], st[:, dt * 128:(dt + 1) * 128], identf[:], r=[sk, "identf"], w=[pk(pi)])
                        cp("dve" if half == 0 else "act", x_sb[:, half * 4:(half + 1) * 4, tt_ * 128:(tt_ + 1) * 128],
                           ps[pi][:].rearrange("p (q t) -> p q t", t=128), r=[pk(pi)],
                           pw=[("x", half * 4 + q, bi) for q in range(4)])
                S.barrier()

        def layer_scalars(l, b):
            for kind, (mi, nt, nk) in enumerate([(1, nmix, "nmix"), (1, nmix, "nmix"), (4, nffn, "nffn"), (4, nffn, "nffn")]):
                j = b if kind % 2 == 0 else nb
                stt("dve", s1t[:, kind, :], m_all[:, l, mi * 8:mi * 8 + 8, j], 1.0, nt[:, l, :], ALU.add, ALU.mult,
                    r=["m_all", nk], w=[("s1", kind)])

        def norm_to_h(l, b, which, blocks):
            for bi in blocks:
                t0, n = BLK[bi]
                j = nb if bi == 0 else b
                kind = which * 2 + (1 if bi == 0 else 0)
                pst = psn()
                for dt in range(8):
                    sq = g[dt % 2]
                    act(sq[:, :n], x_sb[:, dt, t0:t0 + n], AF.Square, r=[("x", dt, bi)], w=[gk[dt % 2]])
                    mm(ps[pst][:, :n], onesf[:], sq[:, :n], dt == 0, dt == 7, r=["onesf", gk[dt % 2]], w=[pk(pst)])
                act(g[2][:, :n], ps[pst][:, :n], AF.Sqrt, r=[pk(pst), "epsc"], w=[gk[2]], scale=1.0 / D, bias=epsc[:, 0:1])
                recip(g[2][:, :n], g[2][:, :n], r=[gk[2]], w=[gk[2]])
                for dt in range(8):
                    tmp = g[3 + dt % 2]
                    stt("dve", tmp[:, :n], x_sb[:, dt, t0:t0 + n], s1t[:, kind, dt:dt + 1], g[2][:, :n], ALU.mult, ALU.mult,
                        r=[("x", dt, bi), ("s1", kind), gk[2]], w=[gk[3 + dt % 2]])
                    act(h_sb[:, dt, t0:t0 + n], tmp[:, :n], AF.Identity, r=[gk[3 + dt % 2], "m_all"], w=[("h", dt, bi)],
                        bias=mod(l, 3 * which, dt, j), scale=1.0)

        def proj_fm(wv, wkey, bi, ncols=128):
            t0, n = BLK[bi]
            pi = psn()
            for kt in range(8):
                mm(ps[pi][0:ncols, :n], wv[:, kt, :], h_sb[:, kt, t0:t0 + n], kt == 0, kt == 7,
                   r=[wkey, ("h", kt, bi)], w=[pk(pi)])
            return pi

        def hgrn(l, b, last):
            CS = 32
            with ExitStack() as ph:
                def psb_(name, shape, dt):
                    return ph.enter_context(sbt(name, list(shape), dt))
                buf1 = gen[:, 0:T]
                buf2t = psb_("hb2", [128, T], F32)
                buf3t = psb_("hb3", [128, T], F32)
                buf2 = buf2t[:]
                buf3 = buf3t[:]
                qs = psb_("hqs", [128, T], BF16)
                qp = psb_("hqp", [128, T], BF16)
                kp = psb_("hkp", [128, T], BF16)
                vt = psb_("hvt", [64, 36, 128], BF16)
                ystg = [psb_("hys%d" % i_, [128, 512], BF16) for i_ in range(2)]
                ysc_ctr = [0]
                hf = Carver(psb_("hmf", [128, 2048], F32), 2048)
                hb_ = Carver(psb_("hmb", [128, 448], BF16), 448)
                sc = hf([128, 3, NCH])
                esc = hf([128, 3, NCH])
                Sst = hf([128, 128])
                Sbf = hb_([128, 128])
                scm = hb_([64, 32])
                kpT = hb_([64, 128])
                osum = hf([32, 128])
                junk = hf([32, 128])
                ssq = hf([32, 1])
                rst = hf([32, 1])
                sg = hf([32, 128])
                yv = hb_([32, 128])
                obw = [hf([32, 128]) for _ in range(4)]
                obr = [hf([32, 128]) for _ in range(4)]
                obc = [0, 0]
                b2_3 = buf2.rearrange("p (c i) -> p c i", i=CS)
                b3_3 = buf3.rearrange("p (c i) -> p c i", i=CS)
                MID, LST, REFB = 15, 31, 16

                for h in range(4):
                    sA = wslot()
                    Wq_ = wload(sA, 0, w_in[l][:, CQ + h * 128:CQ + (h + 1) * 128], 8, 128)
                    Wf_ = wload(sA, 1024, w_in[l][:, CF + h * 128:CF + (h + 1) * 128], 8, 128)
                    sB = wslot()
                    Wb_ = wload(sB, 0, w_in[l][:, CB + h * 128:CB + (h + 1) * 128], 8, 128)
                    Wi_ = wload(sB, 1024, w_in[l][:, CI + h * 128:CI + (h + 1) * 128], 8, 128)
                    sC = wslot()
                    Wg_ = wload(sC, 0, w_in[l][:, CG + h * 128:CG + (h + 1) * 128], 8, 128)
                    for bi in range(5):
                        t0, n = BLK[bi]
                        pi = proj_fm(Wq_, wk(sA), bi)
                        act(qs[:, t0:t0 + n], ps[pi][:, :n], AF.Silu, r=[pk(pi)], pw=["hqs"])
                    for cpair in range(18):
                        pi = psn()
                        for cc in range(2):
                            c = cpair * 2 + cc
                            bi = blk_of(c * 64)
                            for kt in range(8):
                                mm(ps[pi][0:64, cc * 128:(cc + 1) * 128], h_sb[:, kt, c * 64:(c + 1) * 64], Wi_[:, kt, :], kt == 0, kt == 7,
                                   r=[wk(sB), ("h", kt, bi)], w=[pk(pi)])
                        cp("act", vt[0:64, cpair * 2:cpair * 2 + 2, :], ps[pi][0:64, 0:256].rearrange("p (c e) -> p c e", e=128),
                           r=[pk(pi)], pw=["hvt"])

                    for dirn in (1, 0):
                        Wz, wzk = (Wf_, wk(sA)) if dirn == 0 else (Wb_, wk(sB))
                        for bi in range(5):
                            t0, n = BLK[bi]
                            pi = proj_fm(Wz, wzk, bi)
                            act(buf1[:, t0:t0 + n], ps[pi][:, :n], AF.Sigmoid, r=[pk(pi)], pw=["hb1"])
                        ts("dve", buf1, buf1, omlt[:, dirn, h, l:l + 1], lbt[:, dirn, h, l:l + 1], ALU.mult, ALU.add,
                           r=["hb1", "lb", "oml"], w=["hb1"])
                        act(buf2, buf1, AF.Ln, r=["hb1"], w=["hb2"])
                        A("dve", lambda e: e.tensor_tensor_scan(out=buf3, data0=onec[:, 0:1].to_broadcast([128, T]), data1=buf2,
                                                                 initial=0.0, op0=ALU.mult, op1=ALU.add),
                          ["hb2", "onec"], ["hb3"])
                        if dirn == 0:
                            cp("dve", sc[:, 0, 0:1], b3_3[:, 0, MID:MID + 1], r=["hb3"], w=["hsc0a"])
                            tt("dve", sc[:, 0, 1:NCH], b3_3[:, 1:NCH, MID], b3_3[:, 0:NCH - 1, LST], ALU.subtract, r=["hb3"], w=["hsc0b"])
                            cp("dve", sc[:, 1, 0:1], b3_3[:, 0, LST:LST + 1], r=["hb3"], w=["hsc1a"])
                            tt("dve", sc[:, 1, 1:NCH], b3_3[:, 1:NCH, LST], b3_3[:, 0:NCH - 1, LST], ALU.subtract, r=["hb3"], w=["hsc1b"])
                            tt("dve", sc[:, 2, :], b3_3[:, :, LST], b3_3[:, :, MID], ALU.subtract, r=["hb3"], w=["hsc2"])
                            act(esc[:], sc[:], AF.Exp, r=["hsc0a", "hsc0b", "hsc1a", "hsc1b", "hsc2"], w=["hesc"])
                            tt("dve", b2_3, b3_3, b3_3[:, :, MID:MID + 1].to_broadcast([128, NCH, CS]), ALU.subtract, r=["hb3", "hb2"], w=["hb2"])
                            dsrc, dk, ebuf, ek = buf2, "hb2", buf3, "hb3"
                        else:
                            tt("dve", buf2, buf2, buf3, ALU.subtract, r=["hb2", "hb3"], w=["hb2"])
                            tt("dve", sc[:, 0, :], b3_3[:, :, LST], b2_3[:, :, REFB], ALU.add, r=["hb3", "hb2"], w=["hsc0a"])
                            tt("dve", sc[:, 1, :], b3_3[:, :, LST], b2_3[:, :, 0], ALU.add, r=["hb3", "hb2"], w=["hsc1a"])
                            tt("dve", sc[:, 2, :], b2_3[:, :, 0], b2_3[:, :, REFB], ALU.subtract, r=["hb2"], w=["hsc2"])
                            act(esc[:], sc[:], AF.Exp, r=["hsc0a", "hsc1a", "hsc2"], w=["hesc"])
                            tt("dve", b3_3, b2_3, b2_3[:, :, REFB:REFB + 1].to_broadcast([128, NCH, CS]), ALU.subtract, r=["hb2", "hb3"], w=["hb3"])
                            dsrc, dk, ebuf, ek = buf3, "hb3", buf2, "hb2"
                        act(ebuf, dsrc, AF.Exp, r=[dk], w=[ek])
                        tt("dve", qp[:], qs[:], ebuf, ALU.mult, r=["hqs", ek], w=["hqp"])
                        act(ebuf, dsrc, AF.Exp, r=[dk, "hqp"], w=[ek], scale=-1.0)
                        ts("dve", buf1, buf1, -1.0, 1.0, ALU.mult, ALU.add, r=["hb1"], w=["hb1"])
                        tt("dve", kp[:], buf1, ebuf, ALU.mult, r=["hb1", ek], w=["hkp"])

                        order = list(range(7, -1, -1)) + list(range(NCH - 1, 7, -1)) if dirn == 1 else list(range(NCH))
                        final = dirn == 0
                        first = True
                        for idx, c in enumerate(order):
                            if HSTAGE == 1:
                                break
                            par = c % 2
                            w_ = c // 2
                            csl = slice(c * CS, (c + 1) * CS)
                            wsl_ = slice(w_ * 64, (w_ + 1) * 64)
                            prow = slice(par * 32, par * 32 + 32)
                            bi = blk_of(c * CS)
                            pi = psn()
                            sck = "hscm%d" % par
                            if par == 0:
                                mm(ps[pi][0:32, 0:32], kp[:, csl], qp[:, csl], True, True, r=["hkp", "hqp"], w=[pk(pi)])
                            else:
                                mm(ps[pi][0:64, 0:32], kp[:, wsl_], qp[:, csl], True, True, r=["hkp", "hqp"], w=[pk(pi)])
                            tt("dve", scm[prow, :], ps[pi][prow, 0:32], tri[prow, dirn * 32:(dirn + 1) * 32], ALU.mult, r=[pk(pi), "tri"], w=[sck])
                            po = psn()
                            mm(ps[po][0:32, 0:128], scm[prow, :], vt[prow, w_, :], True, first, r=[sck, "hvt"], w=[pk(po)])
                            if not first:
                                mm(ps[po][0:32, 0:128], qp[:, csl], Sbf[:], False, True, r=["hqp", "hSb"], w=[pk(po)])
                            if not final:
                                k_ = obc[0] % 4
                                obc[0] += 1
                                cp("act", obw[k_], ps[po][0:32, 0:128], r=[pk(po)], w=[("obw", k_)])
                                if HSTAGE != 2:
                                    dma("sp", ob_d[c], obw[k_], r=[("obw", k_)], w=[("obd", c)])
                            elif HSTAGE not in (2, 3) and not (last and c < 8):
                                k_ = obc[1] % 4
                                obc[1] += 1
                                dma("sp", obr[k_], ob_d[c], r=[("obd", c)], w=[("obr", k_)])
                                tt("dve", osum, ps[po][0:32, 0:128], obr[k_], ALU.add, r=[pk(po), ("obr", k_)], w=["hosum"])
                                tt("dve", junk, osum, osum, ALU.mult, r=["hosum"], w=["hjunk"])
                                A("dve", lambda e: e.reduce_sum(out=ssq, in_=junk, axis=AX.X), ["hjunk"], ["hssq"])
                                act(rst, ssq, AF.Sqrt, r=["hssq", "epsc"], w=["hrst"], scale=1.0 / 128, bias=epsc[0:32, 0:1])
                                recip(rst, rst, r=["hrst"], w=["hrst"])
                                pg = psn()
                                for kt in range(8):
                                    mm(ps[pg][0:32, 0:128], h_sb[:, kt, csl], Wg_[:, kt, :], kt == 0, kt == 7,
                                       r=[wk(sC), ("h", kt, bi)], w=[pk(pg)])
                                act(sg, ps[pg][0:32, 0:128], AF.Silu, r=[pk(pg)], w=["hsg"])
                                tt("dve", sg, sg, gnrep[0:32, l * 128:(l + 1) * 128], ALU.mult, r=["hsg", "gnrep"], w=["hsg"])
                                stt("dve", yv, osum, rst[:, 0:1], sg, ALU.mult, ALU.mult, r=["hosum", "hrst", "hsg"], w=["hyv"])
                                pt2 = psn()
                                tr(psb(pt2)[:, 0:32], yv, identb[0:32, 0:32], r=["hyv", "identb"], w=[pk(pt2)])
                                bt0, bn = BLK[bi]
                                sidx = ysc_ctr[0] % 2
                                cp("act", ystg[sidx][:, c * CS - bt0:c * CS - bt0 + CS], psb(pt2)[:, 0:32], r=[pk(pt2)], pw=["hys%d" % sidx])
                                if c * CS + CS == bt0 + bn and HSTAGE != 4:
                                    dma("sp", ysc_blk(2, h, bi), ystg[sidx][:].bitcast(F32)[:, 0:bn // 2], r=["hys%d" % sidx], w=[("ysc", 2, h, bi)])
                                    ysc_ctr[0] += 1
                            if idx < len(order) - 1:
                                pt = psn()
                                tr(psb(pt)[0:64, 0:128], kp[:, wsl_], identb[:], r=["hkp", "identb"], w=[pk(pt)])
                                cp("act", kpT[prow, :], psb(pt)[prow, 0:128], r=[pk(pt)], w=["hkpT%d" % par])
                                pd = psn()
                                mm(ps[pd][:, 0:128], kpT[prow, :], vt[prow, w_, :], True, True, r=["hkpT%d" % par, "hvt"], w=[pk(pd)])
                                if first:
                                    ts1("dve", Sst, ps[pd][:, 0:128], esc[:, 2, c:c + 1], ALU.mult, r=[pk(pd), "hesc"], w=["hS"])
                                else:
                                    ts1("dve", Sst, Sst, esc[:, 1, c:c + 1], ALU.mult, r=["hS", "hesc"], w=["hS"])
                                    stt("dve", Sst, ps[pd][:, 0:128], esc[:, 2, c:c + 1], Sst, ALU.mult, ALU.add,
                                        r=[pk(pd), "hesc", "hS"], w=["hS"])
                                cn = order[idx + 1]
                                act(Sbf, Sst, AF.Identity, r=["hS", "hesc", "zeroc"], w=["hSb"], scale=esc[:, 0, cn:cn + 1], bias=zeroc[:, 0:1])
                            first = False
                S.barrier()

        def attn(l, b, br, last):
            qoff, koff, voff = (AQ, AK, AV) if br == 0 else (BQ, BK, BV)
            normed = br == 1
            with ExitStack() as ph:
                def psb_(name, shape, dt):
                    return ph.enter_context(sbt(name, list(shape), dt))
                KT = psb_("aKT", [128, T], BF16)
                Vt = psb_("aVt", [128, 18, 128], BF16)
                WA = psb_("aWA", [128, 8, 128], BF16)
                WB = psb_("aWB", [128, 8, 128], BF16)
                QTb = psb_("aQT", [128, 512], BF16)
                Pb = psb_("aPb", [128, 1024], BF16)
                PT = psb_("aPT", [128, 1024], BF16)
                af = Carver(psb_("amf", [128, 16], F32), 16)
                mx = af([128, 2])
                negm = af([128, 1])
                rs = af([128, 8])
                es_ = af([128, 1])
                rinv2 = af([128, 2])
                on = psb_("aon", [128, 128], BF16)
                ystg = [psb_("ays%d" % i_, [128, 512], BF16) for i_ in range(2)]
                ysc_ctr = [0]

                def build_perm(si, wnat):
                    nat = wnat.rearrange("p k (hh r two) -> p (k hh) two r", hh=2, r=32, two=2)
                    wa = WA[:].rearrange("p k (hh half r) -> p (k hh) half r", hh=2, half=2, r=32)
                    wb = WB[:].rearrange("p k (hh half r) -> p (k hh) half r", hh=2, half=2, r=32)
                    for half in range(2):
                        cp("dve", wa[:, :, half, :], nat[:, :, half, :], r=[wk(si)], pw=["aWA"])
                        cp("dve", wb[:, :, half, :], nat[:, :, 1 - half, :], r=[wk(si)], pw=["aWB"])

                def rope_block(pa, pb, bi, out_ap, okey, gidx):
                    t0, n = BLK[bi]
                    dma("sp", g[0][:, :n], DAP(dr["k_ropec"], 128 * t0, [[n, 128], [1, n]]), w=[gk[0]])
                    dma("sp", g[1][:, :n], DAP(dr["k_ropes"], 128 * t0, [[n, 128], [1, n]]), w=[gk[1]])
                    if not normed:
                        tt("dve", g[2][:, :n], ps[pa][:, :n], g[0][:, :n], ALU.mult, r=[pk(pa), gk[0]], w=[gk[2]])
                        tt("dve", g[3][:, :n], ps[pb][:, :n], g[1][:, :n], ALU.mult, r=[pk(pb), gk[1]], w=[gk[3]])
                        tt("dve", out_ap, g[2][:, :n], g[3][:, :n], ALU.add, r=[gk[2], gk[3]], pw=[okey])
                    else:
                        act(g[4][:, :n], ps[pa][:, :n], AF.Square, r=[pk(pa)], w=[gk[4]])
                        pq = psn()
                        mm(ps[pq][:, :n], bones[:], g[4][:, :n], True, True, r=["bones", gk[4]], w=[pk(pq)])
                        act(g[5][:, :n], ps[pq][:, :n], AF.Sqrt, r=[pk(pq), "epsc"], w=[gk[5]], scale=1.0 / 64, bias=epsc[:, 0:1])
                        recip(g[5][:, :n], g[5][:, :n], r=[gk[5]], w=[gk[5]])
                        stt("dve", g[2][:, :n], ps[pa][:, :n], gq[:, gidx, l:l + 1], g[0][:, :n], ALU.mult, ALU.mult,
                            r=[pk(pa), "gq", gk[0]], w=[gk[2]])
                        stt("dve", g[3][:, :n], ps[pb][:, :n], gq[:, gidx + 1, l:l + 1], g[1][:, :n], ALU.mult, ALU.mult,
                            r=[pk(pb), "gq", gk[1]], w=[gk[3]])
                        tt("dve", g[2][:, :n], g[2][:, :n], g[3][:, :n], ALU.add, r=[gk[2], gk[3]], w=[gk[2]])
                        tt("dve", out_ap, g[2][:, :n], g[5][:, :n], ALU.mult, r=[gk[2], gk[5]], pw=[okey])

                si = wslot()
                wn = wload(si, 0, w_in[l][:, koff:koff + 128], 8, 128)
                build_perm(si, wn)
                for bi in range(5):
                    t0, n = BLK[bi]
                    pa = proj_fm(WA[:], "aWA", bi)
                    pb = proj_fm(WB[:], "aWB", bi)
                    rope_block(pa, pb, bi, KT[:, t0:t0 + n], ("aKT", bi), 2)
                si = wslot()
                wv_ = wload(si, 0, w_in[l][:, voff:voff + 128], 8, 128)
                for tg in range(5):
                    tiles = list(range(tg * 4, min(tg * 4 + 4, 18)))
                    pi = psn()
                    for q, tt_ in enumerate(tiles):
                        bi = blk_of(tt_ * 128)
                        for kt in range(8):
                            mm(ps[pi][:, q * 128:(q + 1) * 128], h_sb[:, kt, tt_ * 128:(tt_ + 1) * 128], wv_[:, kt, :], kt == 0, kt == 7,
                               r=[wk(si), ("h", kt, bi)], w=[pk(pi)])
                    nt_ = len(tiles)
                    cp("act", Vt[:, tiles[0]:tiles[0] + nt_, :], ps[pi][:, 0:nt_ * 128].rearrange("p (q e) -> p q e", e=128),
                       r=[pk(pi)], pw=["aVt"])

                def attend(i, qt0, qc, ranges, band_lo, sink, use_max):
                    po = psacc()
                    for hh in range(2):
                        head = i + 4 * hh
                        rows = slice(hh * 64, hh * 64 + 64)
                        lhsT = QTb[rows, qc:qc + 128]
                        pv_list = []
                        if use_max:
                            banks = []
                            for (k0, nk) in ranges:
                                pi = psn()
                                mm(ps[pi][:, 0:nk], lhsT, KT[rows, k0:k0 + nk], True, True,
                                   r=["aQT"] + [("aKT", kb) for kb in sorted({blk_of(k0), blk_of(k0 + nk - 1)})], w=[pk(pi)])
                                banks.append(pi)
                            for j, pi in enumerate(banks):
                                nk = ranges[j][1]
                                A("dve", lambda e, j=j, pi=pi, nk=nk: e.reduce_max(out=mx[:, j:j + 1], in_=ps[pi][:, 0:nk], axis=AX.X),
                                  [pk(pi)], [("amx", j)])
                            mkeys = [("amx", 0)]
                            if len(banks) == 2:
                                tt("dve", mx[:, 0:1], mx[:, 0:1], mx[:, 1:2], ALU.max, r=[("amx", 0), ("amx", 1)], w=[("amx", 0)])
                            if sink:
                                ts("dve", negm[:], mx[:, 0:1], m0125[:, 0:1], negsink[:, l * 8 + head:l * 8 + head + 1], ALU.mult, ALU.min,
                                   r=[("amx", 0), "negsink", "m0125"], w=["anegm"])
                            else:
                                ts1("dve", negm[:], mx[:, 0:1], -0.125, ALU.mult, r=[("amx", 0)], w=["anegm"])
                            col = 0
                            for j, pi in enumerate(banks):
                                nk = ranges[j][1]
                                act(Pb[:, col:col + nk], ps[pi][:, 0:nk], AF.Exp, r=[pk(pi), "anegm"], pw=["aPb"], scale=0.125, bias=negm[:, 0:1])
                                col += nk
                            if band_lo is not None:
                                nbk = ranges[1][1]
                                tt("dve", Pb[:, 256:256 + nbk], Pb[:, 256:256 + nbk], band01[:, band_lo:band_lo + nbk], ALU.mult,
                                   r=["aPb", "band01"], w=["aPb"])
                            A("dve", lambda e, col=col: e.reduce_sum(out=rs[:, 0:1], in_=Pb[:, 0:col], axis=AX.X), ["aPb"], [("ars", 0)])
                            if sink:
                                act(es_[:], sinkrep[:, l * 8 + head:l * 8 + head + 1], AF.Exp, r=["sinkrep", "anegm"], w=["aes"], bias=negm[:, 0:1], scale=1.0)
                                tt("dve", rs[:, 0:1], rs[:, 0:1], es_[:], ALU.add, r=[("ars", 0), "aes"], w=[("ars", 0)])
                            recip(rinv2[:, hh:hh + 1], rs[:, 0:1], r=[("ars", 0)], w=[("arinv", hh)])
                            ntile = col // 128
                            pt = psn()
                            for kt in range(ntile):
                                tr(psb(pt)[:, kt * 128:(kt + 1) * 128], Pb[:, kt * 128:(kt + 1) * 128], identb[:], r=["aPb", "identb"], w=[pk(pt)])
                            cp("act", PT[:, 0:ntile * 128], psb(pt)[:, 0:ntile * 128], r=[pk(pt)], w=["aPT"])
                            ktile = []
                            for (k0, nk) in ranges:
                                ktile += [k0 // 128 + q for q in range(nk // 128)]
                            for kt in range(ntile):
                                mm(ps[po][:, hh * 64:(hh + 1) * 64], PT[:, kt * 128:(kt + 1) * 128], Vt[:, ktile[kt], hh * 64:(hh + 1) * 64],
                                   kt == 0, kt == ntile - 1, r=["aPT", "aVt"], w=[pk(po)])
                        else:
                            nr = len(ranges)
                            for j, (k0, nk) in enumerate(ranges):
                                pi = psn()
                                mm(ps[pi][:, 0:nk], lhsT, KT[rows, k0:k0 + nk], True, True, r=["aQT", ("aKT", blk_of(k0))], w=[pk(pi)])
                                hb = (j % 2) * 512
                                pbk = ("aPb", j % 2)
                                act(Pb[:, hb:hb + nk], ps[pi][:, 0:nk], AF.Exp, r=[pk(pi)], w=[pbk], scale=0.125)
                                A("dve", lambda e, j=j, hb=hb, nk=nk: e.reduce_sum(out=rs[:, j:j + 1], in_=Pb[:, hb:hb + nk], axis=AX.X),
                                  [pbk], [("ars", j)])
                                ntile = nk // 128
                                pt = psn()
                                for kt in range(ntile):
                                    tr(psb(pt)[:, kt * 128:(kt + 1) * 128], Pb[:, hb + kt * 128:hb + (kt + 1) * 128], identb[:], r=[pbk, "identb"], w=[pk(pt)])
                                ptk = ("aPT", j % 2)
                                cp("act" if j % 2 == 0 else "dve", PT[:, hb:hb + ntile * 128], psb(pt)[:, 0:ntile * 128], r=[pk(pt)], w=[ptk])
                                for kt in range(ntile):
                                    mm(ps[po][:, hh * 64:(hh + 1) * 64], PT[:, hb + kt * 128:hb + (kt + 1) * 128],
                                       Vt[:, k0 // 128 + kt, hh * 64:(hh + 1) * 64], j == 0 and kt == 0, j == nr - 1 and kt == ntile - 1,
                                       r=[ptk, "aVt"], w=[pk(po)])
                            if nr > 1:
                                A("dve", lambda e, nr=nr: e.reduce_sum(out=rs[:, 0:1], in_=rs[:, 0:nr], axis=AX.X),
                                  [("ars", j) for j in range(nr)], [("ars", 0)])
                            recip(rinv2[:, hh:hh + 1], rs[:, 0:1], r=[("ars", 0)], w=[("arinv", hh)])
                    tt("dve", on[:].rearrange("p (a d) -> p a d", d=64), ps[po][:, 0:128].rearrange("p (a d) -> p a d", d=64),
                       rinv2[:].unsqueeze(2).to_broadcast([128, 2, 64]), ALU.mult, r=[pk(po), ("arinv", 0), ("arinv", 1)], w=["aon"])
                    pt = psn()
                    tr(psb(pt)[:, 0:128], on[:], identb[:], r=["aon", "identb"], w=[pk(pt)])
                    bi_ = blk_of(qt0)
                    bt0, bn = BLK[bi_]
                    sidx = ysc_ctr[0] % 2
                    cp("act", ystg[sidx][:, qt0 - bt0:qt0 - bt0 + 128], psb(pt)[:, 0:128], r=[pk(pt)], pw=["ays%d" % sidx])
                    if qt0 + 128 == bt0 + bn:
                        dma("sp", ysc_blk(br, i, bi_), ystg[sidx][:].bitcast(F32)[:, 0:bn // 2], r=["ays%d" % sidx], w=[("ysc", br, i, bi_)])
                        ysc_ctr[0] += 1

                for i in range(4):
                    si = wslot()
                    wn = wsl[si][:, 0:1024].rearrange("p (k n) -> p k n", n=128)
                    for hh in range(2):
                        hd = i + 4 * hh
                        dst = wn[:, :, hh * 64:(hh + 1) * 64]
                        src = w_in[l][:, qoff + hd * 64:qoff + (hd + 1) * 64].rearrange("(k p) n -> p k n", p=128)
                        dma("pool", dst, src, pw=[wk(si)])
                    build_perm(si, wn)
                    for bi in range(5):
                        if bi == 0 and last:
                            continue
                        t0, n = BLK[bi]
                        pa = proj_fm(WA[:], "aWA", bi)
                        pb = proj_fm(WB[:], "aWB", bi)
                        rope_block(pa, pb, bi, QTb[:, :n], "aQT", 0)
                        for qi in range(n // 128):
                            qt0 = t0 + qi * 128
                            if bi == 0:
                                attend(i, qt0, qi * 128, [(0, 256)], None, br == 0, br == 0)
                            elif br == 0:
                                nq = (qt0 - 256) // 128
                                lo = max(nq - 1, 0)
                                hi = min(nq + 2, 16)
                                attend(i, qt0, qi * 128, [(0, 256), (256 + lo * 128, (hi - lo) * 128)], 128 if nq == 0 else 0, True, True)
                            else:
                                attend(i, qt0, qi * 128, [(0, 512), (512, 512), (1024, 512), (1536, 512), (2048, 256)], None, False, False)
                S.barrier()

        def merge(l, b, last):
            with ExitStack() as ph:
                uT = ph.enter_context(sbt("uT", [128, 8, 1280], BF16))
                yS = [ph.enter_context(sbt("yS%d" % n_, [128, 4, 1280], BF16)) for n_ in range(3)]
                sbs = [[0, 1, 2], [3, 4]]
                if last:
                    sbs = [[1, 2], [3, 4]]
                for sbi, blocks in enumerate(sbs):
                    base = 0 if sbi == 0 else 1280
                    for n_ in range(3):
                        for bi in blocks:
                            t0, n = BLK[bi]
                            ysf = yS[n_][:].rearrange("p k t -> p (k t)").bitcast(F32).rearrange("p (k t) -> p k t", k=4)
                            for k_ in range(4):
                                dma("sp", ysf[:, k_, (t0 - base) // 2:(t0 - base + n) // 2], ysc_blk(n_, k_, bi),
                                    r=[("ysc", n_, k_, bi)], pw=[("yS", n_)])
                    for dt in range(8):
                        sG = wslot()
                        Wg01 = [wload(sG, n_ * 1024, w_in[l][:, GT + n_ * 1024 + dt * 128:GT + n_ * 1024 + (dt + 1) * 128], 8, 128) for n_ in range(2)]
                        sG2 = wslot()
                        Wg2 = wload(sG2, 0, w_in[l][:, GT + 2048 + dt * 128:GT + 2048 + (dt + 1) * 128], 8, 128)
                        sW = wslot()
                        wbv = wsl[sW][:, 0:1536].rearrange("p (n k c) -> p n k c", n=3, k=4)
                        wb_ = dr["w_branch"]
                        for n_ in range(2):
                            for hh in range(2):
                                src = wb_[l, n_, hh * 256:(hh + 1) * 256, dt * 128:(dt + 1) * 128].rearrange("(j r) c -> r j c", r=64)
                                dma("pool", wbv[hh * 64:(hh + 1) * 64, n_, :, :], src, pw=[wk(sW)])
                        dma("pool", wbv[:, 2, :, :], wb_[l, 2, :, dt * 128:(dt + 1) * 128].rearrange("(k p) c -> p k c", p=128), pw=[wk(sW)])
                        Wgs = [(Wg01[0], wk(sG)), (Wg01[1], wk(sG)), (Wg2, wk(sG2))]
                        for bi in blocks:
                            t0, n = BLK[bi]
                            for n_ in range(3):
                                pg = proj_fm(Wgs[n_][0], Wgs[n_][1], bi)
                                sgt = g[n_ % 2]
                                act(sgt[:, :n], ps[pg][:, :n], AF.Sigmoid, r=[pk(pg)], w=[gk[n_ % 2]])
                                pu = psn()
                                for kt in range(4):
                                    mm(ps[pu][:, :n], wbv[:, n_, kt, :], yS[n_][:, kt, t0 - base:t0 - base + n], kt == 0, kt == 3,
                                       r=[wk(sW), ("yS", n_)], w=[pk(pu)])
                                if n_ == 0:
                                    tt("dve", g[2][:, :n], sgt[:, :n], ps[pu][:, :n], ALU.mult, r=[gk[0], pk(pu)], w=[gk[2]])
                                else:
                                    tt("dve", g[3][:, :n], sgt[:, :n], ps[pu][:, :n], ALU.mult, r=[gk[n_ % 2], pk(pu)], w=[gk[3]])
                                    if n_ == 1:
                                        tt("dve", g[2][:, :n], g[2][:, :n], g[3][:, :n], ALU.add, r=[gk[2], gk[3]], w=[gk[2]])
                                    else:
                                        tt("dve", uT[:, dt, t0 - base:t0 - base + n], g[2][:, :n], g[3][:, :n], ALU.add,
                                           r=[gk[2], gk[3]], pw=[("uT", dt)])
                    for half in range(4):
                        sO = wslot()
                        Wo = wload(sO, 0, dr["w_out"][l][:, half * 256:(half + 1) * 256], 8, 256)
                        for q in range(2):
                            dt2 = half * 2 + q
                            for bi in blocks:
                                t0, n = BLK[bi]
                                j = nb if bi == 0 else b
                                pi = psn()
                                for kt in range(8):
                                    mm(ps[pi][:, :n], Wo[:, kt, q * 128:(q + 1) * 128], uT[:, kt, t0 - base:t0 - base + n], kt == 0, kt == 7,
                                       r=[wk(sO), ("uT", kt)], w=[pk(pi)])
                                stt("dve", x_sb[:, dt2, t0:t0 + n], ps[pi][:, :n], mod(l, 2, dt2, j), x_sb[:, dt2, t0:t0 + n], ALU.mult, ALU.add,
                                    r=[pk(pi), "m_all", ("x", dt2, bi)], w=[("x", dt2, bi)])
                S.barrier()

        def ffn(l, b, last):
            blocks = [1, 2, 3, 4] if last else [0, 1, 2, 3, 4]
            wsrc = dr["w_dw"][l].rearrange("k (ct p) -> (k ct) p", p=128)
            wdf = wdw.rearrange("p k c -> p (k c)")
            for (r0_, rn_) in ((0, 64), (64, 64), (128, 4)):
                loadT(wdf[:, r0_:r0_ + rn_], wsrc[r0_:r0_ + rn_, :], rn_, "wdw")
            bsrc_ = dr["b_dw"][l].rearrange("(ct p) -> ct p", p=128)
            for (r0_, rn_) in ((0, 32), (32, 8), (40, 4)):
                loadT(bdw[:, r0_:r0_ + rn_], bsrc_[r0_:r0_ + rn_, :], rn_, "bdw")
            norm_to_h(l, b, 1, blocks)
            seqs = [(256, T)] if last else [(0, 256), (256, T)]
            with ExitStack() as ph:
                ua = ph.enter_context(sbt("fua", [128, T], F32))
                ug = ph.enter_context(sbt("fug", [128, T], F32))
                ca = ph.enter_context(sbt("fca", [128, T], F32))
                cg_ = ph.enter_context(sbt("fcg", [128, T], F32))
                actb = ph.enter_context(sbt("fact", [128, 2, T], BF16))
                lo_t = seqs[0][0]
                for grp in range(11):
                    sA = wslot()
                    Wa = wload(sA, 0, dr["w_up"][l][:, grp * 256:(grp + 1) * 256], 8, 256)
                    sG = wslot()
                    Wg = wload(sG, 0, dr["w_up"][l][:, 2816 + grp * 256:2816 + (grp + 1) * 256], 8, 256)
                    sD = wslot()
                    Wd = wload(sD, 0, dr["w_down"][l][grp * 256:(grp + 1) * 256, :], 2, 1024)
                    for ci in range(2):
                        cta = grp * 2 + ci
                        ctg = 22 + cta
                        for (Wx, wkx, ct, uu, cc, un, cn) in ((Wa, wk(sA), cta, ua, ca, "fua", "fca"), (Wg, wk(sG), ctg, ug, cg_, "fug", "fcg")):
                            for bi in blocks:
                                t0, n = BLK[bi]
                                pi = proj_fm(Wx[:, :, ci * 128:(ci + 1) * 128], wkx, bi)
                                act(cc[:, t0:t0 + n], ps[pi][:, :n], AF.Identity, r=[pk(pi), "wdw", "bdw"], pw=[cn],
                                    scale=wdw[:, 1, ct:ct + 1], bias=bdw[:, ct:ct + 1])
                                cp("dve", uu[:, t0:t0 + n], ps[pi][:, :n], r=[pk(pi)], pw=[un])
                            for (s0, s1_) in seqs:
                                stt("dve", cc[:, s0 + 1:s1_], uu[:, s0:s1_ - 1], wdw[:, 0, ct:ct + 1], cc[:, s0 + 1:s1_], ALU.mult, ALU.add,
                                    r=[un, cn, "wdw"], w=[cn])
                                stt("dve", cc[:, s0:s1_ - 1], uu[:, s0 + 1:s1_], wdw[:, 2, ct:ct + 1], cc[:, s0:s1_ - 1], ALU.mult, ALU.add,
                                    r=[un, cn, "wdw"], w=[cn])
                        act(cg_[:, lo_t:T], cg_[:, lo_t:T], AF.Silu, r=["fcg"], w=["fcg"])
                        tt("dve", actb[:, ci, lo_t:T], ca[:, lo_t:T], cg_[:, lo_t:T], ALU.mult, r=["fca", "fcg"], w=[("fact", ci)])
                    for dt in range(8):
                        for bi in blocks:
                            t0, n = BLK[bi]
                            j = nb if bi == 0 else b
                            pi = psn()
                            for ci in range(2):
                                mm(ps[pi][:, :n], Wd[:, ci, dt * 128:(dt + 1) * 128], actb[:, ci, t0:t0 + n], ci == 0, ci == 1,
                                   r=[wk(sD), ("fact", ci)], w=[pk(pi)])
                            stt("dve", x_sb[:, dt, t0:t0 + n], ps[pi][:, :n], mod(l, 5, dt, j), x_sb[:, dt, t0:t0 + n], ALU.mult, ALU.add,
                                r=[pk(pi), "m_all", ("x", dt, bi)], w=[("x", dt, bi)])
                S.barrier()

        def final_store(b):
            with ExitStack() as ph:
                stage = [ph.enter_context(sbt("ostg%d" % i, [128, 1024], F32)) for i in range(2)]
                xn = ph.enter_context(sbt("xnf", [128, 8, 512], F32))
                cnt = 0
                for bi in range(1, 5):
                    t0, n = BLK[bi]
                    pst = psn()
                    for dt in range(8):
                        sq = g[dt % 2]
                        act(sq[:, :n], x_sb[:, dt, t0:t0 + n], AF.Square, r=[("x", dt, bi)], w=[gk[dt % 2]])
                        mm(ps[pst][:, :n], onesf[:], sq[:, :n], dt == 0, dt == 7, r=["onesf", gk[dt % 2]], w=[pk(pst)])
                    act(g[2][:, :n], ps[pst][:, :n], AF.Sqrt, r=[pk(pst), "epsc"], w=[gk[2]], scale=1.0 / D, bias=epsc[:, 0:1])
                    recip(g[2][:, :n], g[2][:, :n], r=[gk[2]], w=[gk[2]])
                    for dt in range(8):
                        stt("dve", xn[:, dt, :], x_sb[:, dt, t0:t0 + n], nfin[:, dt:dt + 1], g[2][:, :n], ALU.mult, ALU.mult,
                            r=[("x", dt, bi), "nfin", gk[2]], w=[("xnf", dt)])
                    for qi in range(4):
                        st = stage[cnt % 2]
                        sk = "ostg%d" % (cnt % 2)
                        cnt += 1
                        for half in range(2):
                            pi = psn()
                            for q in range(4):
                                dt = half * 4 + q
                                tr(ps[pi][:, q * 128:(q + 1) * 128], xn[:, dt, qi * 128:(qi + 1) * 128], identf[:], r=[("xnf", dt), "identf"], w=[pk(pi)])
                            cp("dve" if half == 0 else "act", st[:, half * 512:(half + 1) * 512], ps[pi][:], r=[pk(pi)], pw=[sk])
                        tok = t0 - 256 + qi * 128
                        dma("sp", y_d[b, tok:tok + 128, :], st[:], r=[sk])
                S.barrier()

        for b in range(nb):
            load_x(b)
            for l in range(depth):
                last = l == LYR - 1
                S.epoch += 1
                layer_scalars(l, b)
                norm_to_h(l, b, 0, [0, 1, 2, 3, 4])
                if "h" in dump_names and b == 0 and l == 0:
                    dumpx("h", h_sb[:], [128, 8, T], [("h", dt, bi) for dt in range(8) for bi in range(5)])
                if "hgrn" in phases:
                    hgrn(l, b, last)
                if "attnA" in phases:
                    attn(l, b, 0, last)
                if "attnB" in phases:
                    attn(l, b, 1, last)
                if b == 0 and l == 0 and "ysc" in dump_names:
                    dumpx("ysc", ysc, [3 * 4 * 128 * (T // 2)], [("ysc", n_, k_, bi) for n_ in range(3) for k_ in range(4) for bi in range(5)])
                if "merge" in phases:
                    merge(l, b, last)
                if b == 0 and l == 0:
                    dumpx("xmix", x_sb[:], [128, 8, T], [("x", dt, bi) for dt in range(8) for bi in range(5)])
                if "ffn" in phases:
                    ffn(l, b, last)
                if b == 0 and l == 0:
                    dumpx("xffn", x_sb[:], [128, 8, T], [("x", dt, bi) for dt in range(8) for bi in range(5)])
            final_store(b)
        S.emit(nc)
    return nc, dump_d


_CONSTS = None


def run(inputs, nb=2, depth=4, dump=(), ncores=8, trace=False, phases=("hgrn", "attnA", "attnB", "merge", "ffn")):
    global _CONSTS
    if _CONSTS is None:
        _CONSTS = make_consts()
    nc, dump_d = build(nb=nb, depth=depth, dump=dump, phases=phases)
    shared = {}
    for name, _ in WNAMES:
        shared[name] = np.ascontiguousarray(np.asarray(inputs[name], dtype=np.float32))
    shared["c_ctx"] = np.ascontiguousarray(np.asarray(inputs["c_ctx"], dtype=np.float32))
    shared.update(_CONSTS)
    x = np.asarray(inputs["x"], dtype=np.float32)
    c = np.asarray(inputs["c"], dtype=np.float32)
    ctx = np.asarray(inputs["ctx"], dtype=np.float32)
    in_maps = []
    for core in range(ncores):
        m = dict(shared)
        m["x"] = np.ascontiguousarray(x[core * nb:(core + 1) * nb])
        m["c"] = np.ascontiguousarray(c[core * nb:(core + 1) * nb])
        m["ctx"] = np.ascontiguousarray(ctx[core * nb:(core + 1) * nb])
        in_maps.append(m)
    res = run_bass_kernel_spmd(nc, in_maps, core_ids=list(range(ncores)), trace=trace)
    return res


def kernel(**inputs):
    res = run(inputs, nb=2, depth=4, ncores=8)
    out = np.concatenate([np.asarray(r["y"]) for r in res.results], axis=0)
    return out.astype(np.float32)
```

```python
import os
import numpy as np
import concourse.bass as bass
import concourse.mybir as mybir
from concourse.bass_utils import run_bass_kernel_spmd
from contextlib import ExitStack

F32 = mybir.dt.float32
BF16 = mybir.dt.bfloat16
AF = mybir.ActivationFunctionType
ALU = mybir.AluOpType
AX = mybir.AxisListType

D = 1024
SL = 2048
NCX = 256
T = 2304
LYR = 4
NCH = 72
BLK = [(0, 256), (256, 512), (768, 512), (1280, 512), (1792, 512)]
EPS = 1e-6
HSTAGE = int(os.environ.get("HSTAGE", "0"))
AQ, AK, AV, BQ, BK, BV, CQ, CF, CB, CI, CG, GT = 0, 512, 640, 768, 1280, 1408, 1536, 2048, 2560, 3072, 3584, 4096


def blk_of(t):
    return 0 if t < 256 else 1 + (t - 256) // 512


ENGS = ("pe", "dve", "act", "pool", "sp")
NSLOT = {"sp": 8, "act": 4, "pool": 8}


class Op:
    __slots__ = ("eng", "fn", "cs", "deps", "needs_inc", "value", "is_dma", "idx")


class Sched:
    def __init__(self):
        self.ops = []
        self.lastw = {}
        self.readers = {}
        self.dma_rr = {"sp": 0, "act": 0, "pool": 0}
        self.slot_last = {}
        self.pending_bar = {}
        self.last_of_cs = {}
        self.genbase = {}
        self.epoch = 0

    def barrier(self):
        deps = list(self.last_of_cs.values())
        for e in ENGS:
            self.pending_bar[e] = set(deps) | self.pending_bar.get(e, set())

    def add(self, eng, fn, reads=(), writes=(), dma=False, pwrites=()):
        op = Op()
        op.eng = eng
        op.fn = fn
        op.is_dma = dma
        op.idx = len(self.ops)
        op.needs_inc = dma
        op.value = None
        if dma:
            s = self.dma_rr[eng]
            self.dma_rr[eng] = (s + 1) % NSLOT[eng]
            op.cs = ("dma", eng, s)
        else:
            op.cs = (eng, self.epoch)
        deps = set()
        raww = set()
        for k in reads:
            for w in self.lastw.get(k, ()):
                deps.add(w)
                raww.add(w)
            if isinstance(k, str) and k.startswith("ps") and len(k) == 3:
                for cs, i in self.readers.get(k, {}).items():
                    if cs != op.cs:
                        deps.add(i)
        for k in writes:
            for w in self.lastw.get(k, ()):
                deps.add(w)
                raww.add(w)
            for cs, i in self.readers.get(k, {}).items():
                deps.add(i)
        for k in pwrites:
            rd = self.readers.get(k)
            if rd:
                self.genbase[k] = set(rd.values()) | set(self.lastw.get(k, ()))
            for w in self.genbase.get(k, ()):
                deps.add(w)
                raww.add(w)
        if dma:
            pl = self.slot_last.get(op.cs)
            if pl is not None:
                deps.add(pl)
            self.slot_last[op.cs] = op.idx
        bar = self.pending_bar.pop(eng, None)
        if bar:
            deps |= bar
        final = []
        for d in deps:
            dop = self.ops[d]
            if dop.cs == op.cs and not dma:
                if op.eng == "pe":
                    continue
            final.append(d)
        op.deps = final
        for k in reads:
            self.readers.setdefault(k, {})[op.cs] = op.idx
        for k in writes:
            self.lastw[k] = [op.idx]
            self.readers[k] = {}
            self.genbase[k] = {op.idx}
        for k in pwrites:
            if self.readers.get(k):
                self.lastw[k] = [op.idx]
                self.readers[k] = {}
            else:
                self.lastw.setdefault(k, []).append(op.idx)
        self.last_of_cs[op.cs] = op.idx
        self.ops.append(op)
        return op

    def emit(self, nc, final_wait_eng="sp"):
        ops = self.ops
        for op in ops:
            for d in op.deps:
                ops[d].needs_inc = True
        cnt = {}
        for op in ops:
            if op.needs_inc:
                inc = 16 if op.is_dma else 1
                cnt[op.cs] = cnt.get(op.cs, 0) + inc
                op.value = cnt[op.cs]
        cs_list = sorted(cnt.keys(), key=str)
        with ExitStack() as es:
            sems = {}
            for cs in cs_list:
                nm = "s_" + ("_".join(str(c) for c in cs) if isinstance(cs, tuple) else cs)
                nm = nm.replace(" ", "")
                sems[cs] = es.enter_context(nc.semaphore(nm))
            block = es.enter_context(nc.Block())
            streams = {e: [] for e in ENGS}
            for op in ops:
                streams[op.eng].append(op)

            def run(eng_name, eng):
                seen = {}
                for op in streams[eng_name]:
                    need = {}
                    for d in op.deps:
                        dop = ops[d]
                        v = dop.value
                        if seen.get(dop.cs, 0) < v and need.get(dop.cs, 0) < v:
                            need[dop.cs] = v
                    for cs, v in need.items():
                        eng.wait_ge(sems[cs], v)
                        seen[cs] = v
                    ins = op.fn(eng)
                    if op.needs_inc:
                        ins.then_inc(sems[op.cs], 16 if op.is_dma else 1)
                if eng_name == final_wait_eng:
                    for cs, v in cnt.items():
                        if seen.get(cs, 0) < v:
                            eng.wait_ge(sems[cs], v)

            @block.tensor
            def _(e):
                run("pe", e)

            @block.vector
            def _(e):
                run("dve", e)

            @block.scalar
            def _(e):
                run("act", e)

            @block.gpsimd
            def _(e):
                run("pool", e)

            @block.sync
            def _(e):
                run("sp", e)
        return cnt


def make_consts():
    c = {}
    c["k_ident"] = np.eye(128, dtype=np.float32)
    bo = np.zeros((128, 128), np.float32)
    bo[:64, :64] = 1.0
    bo[64:, 64:] = 1.0
    c["k_bones"] = bo
    ii = np.arange(128)[:, None]
    jj = np.arange(128)[None, :]
    band = np.ones((128, 384), np.float32)
    band[:, 0:128] = (jj >= ii)
    band[:, 256:384] = (jj <= ii)
    c["k_band"] = band
    j6 = (np.arange(64) % 32)[:, None]
    t6 = np.arange(32)[None, :]
    tri = np.zeros((64, 64), np.float32)
    tri[:, 0:32] = (j6 <= t6)
    tri[:, 32:64] = (j6 >= t6)
    c["k_tri"] = tri
    pos = np.arange(SL)
    row = (pos // 64).astype(np.float32)
    col = (pos % 64).astype(np.float32)
    inv = (np.float32(10000.0) ** (-np.arange(0, 32, 2, dtype=np.float32) / np.float32(32))).astype(np.float32)
    ang = np.concatenate([row[:, None] * inv, col[:, None] * inv], axis=-1).astype(np.float32)
    cs_ = np.cos(ang).astype(np.float32)
    sn_ = np.sin(ang).astype(np.float32)
    C = np.ones((128, T), np.float32)
    Sg = np.zeros((128, T), np.float32)
    for p in range(128):
        r = p % 64
        pair = r % 32
        C[p, NCX:] = cs_[:, pair]
        Sg[p, NCX:] = sn_[:, pair] * (-1.0 if r < 32 else 1.0)
    def blocked(a):
        return np.concatenate([np.ascontiguousarray(a[:, t0:t0 + n]).reshape(-1) for (t0, n) in BLK]).astype(np.float32)
    c["k_ropec"] = blocked(C)
    c["k_ropes"] = blocked(Sg)
    return c


WNAMES = [("w_mod", (4, 1024, 6144)), ("b_mod", (4, 6144)), ("norm_mix", (4, 1024)), ("norm_ffn", (4, 1024)),
          ("w_in", (4, 1024, 7168)), ("sink_a", (4, 8)), ("qn_b", (4, 64)), ("kn_b", (4, 64)),
          ("lb_fwd", (4, 512)), ("lb_bwd", (4, 512)), ("gn_c", (4, 128)), ("w_branch", (4, 3, 512, 1024)),
          ("w_out", (4, 1024, 1024)), ("w_up", (4, 1024, 5632)), ("w_dw", (4, 3, 5632)), ("b_dw", (4, 5632)),
          ("w_down", (4, 2816, 1024)), ("norm_final", (1024,))]
CNAMES = [("k_ident", (128, 128)), ("k_bones", (128, 128)), ("k_band", (128, 384)), ("k_tri", (64, 64)),
          ("k_ropec", (128 * T,)), ("k_ropes", (128 * T,))]


def build(nb=2, depth=4, dump=(), phases=("hgrn", "attnA", "attnB", "merge", "ffn")):
    nc = bass.Bass("TRN2", target_bir_lowering=False)
    S = Sched()
    dr = {}
    for name, shape in [("x", (nb, SL, D)), ("c", (nb, D)), ("ctx", (nb, NCX, D)), ("c_ctx", (D,))] + WNAMES + CNAMES:
        dr[name] = nc.dram_tensor(name, list(shape), F32, kind="ExternalInput").ap()
    y_d = nc.dram_tensor("y", [nb, SL, D], F32, kind="ExternalOutput").ap()
    ysc = nc.dram_tensor("ysc", [3 * 4 * 128 * (T // 2)], F32).ap()

    def ysc_blk(n_, k_, bi_):
        t0_, bn_ = BLK[bi_]
        off = (n_ * 4 + k_) * 128 * (T // 2) + 128 * (t0_ // 2)
        return bass.AP(ysc.tensor, off, [[bn_ // 2, 128], [1, bn_ // 2]])

    def ysc_blk4(n_, bi_):
        t0_, bn_ = BLK[bi_]
        off = (n_ * 4) * 128 * (T // 2) + 128 * (t0_ // 2)
        return bass.AP(ysc.tensor, off, [[bn_ // 2, 128], [128 * (T // 2), 4], [1, bn_ // 2]])
    ob_d = nc.dram_tensor("ob_d", [NCH, 32, 128], F32).ap()
    dump_d = {}
    dump_names = set(dump)
    w_in = dr["w_in"]

    def DAP(base, offset, pat):
        return bass.AP(base.tensor, offset, pat)

    es = ExitStack()
    with es:
        uid = [0]

        def sbt(name, shape, dt):
            uid[0] += 1
            return nc.sbuf_tensor("%s_%d" % (name, uid[0]), list(shape), dt)

        def sb(name, shape, dt):
            return es.enter_context(sbt(name, list(shape), dt))

        def A(eng, fn, r=(), w=(), dma=False, pw=()):
            return S.add(eng, fn, r, w, dma, pw)

        def dma(eng, out, in_, r=(), w=(), pw=(), slow=False):
            if slow:
                A(eng, lambda e: e.dma_start(out=out, in_=in_, allow_slow_non_contiguous=True), r, w, True, pw)
            else:
                A(eng, lambda e: e.dma_start(out=out, in_=in_), r, w, True, pw)

        def mm(out, lhsT, rhs, st, sp, r, w):
            A("pe", lambda e: e.matmul(out, lhsT=lhsT, rhs=rhs, start=st, stop=sp), r, w)

        def tr(out, in_, ident, r, w):
            A("pe", lambda e: e.transpose(out=out, in_=in_, identity=ident), r, w)

        def act(out, in_, func, r=(), w=(), pw=(), **kw):
            A("act", lambda e: e.activation(out=out, in_=in_, func=func, **kw), r, w, False, pw)

        def tt(eng, out, in0, in1, op, r=(), w=(), pw=()):
            A(eng, lambda e: e.tensor_tensor(out=out, in0=in0, in1=in1, op=op), r, w, False, pw)

        def ts(eng, out, in0, s1, s2, op0, op1, r=(), w=(), pw=()):
            A(eng, lambda e: e.tensor_scalar(out=out, in0=in0, scalar1=s1, scalar2=s2, op0=op0, op1=op1), r, w, False, pw)

        def ts1(eng, out, in0, s1, op0, r=(), w=(), pw=()):
            np_ = out.shape[0]
            if isinstance(s1, (int, float)):
                A(eng, lambda e: e.tensor_scalar(out=out, in0=in0, scalar1=float(s1), scalar2=0.0, op0=op0, op1=ALU.add), r, w, False, pw)
            else:
                A(eng, lambda e: e.tensor_scalar(out=out, in0=in0, scalar1=s1, scalar2=zeroc[0:np_, 0:1], op0=op0, op1=ALU.add),
                  list(r) + ["zeroc"], w, False, pw)

        def stt(eng, out, in0, sc, in1, op0, op1, r=(), w=(), pw=()):
            A(eng, lambda e: e.scalar_tensor_tensor(out=out, in0=in0, scalar=sc, in1=in1, op0=op0, op1=op1), r, w, False, pw)

        def cp(eng, out, in_, r=(), w=(), pw=()):
            if eng == "act":
                A(eng, lambda e: e.activation(out=out, in_=in_, func=AF.Copy), r, w, False, pw)
            else:
                A(eng, lambda e: e.tensor_copy(out=out, in_=in_), r, w, False, pw)

        def recip(out, in_, r=(), w=()):
            A("dve", lambda e: e.reciprocal(out=out, in_=in_), r, w)

        def memset(eng, ap, val, w=()):
            A(eng, lambda e: e.memset(ap, val), (), w)

        def dumpx(name, ap, shape, keys):
            if name in dump_names:
                d_ = nc.dram_tensor("dbg_" + name, list(shape), F32 if ap.dtype == F32 else BF16, kind="ExternalOutput").ap()
                dump_d[name] = d_
                dma("sp", d_, ap, r=keys)

        ps = [es.enter_context(nc.psum_tensor("ps%d" % i, [128, 512], F32)) for i in range(8)]
        pctr = [0]

        def psn():
            i = pctr[0] % 6
            pctr[0] += 1
            return i

        pactr = [0]

        def psacc():
            i = 6 + pactr[0] % 2
            pactr[0] += 1
            return i

        def pk(i):
            return "ps%d" % i

        def psb(i):
            return ps[i][:].bitcast(BF16)

        NWS = 4
        wsl = [sb("ws%d" % i, [128, 2048], BF16) for i in range(NWS)]
        wctr = [0]

        def wslot():
            i = wctr[0] % NWS
            wctr[0] += 1
            return i

        def wk(i):
            return "ws%d" % i

        def wload(si, off, src2d, kt, n):
            dst = wsl[si][:, off:off + kt * n].rearrange("p (k n) -> p k n", n=n)
            src = src2d.rearrange("(k p) n -> p k n", p=128)
            dma("pool", dst, src, pw=[wk(si)])
            return dst

        x_sb = sb("x_sb", [128, 8, T], F32)
        h_sb = sb("h_sb", [128, 8, T], BF16)
        gen = sb("gen", [128, 3072], F32)
        g = [gen[:, i * 512:(i + 1) * 512] for i in range(6)]
        gk = ["g%d" % i for i in range(6)]
        class Carver:
            def __init__(self, tile, n):
                self.t, self.n, self.off = tile, n, 0

            def __call__(self, shape):
                m = 1
                for d_ in shape[1:]:
                    m *= d_
                assert self.off + m <= self.n, (self.off, m, self.n)
                ap = self.t[0:shape[0], self.off:self.off + m]
                self.off += m
                if len(shape) == 3:
                    ap = ap.rearrange("p (a b) -> p a b", b=shape[2])
                elif len(shape) == 4:
                    ap = ap.rearrange("p (a b c) -> p a b c", b=shape[2], c=shape[3])
                return ap

        cf = Carver(sb("miscf", [128, 2600], F32), 2600)
        cb = Carver(sb("miscb", [128, 560], BF16), 560)
        identf = cf([128, 128])
        identb = cb([128, 128])
        onesf = cf([128, 128])
        bones = cf([128, 128])
        band01 = cb([128, 384])
        tri = cf([64, 64])
        m_all = cf([128, LYR, 48, 4])
        bmt = cf([128, LYR, 48])
        nmix = cf([128, LYR, 8])
        nffn = cf([128, LYR, 8])
        nfin = cf([128, 8])
        cT = cf([128, 4, 8])
        csb = cb([128, 4, 8])
        s1t = cf([128, 4, 8])
        lraw = cf([128, 2, 4, LYR])
        lbt = cf([128, 2, 4, LYR])
        omlt = cf([128, 2, 4, LYR])
        lsum = cf([128, 8])
        sinkrep = cf([128, LYR * 8])
        negsink = cf([128, LYR * 8])
        gq = cf([128, 4, LYR])
        gnrep = cf([64, LYR * 128])
        wdw = cf([128, 3, 44])
        bdw = cf([128, 44])
        epsc = cf([128, 1])
        onec = cf([128, 1])
        zeroc = cf([128, 1])
        m0125 = cf([128, 1])
        dma("sp", identf[:], dr["k_ident"], w=["identf"])
        dma("pool", identb[:], dr["k_ident"], w=["identb"])
        dma("sp", bones[:], dr["k_bones"], w=["bones"])
        dma("pool", band01[:], dr["k_band"], w=["band01"])
        dma("sp", tri[:], dr["k_tri"], w=["tri"])
        memset("dve", onesf[:], 1.0, w=["onesf"])
        memset("dve", epsc[:], EPS, w=["epsc"])
        memset("dve", onec[:], 1.0, w=["onec"])
        memset("dve", zeroc[:], 0.0, w=["zeroc"])
        memset("dve", m0125[:], -0.125, w=["m0125"])
        def loadT(dst_ap, src_rows, R, dkey, in_view=None):
            dma("sp", g[5][0:R, 0:128], src_rows, w=[gk[5]])
            pi = psn()
            tr(ps[pi][:, 0:R], g[5][0:R, 0:128], identf[0:R, 0:R], r=[gk[5], "identf"], w=[pk(pi)])
            src = ps[pi][:, 0:R] if in_view is None else in_view(ps[pi][:, 0:R])
            cp("dve", dst_ap, src, r=[pk(pi)], pw=[dkey])

        loadT(nmix.rearrange("p l d -> p (l d)"), dr["norm_mix"].rearrange("l (dt p) -> (l dt) p", p=128), 32, "nmix")
        loadT(nffn.rearrange("p l d -> p (l d)"), dr["norm_ffn"].rearrange("l (dt p) -> (l dt) p", p=128), 32, "nffn")
        loadT(nfin, dr["norm_final"].rearrange("(dt p) -> dt p", p=128), 8, "nfin")
        bmf = bmt.rearrange("p l c -> p (l c)")
        bsrc = dr["b_mod"].rearrange("l (ct p) -> (l ct) p", p=128)
        for hf_ in range(2):
            loadT(bmf[:, hf_ * 96:(hf_ + 1) * 96], bsrc[hf_ * 96:(hf_ + 1) * 96, :], 96, "bmt")
        for a_, nm_ in enumerate(["lb_fwd", "lb_bwd"]):
            loadT(lraw[:, a_].rearrange("p h l -> p l h"), dr[nm_].rearrange("l (h p) -> (l h) p", p=128), 16, "lraw%d" % a_,
                  in_view=lambda ap: ap.rearrange("p (l h) -> p l h", h=4))
        dma("sp", sinkrep[:], DAP(dr["sink_a"], 0, [[0, 128], [1, LYR * 8]]), w=["sinkrep"], slow=True)
        dma("sp", gnrep[:], DAP(dr["gn_c"], 0, [[0, 64], [1, LYR * 128]]), w=["gnrep"], slow=True)
        for gi, (nm, flip) in enumerate([("qn_b", 0), ("qn_b", 1), ("kn_b", 0), ("kn_b", 1)]):
            dma("sp", g[4][0:4, 0:64], dr[nm], w=[gk[4]])
            srcv = g[4][0:4, 0:64].rearrange("l (r two) -> l two r", two=2)
            for hp in range(2):
                for half in range(2):
                    c0 = 128 + hp * 64 + half * 32
                    cp("dve", g[4][0:4, c0:c0 + 32], srcv[:, half ^ flip, :], r=[gk[4]], pw=["g4perm"])
            pi = psn()
            tr(ps[pi][:, 0:4], g[4][0:4, 128:256], identf[0:4, 0:4], r=["g4perm", "identf"], w=[pk(pi)])
            cp("dve", gq[:, gi, :], ps[pi][:, 0:4], r=[pk(pi)], pw=["gq"])
        ts1("dve", negsink[:], sinkrep[:], -1.0, ALU.mult, r=["sinkrep"], w=["negsink"])

        lr = lraw[:].rearrange("p a h l -> p (a h) l")
        lb3 = lbt[:].rearrange("p a h l -> p (a h) l")
        om3 = omlt[:].rearrange("p a h l -> p (a h) l")
        act(lr, lr, AF.Exp, r=["lraw0", "lraw1"], w=["lraw"])
        A("dve", lambda e: e.reduce_sum(out=lsum[:], in_=lr, axis=AX.X), ["lraw"], ["lsum"])
        recip(lsum[:], lsum[:], r=["lsum"], w=["lsum"])
        tt("dve", lr, lr, lsum[:].unsqueeze(2).to_broadcast([128, 8, LYR]), ALU.mult, r=["lraw", "lsum"], w=["lraw"])
        memset("dve", lb3[:, :, 0:1], 0.0, w=["lb0"])
        cp("dve", lb3[:, :, 1:2], lr[:, :, 1:2], r=["lraw", "lb0"], w=["lb1"])
        tt("dve", lb3[:, :, 2:3], lb3[:, :, 1:2], lr[:, :, 2:3], ALU.add, r=["lraw", "lb1"], w=["lb2"])
        tt("dve", lb3[:, :, 3:4], lb3[:, :, 2:3], lr[:, :, 3:4], ALU.add, r=["lraw", "lb2"], w=["lb3"])
        ts("dve", lb3, lb3, 0.0, 1.0, ALU.max, ALU.min, r=["lb0", "lb1", "lb2", "lb3"], w=["lb"])
        ts("dve", om3, lb3, -1.0, 1.0, ALU.mult, ALU.add, r=["lb"], w=["oml"])

        NS = nb + 1
        memset("dve", cT[:], 0.0, w=["cT"])
        loadT(cT.rearrange("p j k -> p (j k)")[:, 0:nb * 8], dr["c"].rearrange("j (kt p) -> (j kt) p", p=128), nb * 8, "cT")
        loadT(cT[:, nb, :], dr["c_ctx"].rearrange("(kt p) -> kt p", p=128), 8, "cT")
        act(csb[:], cT[:], AF.Silu, r=["cT"], w=["csb"])
        for l in range(depth):
            for gi in range(24):
                si = wslot()
                wv = wload(si, 0, dr["w_mod"][l][:, gi * 256:(gi + 1) * 256], 8, 256)
                pi = psn()
                for ci in range(2):
                    for kt in range(8):
                        mm(ps[pi][:, ci * 4:ci * 4 + 4], wv[:, kt, ci * 128:(ci + 1) * 128], csb[:, :, kt], kt == 0, kt == 7,
                           r=[wk(si), "csb"], w=[pk(pi)])
                tt("dve", m_all[:, l, gi * 2:gi * 2 + 2, :], ps[pi][:, 0:8].rearrange("p (c j) -> p c j", j=4),
                   bmt[:, l, gi * 2:gi * 2 + 2].unsqueeze(2).to_broadcast([128, 2, 4]), ALU.add,
                   r=[pk(pi), "bmt"], pw=["m_all"])

        def mod(l, i, dt, j):
            return m_all[:, l, i * 8 + dt, j:j + 1]

        def load_x(b):
            with ExitStack() as ph:
                stage = [ph.enter_context(sbt("stg%d" % i, [128, 1024], F32)) for i in range(2)]
                for tt_ in range(18):
                    src = dr["ctx"][b, tt_ * 128:(tt_ + 1) * 128, :] if tt_ < 2 else dr["x"][b, (tt_ - 2) * 128:(tt_ - 1) * 128, :]
                    st = stage[tt_ % 2]
                    sk = "stg%d" % (tt_ % 2)
                    dma("sp", st[:], src, w=[sk])
                    bi = blk_of(tt_ * 128)
                    for half in range(2):
                        pi = psn()
                        for q in range(4):
                            dt = half * 4 + q
                            tr(ps[pi][:, q * 128:(q + 1) * 128], st[:, dt * 128:(dt + 1) * 128], identf[:], r=[sk, "identf"], w=[pk(pi)])
                        cp("dve" if half == 0 else "act", x_sb[:, half * 4:(half + 1) * 4, tt_ * 128:(tt_ + 1) * 128],
                           ps[pi][:].rearrange("p (q t) -> p q t", t=128), r=[pk(pi)],
                           pw=[("x", half * 4 + q, bi) for q in range(4)])
                S.barrier()

        def layer_scalars(l, b):
            for kind, (mi, nt, nk) in enumerate([(1, nmix, "nmix"), (1, nmix, "nmix"), (4, nffn, "nffn"), (4, nffn, "nffn")]):
                j = b if kind % 2 == 0 else nb
                stt("dve", s1t[:, kind, :], m_all[:, l, mi * 8:mi * 8 + 8, j], 1.0, nt[:, l, :], ALU.add, ALU.mult,
                    r=["m_all", nk], w=[("s1", kind)])

        def norm_to_h(l, b, which, blocks):
            for bi in blocks:
                t0, n = BLK[bi]
                j = nb if bi == 0 else b
                kind = which * 2 + (1 if bi == 0 else 0)
                pst = psn()
                for dt in range(8):
                    sq = g[dt % 2]
                    act(sq[:, :n], x_sb[:, dt, t0:t0 + n], AF.Square, r=[("x", dt, bi)], w=[gk[dt % 2]])
                    mm(ps[pst][:, :n], onesf[:], sq[:, :n], dt == 0, dt == 7, r=["onesf", gk[dt % 2]], w=[pk(pst)])
                act(g[2][:, :n], ps[pst][:, :n], AF.Sqrt, r=[pk(pst), "epsc"], w=[gk[2]], scale=1.0 / D, bias=epsc[:, 0:1])
                recip(g[2][:, :n], g[2][:, :n], r=[gk[2]], w=[gk[2]])
                for dt in range(8):
                    tmp = g[3 + dt % 2]
                    stt("dve", tmp[:, :n], x_sb[:, dt, t0:t0 + n], s1t[:, kind, dt:dt + 1], g[2][:, :n], ALU.mult, ALU.mult,
                        r=[("x", dt, bi), ("s1", kind), gk[2]], w=[gk[3 + dt % 2]])
                    act(h_sb[:, dt, t0:t0 + n], tmp[:, :n], AF.Identity, r=[gk[3 + dt % 2], "m_all"], w=[("h", dt, bi)],
                        bias=mod(l, 3 * which, dt, j), scale=1.0)

        def proj_fm(wv, wkey, bi, ncols=128):
            t0, n = BLK[bi]
            pi = psn()
            for kt in range(8):
                mm(ps[pi][0:ncols, :n], wv[:, kt, :], h_sb[:, kt, t0:t0 + n], kt == 0, kt == 7,
                   r=[wkey, ("h", kt, bi)], w=[pk(pi)])
            return pi

        def hgrn(l, b, last):
            CS = 32
            with ExitStack() as ph:
                def psb_(name, shape, dt):
                    return ph.enter_context(sbt(name, list(shape), dt))
                buf1 = gen[:, 0:T]
                buf2t = psb_("hb2", [128, T], F32)
                buf3t = psb_("hb3", [128, T], F32)
                buf2 = buf2t[:]
                buf3 = buf3t[:]
                qs = psb_("hqs", [128, T], BF16)
                qp = psb_("hqp", [128, T], BF16)
                kp = psb_("hkp", [128, T], BF16)
                vt = psb_("hvt", [64, 36, 128], BF16)
                ystg = [psb_("hys%d" % i_, [128, 512], BF16) for i_ in range(2)]
                ysc_ctr = [0]
                hf = Carver(psb_("hmf", [128, 2048], F32), 2048)
                hb_ = Carver(psb_("hmb", [128, 576], BF16), 576)
                sc = hf([128, 3, NCH])
                esc = hf([128, 3, NCH])
                Sst = hf([128, 128])
                Sbf2 = [hb_([128, 128]), hb_([128, 128])]
                scm = hb_([64, 32])
                kpT = hb_([64, 128])
                osum = hf([32, 128])
                junk = hf([32, 128])
                ssq = hf([32, 1])
                rst = hf([32, 1])
                sg = hf([32, 128])
                yv = hb_([32, 128])
                obw = [hf([32, 128]) for _ in range(4)]
                obr = [hf([32, 128]) for _ in range(4)]
                obc = [0, 0]
                b2_3 = buf2.rearrange("p (c i) -> p c i", i=CS)
                b3_3 = buf3.rearrange("p (c i) -> p c i", i=CS)
                MID, LST, REFB = 15, 31, 16

                for h in range(4):
                    sA = wslot()
                    Wq_ = wload(sA, 0, w_in[l][:, CQ + h * 128:CQ + (h + 1) * 128], 8, 128)
                    Wf_ = wload(sA, 1024, w_in[l][:, CF + h * 128:CF + (h + 1) * 128], 8, 128)
                    sB = wslot()
                    Wb_ = wload(sB, 0, w_in[l][:, CB + h * 128:CB + (h + 1) * 128], 8, 128)
                    Wi_ = wload(sB, 1024, w_in[l][:, CI + h * 128:CI + (h + 1) * 128], 8, 128)
                    sC = wslot()
                    Wg_ = wload(sC, 0, w_in[l][:, CG + h * 128:CG + (h + 1) * 128], 8, 128)
                    for bi in range(5):
                        t0, n = BLK[bi]
                        pi = proj_fm(Wq_, wk(sA), bi)
                        act(qs[:, t0:t0 + n], ps[pi][:, :n], AF.Silu, r=[pk(pi)], pw=["hqs"])
                    for cpair in range(18):
                        pi = psn()
                        for cc in range(2):
                            c = cpair * 2 + cc
                            bi = blk_of(c * 64)
                            for kt in range(8):
                                mm(ps[pi][0:64, cc * 128:(cc + 1) * 128], h_sb[:, kt, c * 64:(c + 1) * 64], Wi_[:, kt, :], kt == 0, kt == 7,
                                   r=[wk(sB), ("h", kt, bi)], w=[pk(pi)])
                        cp("act", vt[0:64, cpair * 2:cpair * 2 + 2, :], ps[pi][0:64, 0:256].rearrange("p (c e) -> p c e", e=128),
                           r=[pk(pi)], pw=["hvt"])

                    for dirn in (1, 0):
                        Wz, wzk = (Wf_, wk(sA)) if dirn == 0 else (Wb_, wk(sB))
                        for bi in range(5):
                            t0, n = BLK[bi]
                            pi = proj_fm(Wz, wzk, bi)
                            act(buf1[:, t0:t0 + n], ps[pi][:, :n], AF.Sigmoid, r=[pk(pi)], pw=["hb1"])
                        ts("dve", buf1, buf1, omlt[:, dirn, h, l:l + 1], lbt[:, dirn, h, l:l + 1], ALU.mult, ALU.add,
                           r=["hb1", "lb", "oml"], w=["hb1"])
                        act(buf2, buf1, AF.Ln, r=["hb1"], w=["hb2"])
                        A("dve", lambda e: e.tensor_tensor_scan(out=buf3, data0=onec[:, 0:1].to_broadcast([128, T]), data1=buf2,
                                                                 initial=0.0, op0=ALU.mult, op1=ALU.add),
                          ["hb2", "onec"], ["hb3"])
                        if dirn == 0:
                            cp("dve", sc[:, 0, 0:1], b3_3[:, 0, MID:MID + 1], r=["hb3"], w=["hsc0a"])
                            tt("dve", sc[:, 0, 1:NCH], b3_3[:, 1:NCH, MID], b3_3[:, 0:NCH - 1, LST], ALU.subtract, r=["hb3"], w=["hsc0b"])
                            cp("dve", sc[:, 1, 0:1], b3_3[:, 0, LST:LST + 1], r=["hb3"], w=["hsc1a"])
                            tt("dve", sc[:, 1, 1:NCH], b3_3[:, 1:NCH, LST], b3_3[:, 0:NCH - 1, LST], ALU.subtract, r=["hb3"], w=["hsc1b"])
                            tt("dve", sc[:, 2, :], b3_3[:, :, LST], b3_3[:, :, MID], ALU.subtract, r=["hb3"], w=["hsc2"])
                            act(esc[:], sc[:], AF.Exp, r=["hsc0a", "hsc0b", "hsc1a", "hsc1b", "hsc2"], w=["hesc"])
                            tt("dve", b2_3, b3_3, b3_3[:, :, MID:MID + 1].to_broadcast([128, NCH, CS]), ALU.subtract, r=["hb3", "hb2"], w=["hb2"])
                            dsrc, dk, ebuf, ek = buf2, "hb2", buf3, "hb3"
                        else:
                            tt("dve", buf2, buf2, buf3, ALU.subtract, r=["hb2", "hb3"], w=["hb2"])
                            tt("dve", sc[:, 0, :], b3_3[:, :, LST], b2_3[:, :, REFB], ALU.add, r=["hb3", "hb2"], w=["hsc0a"])
                            tt("dve", sc[:, 1, :], b3_3[:, :, LST], b2_3[:, :, 0], ALU.add, r=["hb3", "hb2"], w=["hsc1a"])
                            tt("dve", sc[:, 2, :], b2_3[:, :, 0], b2_3[:, :, REFB], ALU.subtract, r=["hb2"], w=["hsc2"])
                            act(esc[:], sc[:], AF.Exp, r=["hsc0a", "hsc1a", "hsc2"], w=["hesc"])
                            tt("dve", b3_3, b2_3, b2_3[:, :, REFB:REFB + 1].to_broadcast([128, NCH, CS]), ALU.subtract, r=["hb2", "hb3"], w=["hb3"])
                            dsrc, dk, ebuf, ek = buf3, "hb3", buf2, "hb2"
                        act(ebuf, dsrc, AF.Exp, r=[dk], w=[ek])
                        tt("dve", qp[:], qs[:], ebuf, ALU.mult, r=["hqs", ek], w=["hqp"])
                        act(ebuf, dsrc, AF.Exp, r=[dk, "hqp"], w=[ek], scale=-1.0)
                        ts("dve", buf1, buf1, -1.0, 1.0, ALU.mult, ALU.add, r=["hb1"], w=["hb1"])
                        tt("dve", kp[:], buf1, ebuf, ALU.mult, r=["hb1", ek], w=["hkp"])

                        order = list(range(7, -1, -1)) + list(range(NCH - 1, 7, -1)) if dirn == 1 else list(range(NCH))
                        final = dirn == 0
                        first = True
                        for idx, c in enumerate(order):
                            if HSTAGE == 1:
                                break
                            par = c % 2
                            w_ = c // 2
                            csl = slice(c * CS, (c + 1) * CS)
                            wsl_ = slice(w_ * 64, (w_ + 1) * 64)
                            prow = slice(par * 32, par * 32 + 32)
                            bi = blk_of(c * CS)
                            pi = psn()
                            sck = "hscm%d" % par
                            if par == 0:
                                mm(ps[pi][0:32, 0:32], kp[:, csl], qp[:, csl], True, True, r=["hkp", "hqp"], w=[pk(pi)])
                            else:
                                mm(ps[pi][0:64, 0:32], kp[:, wsl_], qp[:, csl], True, True, r=["hkp", "hqp"], w=[pk(pi)])
                            tt("dve", scm[prow, :], ps[pi][prow, 0:32], tri[prow, dirn * 32:(dirn + 1) * 32], ALU.mult, r=[pk(pi), "tri"], w=[sck])
                            if idx < len(order) - 1:
                                pt = psn()
                                tr(psb(pt)[0:64, 0:128], kp[:, wsl_], identb[:], r=["hkp", "identb"], w=[pk(pt)])
                                cp("act", kpT[prow, :], psb(pt)[prow, 0:128], r=[pk(pt)], w=["hkpT%d" % par])
                                pd = psn()
                                mm(ps[pd][:, 0:128], kpT[prow, :], vt[prow, w_, :], True, True, r=["hkpT%d" % par, "hvt"], w=[pk(pd)])
                                if first:
                                    ts1("dve", Sst, ps[pd][:, 0:128], esc[:, 2, c:c + 1], ALU.mult, r=[pk(pd), "hesc"], w=["hS"])
                                else:
                                    ts1("dve", Sst, Sst, esc[:, 1, c:c + 1], ALU.mult, r=["hS", "hesc"], w=["hS"])
                                    stt("dve", Sst, ps[pd][:, 0:128], esc[:, 2, c:c + 1], Sst, ALU.mult, ALU.add,
                                        r=[pk(pd), "hesc", "hS"], w=["hS"])
                                cn = order[idx + 1]
                                act(Sbf2[(idx + 1) % 2], Sst, AF.Identity, r=["hS", "hesc", "zeroc"], w=["hSb%d" % ((idx + 1) % 2)], scale=esc[:, 0, cn:cn + 1], bias=zeroc[:, 0:1])
                            po = psn()
                            mm(ps[po][0:32, 0:128], scm[prow, :], vt[prow, w_, :], True, first, r=[sck, "hvt"], w=[pk(po)])
                            if not first:
                                mm(ps[po][0:32, 0:128], qp[:, csl], Sbf2[idx % 2], False, True, r=["hqp", "hSb%d" % (idx % 2)], w=[pk(po)])
                            if not final:
                                k_ = obc[0] % 4
                                obc[0] += 1
                                cp("act", obw[k_], ps[po][0:32, 0:128], r=[pk(po)], w=[("obw", k_)])
                                if HSTAGE != 2:
                                    dma("sp", ob_d[c], obw[k_], r=[("obw", k_)], w=[("obd", c)])
                            elif HSTAGE not in (2, 3) and not (last and c < 8):
                                k_ = obc[1] % 4
                                obc[1] += 1
                                dma("sp", obr[k_], ob_d[c], r=[("obd", c)], w=[("obr", k_)])
                                tt("dve", osum, ps[po][0:32, 0:128], obr[k_], ALU.add, r=[pk(po), ("obr", k_)], w=["hosum"])
                                tt("dve", junk, osum, osum, ALU.mult, r=["hosum"], w=["hjunk"])
                                A("dve", lambda e: e.reduce_sum(out=ssq, in_=junk, axis=AX.X), ["hjunk"], ["hssq"])
                                act(rst, ssq, AF.Sqrt, r=["hssq", "epsc"], w=["hrst"], scale=1.0 / 128, bias=epsc[0:32, 0:1])
                                recip(rst, rst, r=["hrst"], w=["hrst"])
                                pg = psn()
                                for kt in range(8):
                                    mm(ps[pg][0:32, 0:128], h_sb[:, kt, csl], Wg_[:, kt, :], kt == 0, kt == 7,
                                       r=[wk(sC), ("h", kt, bi)], w=[pk(pg)])
                                act(sg, ps[pg][0:32, 0:128], AF.Silu, r=[pk(pg)], w=["hsg"])
                                tt("dve", sg, sg, gnrep[0:32, l * 128:(l + 1) * 128], ALU.mult, r=["hsg", "gnrep"], w=["hsg"])
                                stt("dve", yv, osum, rst[:, 0:1], sg, ALU.mult, ALU.mult, r=["hosum", "hrst", "hsg"], w=["hyv"])
                                pt2 = psn()
                                tr(psb(pt2)[:, 0:32], yv, identb[0:32, 0:32], r=["hyv", "identb"], w=[pk(pt2)])
                                bt0, bn = BLK[bi]
                                sidx = ysc_ctr[0] % 2
                                cp("act", ystg[sidx][:, c * CS - bt0:c * CS - bt0 + CS], psb(pt2)[:, 0:32], r=[pk(pt2)], pw=["hys%d" % sidx])
                                if c * CS + CS == bt0 + bn and HSTAGE != 4:
                                    dma("sp", ysc_blk(2, h, bi), ystg[sidx][:].bitcast(F32)[:, 0:bn // 2], r=["hys%d" % sidx], w=[("ysc", 2, h, bi)])
                                    ysc_ctr[0] += 1
                            first = False
                S.barrier()

        def attn(l, b, br, last):
            qoff, koff, voff = (AQ, AK, AV) if br == 0 else (BQ, BK, BV)
            normed = br == 1
            with ExitStack() as ph:
                def psb_(name, shape, dt):
                    return ph.enter_context(sbt(name, list(shape), dt))
                KT = psb_("aKT", [128, T], BF16)
                Vt = psb_("aVt", [128, 18, 128], BF16)
                WA = psb_("aWA", [128, 8, 128], BF16)
                WB = psb_("aWB", [128, 8, 128], BF16)
                QTb = psb_("aQT", [128, 512], BF16)
                Pb = psb_("aPb", [128, 1024], BF16)
                PT = psb_("aPT", [128, 1024], BF16)
                af = Carver(psb_("amf", [128, 16], F32), 16)
                mx = af([128, 2])
                negm = af([128, 1])
                rs = af([128, 8])
                es_ = af([128, 1])
                rinv2 = af([128, 2])
                on = psb_("aon", [128, 128], BF16)
                ystg = [psb_("ays%d" % i_, [128, 512], BF16) for i_ in range(2)]
                ysc_ctr = [0]

                def build_perm(si, wnat):
                    nat = wnat.rearrange("p k (hh r two) -> p (k hh) two r", hh=2, r=32, two=2)
                    wa = WA[:].rearrange("p k (hh half r) -> p (k hh) half r", hh=2, half=2, r=32)
                    wb = WB[:].rearrange("p k (hh half r) -> p (k hh) half r", hh=2, half=2, r=32)
                    for half in range(2):
                        cp("dve", wa[:, :, half, :], nat[:, :, half, :], r=[wk(si)], pw=["aWA"])
                        cp("dve", wb[:, :, half, :], nat[:, :, 1 - half, :], r=[wk(si)], pw=["aWB"])

                def rope_block(pa, pb, bi, out_ap, okey, gidx):
                    t0, n = BLK[bi]
                    dma("sp", g[0][:, :n], DAP(dr["k_ropec"], 128 * t0, [[n, 128], [1, n]]), w=[gk[0]])
                    dma("sp", g[1][:, :n], DAP(dr["k_ropes"], 128 * t0, [[n, 128], [1, n]]), w=[gk[1]])
                    if not normed:
                        tt("dve", g[2][:, :n], ps[pa][:, :n], g[0][:, :n], ALU.mult, r=[pk(pa), gk[0]], w=[gk[2]])
                        tt("dve", g[3][:, :n], ps[pb][:, :n], g[1][:, :n], ALU.mult, r=[pk(pb), gk[1]], w=[gk[3]])
                        tt("dve", out_ap, g[2][:, :n], g[3][:, :n], ALU.add, r=[gk[2], gk[3]], pw=[okey])
                    else:
                        act(g[4][:, :n], ps[pa][:, :n], AF.Square, r=[pk(pa)], w=[gk[4]])
                        pq = psn()
                        mm(ps[pq][:, :n], bones[:], g[4][:, :n], True, True, r=["bones", gk[4]], w=[pk(pq)])
                        act(g[5][:, :n], ps[pq][:, :n], AF.Sqrt, r=[pk(pq), "epsc"], w=[gk[5]], scale=1.0 / 64, bias=epsc[:, 0:1])
                        recip(g[5][:, :n], g[5][:, :n], r=[gk[5]], w=[gk[5]])
                        stt("dve", g[2][:, :n], ps[pa][:, :n], gq[:, gidx, l:l + 1], g[0][:, :n], ALU.mult, ALU.mult,
                            r=[pk(pa), "gq", gk[0]], w=[gk[2]])
                        stt("dve", g[3][:, :n], ps[pb][:, :n], gq[:, gidx + 1, l:l + 1], g[1][:, :n], ALU.mult, ALU.mult,
                            r=[pk(pb), "gq", gk[1]], w=[gk[3]])
                        tt("dve", g[2][:, :n], g[2][:, :n], g[3][:, :n], ALU.add, r=[gk[2], gk[3]], w=[gk[2]])
                        tt("dve", out_ap, g[2][:, :n], g[5][:, :n], ALU.mult, r=[gk[2], gk[5]], pw=[okey])

                si = wslot()
                wn = wload(si, 0, w_in[l][:, koff:koff + 128], 8, 128)
                build_perm(si, wn)
                for bi in range(5):
                    t0, n = BLK[bi]
                    pa = proj_fm(WA[:], "aWA", bi)
                    pb = proj_fm(WB[:], "aWB", bi)
                    rope_block(pa, pb, bi, KT[:, t0:t0 + n], ("aKT", bi), 2)
                si = wslot()
                wv_ = wload(si, 0, w_in[l][:, voff:voff + 128], 8, 128)
                for tg in range(5):
                    tiles = list(range(tg * 4, min(tg * 4 + 4, 18)))
                    pi = psn()
                    for q, tt_ in enumerate(tiles):
                        bi = blk_of(tt_ * 128)
                        for kt in range(8):
                            mm(ps[pi][:, q * 128:(q + 1) * 128], h_sb[:, kt, tt_ * 128:(tt_ + 1) * 128], wv_[:, kt, :], kt == 0, kt == 7,
                               r=[wk(si), ("h", kt, bi)], w=[pk(pi)])
                    nt_ = len(tiles)
                    cp("act", Vt[:, tiles[0]:tiles[0] + nt_, :], ps[pi][:, 0:nt_ * 128].rearrange("p (q e) -> p q e", e=128),
                       r=[pk(pi)], pw=["aVt"])

                def attend(i, qt0, qc, ranges, band_lo, sink, use_max):
                    po = psacc()
                    for hh in range(2):
                        head = i + 4 * hh
                        rows = slice(hh * 64, hh * 64 + 64)
                        lhsT = QTb[rows, qc:qc + 128]
                        pv_list = []
                        if use_max:
                            banks = []
                            for (k0, nk) in ranges:
                                pi = psn()
                                mm(ps[pi][:, 0:nk], lhsT, KT[rows, k0:k0 + nk], True, True,
                                   r=["aQT"] + [("aKT", kb) for kb in sorted({blk_of(k0), blk_of(k0 + nk - 1)})], w=[pk(pi)])
                                banks.append(pi)
                            for j, pi in enumerate(banks):
                                nk = ranges[j][1]
                                A("dve", lambda e, j=j, pi=pi, nk=nk: e.reduce_max(out=mx[:, j:j + 1], in_=ps[pi][:, 0:nk], axis=AX.X),
                                  [pk(pi)], [("amx", j)])
                            mkeys = [("amx", 0)]
                            if len(banks) == 2:
                                tt("dve", mx[:, 0:1], mx[:, 0:1], mx[:, 1:2], ALU.max, r=[("amx", 0), ("amx", 1)], w=[("amx", 0)])
                            if sink:
                                ts("dve", negm[:], mx[:, 0:1], m0125[:, 0:1], negsink[:, l * 8 + head:l * 8 + head + 1], ALU.mult, ALU.min,
                                   r=[("amx", 0), "negsink", "m0125"], w=["anegm"])
                            else:
                                ts1("dve", negm[:], mx[:, 0:1], -0.125, ALU.mult, r=[("amx", 0)], w=["anegm"])
                            col = 0
                            for j, pi in enumerate(banks):
                                nk = ranges[j][1]
                                act(Pb[:, col:col + nk], ps[pi][:, 0:nk], AF.Exp, r=[pk(pi), "anegm"], pw=["aPb"], scale=0.125, bias=negm[:, 0:1])
                                col += nk
                            if band_lo is not None:
                                nbk = ranges[1][1]
                                tt("dve", Pb[:, 256:256 + nbk], Pb[:, 256:256 + nbk], band01[:, band_lo:band_lo + nbk], ALU.mult,
                                   r=["aPb", "band01"], w=["aPb"])
                            A("dve", lambda e, col=col: e.reduce_sum(out=rs[:, 0:1], in_=Pb[:, 0:col], axis=AX.X), ["aPb"], [("ars", 0)])
                            if sink:
                                act(es_[:], sinkrep[:, l * 8 + head:l * 8 + head + 1], AF.Exp, r=["sinkrep", "anegm"], w=["aes"], bias=negm[:, 0:1], scale=1.0)
                                tt("dve", rs[:, 0:1], rs[:, 0:1], es_[:], ALU.add, r=[("ars", 0), "aes"], w=[("ars", 0)])
                            recip(rinv2[:, hh:hh + 1], rs[:, 0:1], r=[("ars", 0)], w=[("arinv", hh)])
                            ntile = col // 128
                            pt = psn()
                            for kt in range(ntile):
                                tr(psb(pt)[:, kt * 128:(kt + 1) * 128], Pb[:, kt * 128:(kt + 1) * 128], identb[:], r=["aPb", "identb"], w=[pk(pt)])
                            cp("act", PT[:, 0:ntile * 128], psb(pt)[:, 0:ntile * 128], r=[pk(pt)], w=["aPT"])
                            ktile = []
                            for (k0, nk) in ranges:
                                ktile += [k0 // 128 + q for q in range(nk // 128)]
                            for kt in range(ntile):
                                mm(ps[po][:, hh * 64:(hh + 1) * 64], PT[:, kt * 128:(kt + 1) * 128], Vt[:, ktile[kt], hh * 64:(hh + 1) * 64],
                                   kt == 0, kt == ntile - 1, r=["aPT", "aVt"], w=[pk(po)])
                        else:
                            nr = len(ranges)

                            def stage1(j):
                                k0, nk = ranges[j]
                                pi = psn()
                                mm(ps[pi][:, 0:nk], lhsT, KT[rows, k0:k0 + nk], True, True, r=["aQT", ("aKT", blk_of(k0))], w=[pk(pi)])
                                hb = (j % 2) * 512
                                act(Pb[:, hb:hb + nk], ps[pi][:, 0:nk], AF.Exp, r=[pk(pi)], w=[("aPb", j % 2), ("ars", j)], scale=0.125,
                                    accum_out=rs[:, j:j + 1])

                            def stage2(j):
                                k0, nk = ranges[j]
                                hb = (j % 2) * 512
                                pbk = ("aPb", j % 2)
                                ntile = nk // 128
                                pt = psn()
                                for kt in range(ntile):
                                    tr(psb(pt)[:, kt * 128:(kt + 1) * 128], Pb[:, hb + kt * 128:hb + (kt + 1) * 128], identb[:], r=[pbk, "identb"], w=[pk(pt)])
                                ptk = ("aPT", j % 2)
                                cp("act" if j % 2 == 0 else "dve", PT[:, hb:hb + ntile * 128], psb(pt)[:, 0:ntile * 128], r=[pk(pt)], w=[ptk])
                                for kt in range(ntile):
                                    mm(ps[po][:, hh * 64:(hh + 1) * 64], PT[:, hb + kt * 128:hb + (kt + 1) * 128],
                                       Vt[:, k0 // 128 + kt, hh * 64:(hh + 1) * 64], j == 0 and kt == 0, j == nr - 1 and kt == ntile - 1,
                                       r=[ptk, "aVt"], w=[pk(po)])

                            stage1(0)
                            for j in range(nr):
                                if j + 1 < nr:
                                    stage1(j + 1)
                                stage2(j)
                            if nr > 1:
                                A("dve", lambda e, nr=nr: e.reduce_sum(out=rs[:, 0:1], in_=rs[:, 0:nr], axis=AX.X),
                                  [("ars", j) for j in range(nr)], [("ars", 0)])
                            recip(rinv2[:, hh:hh + 1], rs[:, 0:1], r=[("ars", 0)], w=[("arinv", hh)])
                    tt("dve", on[:].rearrange("p (a d) -> p a d", d=64), ps[po][:, 0:128].rearrange("p (a d) -> p a d", d=64),
                       rinv2[:].unsqueeze(2).to_broadcast([128, 2, 64]), ALU.mult, r=[pk(po), ("arinv", 0), ("arinv", 1)], w=["aon"])
                    pt = psn()
                    tr(psb(pt)[:, 0:128], on[:], identb[:], r=["aon", "identb"], w=[pk(pt)])
                    bi_ = blk_of(qt0)
                    bt0, bn = BLK[bi_]
                    sidx = ysc_ctr[0] % 2
                    cp("act", ystg[sidx][:, qt0 - bt0:qt0 - bt0 + 128], psb(pt)[:, 0:128], r=[pk(pt)], pw=["ays%d" % sidx])
                    if qt0 + 128 == bt0 + bn:
                        dma("sp", ysc_blk(br, i, bi_), ystg[sidx][:].bitcast(F32)[:, 0:bn // 2], r=["ays%d" % sidx], w=[("ysc", br, i, bi_)])
                        ysc_ctr[0] += 1

                for i in range(4):
                    si = wslot()
                    wn = wsl[si][:, 0:1024].rearrange("p (k n) -> p k n", n=128)
                    for hh in range(2):
                        hd = i + 4 * hh
                        dst = wn[:, :, hh * 64:(hh + 1) * 64]
                        src = w_in[l][:, qoff + hd * 64:qoff + (hd + 1) * 64].rearrange("(k p) n -> p k n", p=128)
                        dma("pool", dst, src, pw=[wk(si)])
                    build_perm(si, wn)
                    for bi in range(5):
                        if bi == 0 and last:
                            continue
                        t0, n = BLK[bi]
                        pa = proj_fm(WA[:], "aWA", bi)
                        pb = proj_fm(WB[:], "aWB", bi)
                        rope_block(pa, pb, bi, QTb[:, :n], "aQT", 0)
                        for qi in range(n // 128):
                            qt0 = t0 + qi * 128
                            if bi == 0:
                                attend(i, qt0, qi * 128, [(0, 256)], None, br == 0, br == 0)
                            elif br == 0:
                                nq = (qt0 - 256) // 128
                                lo = max(nq - 1, 0)
                                hi = min(nq + 2, 16)
                                attend(i, qt0, qi * 128, [(0, 256), (256 + lo * 128, (hi - lo) * 128)], 128 if nq == 0 else 0, True, True)
                            else:
                                attend(i, qt0, qi * 128, [(0, 512), (512, 512), (1024, 512), (1536, 512), (2048, 256)], None, False, False)
                S.barrier()

        def merge(l, b, last):
            with ExitStack() as ph:
                uT = ph.enter_context(sbt("uT", [128, 8, 1280], BF16))
                yS = [ph.enter_context(sbt("yS%d" % n_, [128, 4, 1280], BF16)) for n_ in range(3)]
                sbs = [[0, 1, 2], [3, 4]]
                if last:
                    sbs = [[1, 2], [3, 4]]
                for sbi, blocks in enumerate(sbs):
                    base = 0 if sbi == 0 else 1280
                    for n_ in range(3):
                        for bi in blocks:
                            t0, n = BLK[bi]
                            ysf = yS[n_][:].rearrange("p k t -> p (k t)").bitcast(F32).rearrange("p (k t) -> p k t", k=4)
                            for k_ in range(4):
                                dma("sp", ysf[:, k_, (t0 - base) // 2:(t0 - base + n) // 2], ysc_blk(n_, k_, bi),
                                    r=[("ysc", n_, k_, bi)], pw=[("yS", n_)])
                    for dt in range(8):
                        sG = wslot()
                        Wg01 = [wload(sG, n_ * 1024, w_in[l][:, GT + n_ * 1024 + dt * 128:GT + n_ * 1024 + (dt + 1) * 128], 8, 128) for n_ in range(2)]
                        sG2 = wslot()
                        Wg2 = wload(sG2, 0, w_in[l][:, GT + 2048 + dt * 128:GT + 2048 + (dt + 1) * 128], 8, 128)
                        sW = wslot()
                        wbv = wsl[sW][:, 0:1536].rearrange("p (n k c) -> p n k c", n=3, k=4)
                        wb_ = dr["w_branch"]
                        for n_ in range(2):
                            for hh in range(2):
                                src = wb_[l, n_, hh * 256:(hh + 1) * 256, dt * 128:(dt + 1) * 128].rearrange("(j r) c -> r j c", r=64)
                                dma("pool", wbv[hh * 64:(hh + 1) * 64, n_, :, :], src, pw=[wk(sW)])
                        dma("pool", wbv[:, 2, :, :], wb_[l, 2, :, dt * 128:(dt + 1) * 128].rearrange("(k p) c -> p k c", p=128), pw=[wk(sW)])
                        Wgs = [(Wg01[0], wk(sG)), (Wg01[1], wk(sG)), (Wg2, wk(sG2))]
                        for bi in blocks:
                            t0, n = BLK[bi]
                            for n_ in range(3):
                                pg = proj_fm(Wgs[n_][0], Wgs[n_][1], bi)
                                sgt = g[n_ % 2]
                                act(sgt[:, :n], ps[pg][:, :n], AF.Sigmoid, r=[pk(pg)], w=[gk[n_ % 2]])
                                pu = psn()
                                for kt in range(4):
                                    mm(ps[pu][:, :n], wbv[:, n_, kt, :], yS[n_][:, kt, t0 - base:t0 - base + n], kt == 0, kt == 3,
                                       r=[wk(sW), ("yS", n_)], w=[pk(pu)])
                                if n_ == 0:
                                    tt("dve", g[2][:, :n], sgt[:, :n], ps[pu][:, :n], ALU.mult, r=[gk[0], pk(pu)], w=[gk[2]])
                                else:
                                    tt("dve", g[3][:, :n], sgt[:, :n], ps[pu][:, :n], ALU.mult, r=[gk[n_ % 2], pk(pu)], w=[gk[3]])
                                    if n_ == 1:
                                        tt("dve", g[2][:, :n], g[2][:, :n], g[3][:, :n], ALU.add, r=[gk[2], gk[3]], w=[gk[2]])
                                    else:
                                        tt("dve", uT[:, dt, t0 - base:t0 - base + n], g[2][:, :n], g[3][:, :n], ALU.add,
                                           r=[gk[2], gk[3]], pw=[("uT", dt)])
                    for half in range(4):
                        sO = wslot()
                        Wo = wload(sO, 0, dr["w_out"][l][:, half * 256:(half + 1) * 256], 8, 256)
                        for q in range(2):
                            dt2 = half * 2 + q
                            for bi in blocks:
                                t0, n = BLK[bi]
                                j = nb if bi == 0 else b
                                pi = psn()
                                for kt in range(8):
                                    mm(ps[pi][:, :n], Wo[:, kt, q * 128:(q + 1) * 128], uT[:, kt, t0 - base:t0 - base + n], kt == 0, kt == 7,
                                       r=[wk(sO), ("uT", kt)], w=[pk(pi)])
                                stt("dve", x_sb[:, dt2, t0:t0 + n], ps[pi][:, :n], mod(l, 2, dt2, j), x_sb[:, dt2, t0:t0 + n], ALU.mult, ALU.add,
                                    r=[pk(pi), "m_all", ("x", dt2, bi)], w=[("x", dt2, bi)])
                S.barrier()

        def ffn(l, b, last):
            blocks = [1, 2, 3, 4] if last else [0, 1, 2, 3, 4]
            wsrc = dr["w_dw"][l].rearrange("k (ct p) -> (k ct) p", p=128)
            wdf = wdw.rearrange("p k c -> p (k c)")
            for (r0_, rn_) in ((0, 64), (64, 64), (128, 4)):
                loadT(wdf[:, r0_:r0_ + rn_], wsrc[r0_:r0_ + rn_, :], rn_, "wdw")
            bsrc_ = dr["b_dw"][l].rearrange("(ct p) -> ct p", p=128)
            for (r0_, rn_) in ((0, 32), (32, 8), (40, 4)):
                loadT(bdw[:, r0_:r0_ + rn_], bsrc_[r0_:r0_ + rn_, :], rn_, "bdw")
            norm_to_h(l, b, 1, blocks)
            seqs = [(256, T)] if last else [(0, 256), (256, T)]
            with ExitStack() as ph:
                ua = ph.enter_context(sbt("fua", [128, T], F32))
                ug = ph.enter_context(sbt("fug", [128, T], F32))
                ca = ph.enter_context(sbt("fca", [128, T], F32))
                cg_ = ph.enter_context(sbt("fcg", [128, T], F32))
                actb = ph.enter_context(sbt("fact", [128, 2, T], BF16))
                lo_t = seqs[0][0]
                for grp in range(11):
                    sA = wslot()
                    Wa = wload(sA, 0, dr["w_up"][l][:, grp * 256:(grp + 1) * 256], 8, 256)
                    sG = wslot()
                    Wg = wload(sG, 0, dr["w_up"][l][:, 2816 + grp * 256:2816 + (grp + 1) * 256], 8, 256)
                    sD = wslot()
                    Wd = wload(sD, 0, dr["w_down"][l][grp * 256:(grp + 1) * 256, :], 2, 1024)
                    for ci in range(2):
                        cta = grp * 2 + ci
                        ctg = 22 + cta
                        for (Wx, wkx, ct, uu, cc, un, cn) in ((Wa, wk(sA), cta, ua, ca, "fua", "fca"), (Wg, wk(sG), ctg, ug, cg_, "fug", "fcg")):
                            for bi in blocks:
                                t0, n = BLK[bi]
                                pi = proj_fm(Wx[:, :, ci * 128:(ci + 1) * 128], wkx, bi)
                                act(cc[:, t0:t0 + n], ps[pi][:, :n], AF.Identity, r=[pk(pi), "wdw", "bdw"], pw=[cn],
                                    scale=wdw[:, 1, ct:ct + 1], bias=bdw[:, ct:ct + 1])
                                cp("dve", uu[:, t0:t0 + n], ps[pi][:, :n], r=[pk(pi)], pw=[un])
                            for (s0, s1_) in seqs:
                                stt("dve", cc[:, s0 + 1:s1_], uu[:, s0:s1_ - 1], wdw[:, 0, ct:ct + 1], cc[:, s0 + 1:s1_], ALU.mult, ALU.add,
                                    r=[un, cn, "wdw"], w=[cn])
                                stt("dve", cc[:, s0:s1_ - 1], uu[:, s0 + 1:s1_], wdw[:, 2, ct:ct + 1], cc[:, s0:s1_ - 1], ALU.mult, ALU.add,
                                    r=[un, cn, "wdw"], w=[cn])
                        act(cg_[:, lo_t:T], cg_[:, lo_t:T], AF.Silu, r=["fcg"], w=["fcg"])
                        tt("dve", actb[:, ci, lo_t:T], ca[:, lo_t:T], cg_[:, lo_t:T], ALU.mult, r=["fca", "fcg"], w=[("fact", ci)])
                    for dt in range(8):
                        for bi in blocks:
                            t0, n = BLK[bi]
                            j = nb if bi == 0 else b
                            pi = psn()
                            for ci in range(2):
                                mm(ps[pi][:, :n], Wd[:, ci, dt * 128:(dt + 1) * 128], actb[:, ci, t0:t0 + n], ci == 0, ci == 1,
                                   r=[wk(sD), ("fact", ci)], w=[pk(pi)])
                            stt("dve", x_sb[:, dt, t0:t0 + n], ps[pi][:, :n], mod(l, 5, dt, j), x_sb[:, dt, t0:t0 + n], ALU.mult, ALU.add,
                                r=[pk(pi), "m_all", ("x", dt, bi)], w=[("x", dt, bi)])
                S.barrier()

        def final_store(b):
            with ExitStack() as ph:
                stage = [ph.enter_context(sbt("ostg%d" % i, [128, 1024], F32)) for i in range(2)]
                xn = ph.enter_context(sbt("xnf", [128, 8, 512], F32))
                cnt = 0
                for bi in range(1, 5):
                    t0, n = BLK[bi]
                    pst = psn()
                    for dt in range(8):
                        sq = g[dt % 2]
                        act(sq[:, :n], x_sb[:, dt, t0:t0 + n], AF.Square, r=[("x", dt, bi)], w=[gk[dt % 2]])
                        mm(ps[pst][:, :n], onesf[:], sq[:, :n], dt == 0, dt == 7, r=["onesf", gk[dt % 2]], w=[pk(pst)])
                    act(g[2][:, :n], ps[pst][:, :n], AF.Sqrt, r=[pk(pst), "epsc"], w=[gk[2]], scale=1.0 / D, bias=epsc[:, 0:1])
                    recip(g[2][:, :n], g[2][:, :n], r=[gk[2]], w=[gk[2]])
                    for dt in range(8):
                        stt("dve", xn[:, dt, :], x_sb[:, dt, t0:t0 + n], nfin[:, dt:dt + 1], g[2][:, :n], ALU.mult, ALU.mult,
                            r=[("x", dt, bi), "nfin", gk[2]], w=[("xnf", dt)])
                    for qi in range(4):
                        st = stage[cnt % 2]
                        sk = "ostg%d" % (cnt % 2)
                        cnt += 1
                        for half in range(2):
                            pi = psn()
                            for q in range(4):
                                dt = half * 4 + q
                                tr(ps[pi][:, q * 128:(q + 1) * 128], xn[:, dt, qi * 128:(qi + 1) * 128], identf[:], r=[("xnf", dt), "identf"], w=[pk(pi)])
                            cp("dve" if half == 0 else "act", st[:, half * 512:(half + 1) * 512], ps[pi][:], r=[pk(pi)], pw=[sk])
                        tok = t0 - 256 + qi * 128
                        dma("sp", y_d[b, tok:tok + 128, :], st[:], r=[sk])
                S.barrier()

        for b in range(nb):
            load_x(b)
            for l in range(depth):
                last = l == LYR - 1
                S.epoch += 1
                layer_scalars(l, b)
                norm_to_h(l, b, 0, [0, 1, 2, 3, 4])
                if "h" in dump_names and b == 0 and l == 0:
                    dumpx("h", h_sb[:], [128, 8, T], [("h", dt, bi) for dt in range(8) for bi in range(5)])
                if "hgrn" in phases:
                    hgrn(l, b, last)
                if "attnA" in phases:
                    attn(l, b, 0, last)
                if "attnB" in phases:
                    attn(l, b, 1, last)
                if b == 0 and l == 0 and "ysc" in dump_names:
                    dumpx("ysc", ysc, [3 * 4 * 128 * (T // 2)], [("ysc", n_, k_, bi) for n_ in range(3) for k_ in range(4) for bi in range(5)])
                if "merge" in phases:
                    merge(l, b, last)
                if b == 0 and l == 0:
                    dumpx("xmix", x_sb[:], [128, 8, T], [("x", dt, bi) for dt in range(8) for bi in range(5)])
                if "ffn" in phases:
                    ffn(l, b, last)
                if b == 0 and l == 0:
                    dumpx("xffn", x_sb[:], [128, 8, T], [("x", dt, bi) for dt in range(8) for bi in range(5)])
            final_store(b)
        S.emit(nc)
    return nc, dump_d


_CONSTS = None


def run(inputs, nb=2, depth=4, dump=(), ncores=8, trace=False, phases=("hgrn", "attnA", "attnB", "merge", "ffn")):
    global _CONSTS
    if _CONSTS is None:
        _CONSTS = make_consts()
    nc, dump_d = build(nb=nb, depth=depth, dump=dump, phases=phases)
    shared = {}
    for name, _ in WNAMES:
        shared[name] = np.ascontiguousarray(np.asarray(inputs[name], dtype=np.float32))
    shared["c_ctx"] = np.ascontiguousarray(np.asarray(inputs["c_ctx"], dtype=np.float32))
    shared.update(_CONSTS)
    x = np.asarray(inputs["x"], dtype=np.float32)
    c = np.asarray(inputs["c"], dtype=np.float32)
    ctx = np.asarray(inputs["ctx"], dtype=np.float32)
    in_maps = []
    for core in range(ncores):
        m = dict(shared)
        m["x"] = np.ascontiguousarray(x[core * nb:(core + 1) * nb])
        m["c"] = np.ascontiguousarray(c[core * nb:(core + 1) * nb])
        m["ctx"] = np.ascontiguousarray(ctx[core * nb:(core + 1) * nb])
        in_maps.append(m)
    res = run_bass_kernel_spmd(nc, in_maps, core_ids=list(range(ncores)), trace=trace)
    return res


def kernel(**inputs):
    res = run(inputs, nb=2, depth=4, ncores=8)
    out = np.concatenate([np.asarray(r["y"]) for r in res.results], axis=0)
    return out.astype(np.float32)
```

```python
import os
import numpy as np
import concourse.bass as bass
import concourse.mybir as mybir
from concourse.bass_utils import run_bass_kernel_spmd
from contextlib import ExitStack

F32 = mybir.dt.float32
BF16 = mybir.dt.bfloat16
AF = mybir.ActivationFunctionType
ALU = mybir.AluOpType
AX = mybir.AxisListType

D = 1024
SL = 2048
NCX = 256
T = 2304
LYR = 4
NCH = 72
BLK = [(0, 256), (256, 512), (768, 512), (1280, 512), (1792, 512)]
EPS = 1e-6
HSTAGE = int(os.environ.get("HSTAGE", "0"))
AQ, AK, AV, BQ, BK, BV, CQ, CF, CB, CI, CG, GT = 0, 512, 640, 768, 1280, 1408, 1536, 2048, 2560, 3072, 3584, 4096


def blk_of(t):
    return 0 if t < 256 else 1 + (t - 256) // 512


ENGS = ("pe", "dve", "act", "pool", "sp")
NSLOT = {"sp": 8, "act": 4, "pool": 8}


class Op:
    __slots__ = ("eng", "fn", "cs", "deps", "needs_inc", "value", "is_dma", "idx")


class Sched:
    def __init__(self):
        self.ops = []
        self.lastw = {}
        self.readers = {}
        self.dma_rr = {"sp": 0, "act": 0, "pool": 0}
        self.slot_last = {}
        self.pending_bar = {}
        self.last_of_cs = {}
        self.genbase = {}
        self.epoch = 0

    def barrier(self):
        deps = list(self.last_of_cs.values())
        for e in ENGS:
            self.pending_bar[e] = set(deps) | self.pending_bar.get(e, set())

    def add(self, eng, fn, reads=(), writes=(), dma=False, pwrites=()):
        op = Op()
        op.eng = eng
        op.fn = fn
        op.is_dma = dma
        op.idx = len(self.ops)
        op.needs_inc = dma
        op.value = None
        if dma:
            s = self.dma_rr[eng]
            self.dma_rr[eng] = (s + 1) % NSLOT[eng]
            op.cs = ("dma", eng, s)
        else:
            op.cs = (eng, self.epoch)
        deps = set()
        raww = set()
        for k in reads:
            for w in self.lastw.get(k, ()):
                deps.add(w)
                raww.add(w)
            if isinstance(k, str) and k.startswith("ps") and len(k) == 3:
                for cs, i in self.readers.get(k, {}).items():
                    if cs != op.cs:
                        deps.add(i)
        for k in writes:
            for w in self.lastw.get(k, ()):
                deps.add(w)
                raww.add(w)
            for cs, i in self.readers.get(k, {}).items():
                deps.add(i)
        for k in pwrites:
            rd = self.readers.get(k)
            if rd:
                self.genbase[k] = set(rd.values()) | set(self.lastw.get(k, ()))
            for w in self.genbase.get(k, ()):
                deps.add(w)
                raww.add(w)
        if dma:
            pl = self.slot_last.get(op.cs)
            if pl is not None:
                deps.add(pl)
            self.slot_last[op.cs] = op.idx
        bar = self.pending_bar.pop(eng, None)
        if bar:
            deps |= bar
        final = []
        for d in deps:
            dop = self.ops[d]
            if dop.cs == op.cs and not dma:
                if op.eng == "pe":
                    continue
            final.append(d)
        op.deps = final
        for k in reads:
            self.readers.setdefault(k, {})[op.cs] = op.idx
        for k in writes:
            self.lastw[k] = [op.idx]
            self.readers[k] = {}
            self.genbase[k] = {op.idx}
        for k in pwrites:
            if self.readers.get(k):
                self.lastw[k] = [op.idx]
                self.readers[k] = {}
            else:
                self.lastw.setdefault(k, []).append(op.idx)
        self.last_of_cs[op.cs] = op.idx
        self.ops.append(op)
        return op

    def emit(self, nc, final_wait_eng="sp"):
        ops = self.ops
        for op in ops:
            for d in op.deps:
                ops[d].needs_inc = True
        cnt = {}
        for op in ops:
            if op.needs_inc:
                inc = 16 if op.is_dma else 1
                cnt[op.cs] = cnt.get(op.cs, 0) + inc
                op.value = cnt[op.cs]
        cs_list = sorted(cnt.keys(), key=str)
        with ExitStack() as es:
            sems = {}
            for cs in cs_list:
                nm = "s_" + ("_".join(str(c) for c in cs) if isinstance(cs, tuple) else cs)
                nm = nm.replace(" ", "")
                sems[cs] = es.enter_context(nc.semaphore(nm))
            block = es.enter_context(nc.Block())
            streams = {e: [] for e in ENGS}
            for op in ops:
                streams[op.eng].append(op)

            def run(eng_name, eng):
                seen = {}
                for op in streams[eng_name]:
                    need = {}
                    for d in op.deps:
                        dop = ops[d]
                        v = dop.value
                        if seen.get(dop.cs, 0) < v and need.get(dop.cs, 0) < v:
                            need[dop.cs] = v
                    for cs, v in need.items():
                        eng.wait_ge(sems[cs], v)
                        seen[cs] = v
                    ins = op.fn(eng)
                    if op.needs_inc:
                        ins.then_inc(sems[op.cs], 16 if op.is_dma else 1)
                if eng_name == final_wait_eng:
                    for cs, v in cnt.items():
                        if seen.get(cs, 0) < v:
                            eng.wait_ge(sems[cs], v)

            @block.tensor
            def _(e):
                run("pe", e)

            @block.vector
            def _(e):
                run("dve", e)

            @block.scalar
            def _(e):
                run("act", e)

            @block.gpsimd
            def _(e):
                run("pool", e)

            @block.sync
            def _(e):
                run("sp", e)
        return cnt


def make_consts():
    c = {}
    c["k_ident"] = np.eye(128, dtype=np.float32)
    bo = np.zeros((128, 128), np.float32)
    bo[:64, :64] = 1.0
    bo[64:, 64:] = 1.0
    c["k_bones"] = bo
    ii = np.arange(128)[:, None]
    jj = np.arange(128)[None, :]
    band = np.ones((128, 384), np.float32)
    band[:, 0:128] = (jj >= ii)
    band[:, 256:384] = (jj <= ii)
    c["k_band"] = band
    j6 = (np.arange(64) % 32)[:, None]
    t6 = np.arange(32)[None, :]
    tri = np.zeros((64, 64), np.float32)
    tri[:, 0:32] = (j6 <= t6)
    tri[:, 32:64] = (j6 >= t6)
    c["k_tri"] = tri
    pos = np.arange(SL)
    row = (pos // 64).astype(np.float32)
    col = (pos % 64).astype(np.float32)
    inv = (np.float32(10000.0) ** (-np.arange(0, 32, 2, dtype=np.float32) / np.float32(32))).astype(np.float32)
    ang = np.concatenate([row[:, None] * inv, col[:, None] * inv], axis=-1).astype(np.float32)
    cs_ = np.cos(ang).astype(np.float32)
    sn_ = np.sin(ang).astype(np.float32)
    C = np.ones((128, T), np.float32)
    Sg = np.zeros((128, T), np.float32)
    for p in range(128):
        r = p % 64
        pair = r % 32
        C[p, NCX:] = cs_[:, pair]
        Sg[p, NCX:] = sn_[:, pair] * (-1.0 if r < 32 else 1.0)
    def blocked(a):
        return np.concatenate([np.ascontiguousarray(a[:, t0:t0 + n]).reshape(-1) for (t0, n) in BLK]).astype(np.float32)
    c["k_ropec"] = blocked(C)
    c["k_ropes"] = blocked(Sg)
    return c


WNAMES = [("w_mod", (4, 1024, 6144)), ("b_mod", (4, 6144)), ("norm_mix", (4, 1024)), ("norm_ffn", (4, 1024)),
          ("w_in", (4, 1024, 7168)), ("sink_a", (4, 8)), ("qn_b", (4, 64)), ("kn_b", (4, 64)),
          ("lb_fwd", (4, 512)), ("lb_bwd", (4, 512)), ("gn_c", (4, 128)), ("w_branch", (4, 3, 512, 1024)),
          ("w_out", (4, 1024, 1024)), ("w_up", (4, 1024, 5632)), ("w_dw", (4, 3, 5632)), ("b_dw", (4, 5632)),
          ("w_down", (4, 2816, 1024)), ("norm_final", (1024,))]
CNAMES = [("k_ident", (128, 128)), ("k_bones", (128, 128)), ("k_band", (128, 384)), ("k_tri", (64, 64)),
          ("k_ropec", (128 * T,)), ("k_ropes", (128 * T,))]


def build(nb=2, depth=4, dump=(), phases=("hgrn", "attnA", "attnB", "merge", "ffn")):
    nc = bass.Bass("TRN2", target_bir_lowering=False)
    S = Sched()
    dr = {}
    for name, shape in [("x", (nb, SL, D)), ("c", (nb, D)), ("ctx", (nb, NCX, D)), ("c_ctx", (D,))] + WNAMES + CNAMES:
        dr[name] = nc.dram_tensor(name, list(shape), F32, kind="ExternalInput").ap()
    y_d = nc.dram_tensor("y", [nb, SL, D], F32, kind="ExternalOutput").ap()
    ysc = nc.dram_tensor("ysc", [3 * 4 * 128 * (T // 2)], F32).ap()

    def ysc_blk(n_, k_, bi_):
        t0_, bn_ = BLK[bi_]
        off = (n_ * 4 + k_) * 128 * (T // 2) + 128 * (t0_ // 2)
        return bass.AP(ysc.tensor, off, [[bn_ // 2, 128], [1, bn_ // 2]])

    def ysc_blk4(n_, bi_):
        t0_, bn_ = BLK[bi_]
        off = (n_ * 4) * 128 * (T // 2) + 128 * (t0_ // 2)
        return bass.AP(ysc.tensor, off, [[bn_ // 2, 128], [128 * (T // 2), 4], [1, bn_ // 2]])
    ob_d = nc.dram_tensor("ob_d", [NCH, 32, 128], F32).ap()
    dump_d = {}
    dump_names = set(dump)
    w_in = dr["w_in"]

    def DAP(base, offset, pat):
        return bass.AP(base.tensor, offset, pat)

    es = ExitStack()
    with es:
        uid = [0]

        def sbt(name, shape, dt):
            uid[0] += 1
            return nc.sbuf_tensor("%s_%d" % (name, uid[0]), list(shape), dt)

        def sb(name, shape, dt):
            return es.enter_context(sbt(name, list(shape), dt))

        def A(eng, fn, r=(), w=(), dma=False, pw=()):
            return S.add(eng, fn, r, w, dma, pw)

        def dma(eng, out, in_, r=(), w=(), pw=(), slow=False):
            if slow:
                A(eng, lambda e: e.dma_start(out=out, in_=in_, allow_slow_non_contiguous=True), r, w, True, pw)
            else:
                A(eng, lambda e: e.dma_start(out=out, in_=in_), r, w, True, pw)

        def mm(out, lhsT, rhs, st, sp, r, w):
            A("pe", lambda e: e.matmul(out, lhsT=lhsT, rhs=rhs, start=st, stop=sp), r, w)

        def tr(out, in_, ident, r, w):
            A("pe", lambda e: e.transpose(out=out, in_=in_, identity=ident), r, w)

        def act(out, in_, func, r=(), w=(), pw=(), **kw):
            A("act", lambda e: e.activation(out=out, in_=in_, func=func, **kw), r, w, False, pw)

        def tt(eng, out, in0, in1, op, r=(), w=(), pw=()):
            A(eng, lambda e: e.tensor_tensor(out=out, in0=in0, in1=in1, op=op), r, w, False, pw)

        def ts(eng, out, in0, s1, s2, op0, op1, r=(), w=(), pw=()):
            A(eng, lambda e: e.tensor_scalar(out=out, in0=in0, scalar1=s1, scalar2=s2, op0=op0, op1=op1), r, w, False, pw)

        def ts1(eng, out, in0, s1, op0, r=(), w=(), pw=()):
            np_ = out.shape[0]
            if isinstance(s1, (int, float)):
                A(eng, lambda e: e.tensor_scalar(out=out, in0=in0, scalar1=float(s1), scalar2=0.0, op0=op0, op1=ALU.add), r, w, False, pw)
            else:
                A(eng, lambda e: e.tensor_scalar(out=out, in0=in0, scalar1=s1, scalar2=zeroc[0:np_, 0:1], op0=op0, op1=ALU.add),
                  list(r) + ["zeroc"], w, False, pw)

        def stt(eng, out, in0, sc, in1, op0, op1, r=(), w=(), pw=()):
            A(eng, lambda e: e.scalar_tensor_tensor(out=out, in0=in0, scalar=sc, in1=in1, op0=op0, op1=op1), r, w, False, pw)

        def cp(eng, out, in_, r=(), w=(), pw=()):
            if eng == "act":
                A(eng, lambda e: e.activation(out=out, in_=in_, func=AF.Copy), r, w, False, pw)
            else:
                A(eng, lambda e: e.tensor_copy(out=out, in_=in_), r, w, False, pw)

        def recip(out, in_, r=(), w=()):
            A("dve", lambda e: e.reciprocal(out=out, in_=in_), r, w)

        def memset(eng, ap, val, w=()):
            A(eng, lambda e: e.memset(ap, val), (), w)

        def dumpx(name, ap, shape, keys):
            if name in dump_names:
                d_ = nc.dram_tensor("dbg_" + name, list(shape), F32 if ap.dtype == F32 else BF16, kind="ExternalOutput").ap()
                dump_d[name] = d_
                dma("sp", d_, ap, r=keys)

        ps = [es.enter_context(nc.psum_tensor("ps%d" % i, [128, 512], F32)) for i in range(8)]
        pctr = [0]

        def psn():
            i = pctr[0] % 6
            pctr[0] += 1
            return i

        pactr = [0]

        def psacc():
            i = 6 + pactr[0] % 2
            pactr[0] += 1
            return i

        def pk(i):
            return "ps%d" % i

        def psb(i):
            return ps[i][:].bitcast(BF16)

        NWS = 4
        wsl = [sb("ws%d" % i, [128, 2048], BF16) for i in range(NWS)]
        wctr = [0]

        def wslot():
            i = wctr[0] % NWS
            wctr[0] += 1
            return i

        def wk(i):
            return "ws%d" % i

        def wload(si, off, src2d, kt, n):
            dst = wsl[si][:, off:off + kt * n].rearrange("p (k n) -> p k n", n=n)
            src = src2d.rearrange("(k p) n -> p k n", p=128)
            dma("pool", dst, src, pw=[wk(si)])
            return dst

        x_sb = sb("x_sb", [128, 8, T], F32)
        h_sb = sb("h_sb", [128, 8, T], BF16)
        gen = sb("gen", [128, 3072], F32)
        g = [gen[:, i * 512:(i + 1) * 512] for i in range(6)]
        gk = ["g%d" % i for i in range(6)]
        class Carver:
            def __init__(self, tile, n):
                self.t, self.n, self.off = tile, n, 0

            def __call__(self, shape):
                m = 1
                for d_ in shape[1:]:
                    m *= d_
                assert self.off + m <= self.n, (self.off, m, self.n)
                ap = self.t[0:shape[0], self.off:self.off + m]
                self.off += m
                if len(shape) == 3:
                    ap = ap.rearrange("p (a b) -> p a b", b=shape[2])
                elif len(shape) == 4:
                    ap = ap.rearrange("p (a b c) -> p a b c", b=shape[2], c=shape[3])
                return ap

        cf = Carver(sb("miscf", [128, 2600], F32), 2600)
        cb = Carver(sb("miscb", [128, 560], BF16), 560)
        identf = cf([128, 128])
        identb = cb([128, 128])
        onesf = cf([128, 128])
        bones = cf([128, 128])
        band01 = cb([128, 384])
        tri = cf([64, 64])
        m_all = cf([128, LYR, 48, 4])
        bmt = cf([128, LYR, 48])
        nmix = cf([128, LYR, 8])
        nffn = cf([128, LYR, 8])
        nfin = cf([128, 8])
        cT = cf([128, 4, 8])
        csb = cb([128, 4, 8])
        s1t = cf([128, 4, 8])
        lraw = cf([128, 2, 4, LYR])
        lbt = cf([128, 2, 4, LYR])
        omlt = cf([128, 2, 4, LYR])
        lsum = cf([128, 8])
        sinkrep = cf([128, LYR * 8])
        negsink = cf([128, LYR * 8])
        gq = cf([128, 4, LYR])
        gnrep = cf([64, LYR * 128])
        wdw = cf([128, 3, 44])
        bdw = cf([128, 44])
        epsc = cf([128, 1])
        onec = cf([128, 1])
        zeroc = cf([128, 1])
        m0125 = cf([128, 1])
        dma("sp", identf[:], dr["k_ident"], w=["identf"])
        dma("pool", identb[:], dr["k_ident"], w=["identb"])
        dma("sp", bones[:], dr["k_bones"], w=["bones"])
        dma("pool", band01[:], dr["k_band"], w=["band01"])
        dma("sp", tri[:], dr["k_tri"], w=["tri"])
        memset("dve", onesf[:], 1.0, w=["onesf"])
        memset("dve", epsc[:], EPS, w=["epsc"])
        memset("dve", onec[:], 1.0, w=["onec"])
        memset("dve", zeroc[:], 0.0, w=["zeroc"])
        memset("dve", m0125[:], -0.125, w=["m0125"])
        def loadT(dst_ap, src_rows, R, dkey, in_view=None):
            dma("sp", g[5][0:R, 0:128], src_rows, w=[gk[5]])
            pi = psn()
            tr(ps[pi][:, 0:R], g[5][0:R, 0:128], identf[0:R, 0:R], r=[gk[5], "identf"], w=[pk(pi)])
            src = ps[pi][:, 0:R] if in_view is None else in_view(ps[pi][:, 0:R])
            cp("dve", dst_ap, src, r=[pk(pi)], pw=[dkey])

        loadT(nmix.rearrange("p l d -> p (l d)"), dr["norm_mix"].rearrange("l (dt p) -> (l dt) p", p=128), 32, "nmix")
        loadT(nffn.rearrange("p l d -> p (l d)"), dr["norm_ffn"].rearrange("l (dt p) -> (l dt) p", p=128), 32, "nffn")
        loadT(nfin, dr["norm_final"].rearrange("(dt p) -> dt p", p=128), 8, "nfin")
        bmf = bmt.rearrange("p l c -> p (l c)")
        bsrc = dr["b_mod"].rearrange("l (ct p) -> (l ct) p", p=128)
        for hf_ in range(2):
            loadT(bmf[:, hf_ * 96:(hf_ + 1) * 96], bsrc[hf_ * 96:(hf_ + 1) * 96, :], 96, "bmt")
        for a_, nm_ in enumerate(["lb_fwd", "lb_bwd"]):
            loadT(lraw[:, a_].rearrange("p h l -> p l h"), dr[nm_].rearrange("l (h p) -> (l h) p", p=128), 16, "lraw%d" % a_,
                  in_view=lambda ap: ap.rearrange("p (l h) -> p l h", h=4))
        dma("sp", sinkrep[:], DAP(dr["sink_a"], 0, [[0, 128], [1, LYR * 8]]), w=["sinkrep"], slow=True)
        dma("sp", gnrep[:], DAP(dr["gn_c"], 0, [[0, 64], [1, LYR * 128]]), w=["gnrep"], slow=True)
        for gi, (nm, flip) in enumerate([("qn_b", 0), ("qn_b", 1), ("kn_b", 0), ("kn_b", 1)]):
            dma("sp", g[4][0:4, 0:64], dr[nm], w=[gk[4]])
            srcv = g[4][0:4, 0:64].rearrange("l (r two) -> l two r", two=2)
            for hp in range(2):
                for half in range(2):
                    c0 = 128 + hp * 64 + half * 32
                    cp("dve", g[4][0:4, c0:c0 + 32], srcv[:, half ^ flip, :], r=[gk[4]], pw=["g4perm"])
            pi = psn()
            tr(ps[pi][:, 0:4], g[4][0:4, 128:256], identf[0:4, 0:4], r=["g4perm", "identf"], w=[pk(pi)])
            cp("dve", gq[:, gi, :], ps[pi][:, 0:4], r=[pk(pi)], pw=["gq"])
        ts1("dve", negsink[:], sinkrep[:], -1.0, ALU.mult, r=["sinkrep"], w=["negsink"])

        lr = lraw[:].rearrange("p a h l -> p (a h) l")
        lb3 = lbt[:].rearrange("p a h l -> p (a h) l")
        om3 = omlt[:].rearrange("p a h l -> p (a h) l")
        act(lr, lr, AF.Exp, r=["lraw0", "lraw1"], w=["lraw"])
        A("dve", lambda e: e.reduce_sum(out=lsum[:], in_=lr, axis=AX.X), ["lraw"], ["lsum"])
        recip(lsum[:], lsum[:], r=["lsum"], w=["lsum"])
        tt("dve", lr, lr, lsum[:].unsqueeze(2).to_broadcast([128, 8, LYR]), ALU.mult, r=["lraw", "lsum"], w=["lraw"])
        memset("dve", lb3[:, :, 0:1], 0.0, w=["lb0"])
        cp("dve", lb3[:, :, 1:2], lr[:, :, 1:2], r=["lraw", "lb0"], w=["lb1"])
        tt("dve", lb3[:, :, 2:3], lb3[:, :, 1:2], lr[:, :, 2:3], ALU.add, r=["lraw", "lb1"], w=["lb2"])
        tt("dve", lb3[:, :, 3:4], lb3[:, :, 2:3], lr[:, :, 3:4], ALU.add, r=["lraw", "lb2"], w=["lb3"])
        ts("dve", lb3, lb3, 0.0, 1.0, ALU.max, ALU.min, r=["lb0", "lb1", "lb2", "lb3"], w=["lb"])
        ts("dve", om3, lb3, -1.0, 1.0, ALU.mult, ALU.add, r=["lb"], w=["oml"])

        NS = nb + 1
        memset("dve", cT[:], 0.0, w=["cT"])
        loadT(cT.rearrange("p j k -> p (j k)")[:, 0:nb * 8], dr["c"].rearrange("j (kt p) -> (j kt) p", p=128), nb * 8, "cT")
        loadT(cT[:, nb, :], dr["c_ctx"].rearrange("(kt p) -> kt p", p=128), 8, "cT")
        act(csb[:], cT[:], AF.Silu, r=["cT"], w=["csb"])
        for l in range(depth):
            for gi in range(24):
                si = wslot()
                wv = wload(si, 0, dr["w_mod"][l][:, gi * 256:(gi + 1) * 256], 8, 256)
                pi = psn()
                for ci in range(2):
                    for kt in range(8):
                        mm(ps[pi][:, ci * 4:ci * 4 + 4], wv[:, kt, ci * 128:(ci + 1) * 128], csb[:, :, kt], kt == 0, kt == 7,
                           r=[wk(si), "csb"], w=[pk(pi)])
                tt("dve", m_all[:, l, gi * 2:gi * 2 + 2, :], ps[pi][:, 0:8].rearrange("p (c j) -> p c j", j=4),
                   bmt[:, l, gi * 2:gi * 2 + 2].unsqueeze(2).to_broadcast([128, 2, 4]), ALU.add,
                   r=[pk(pi), "bmt"], pw=["m_all"])

        def mod(l, i, dt, j):
            return m_all[:, l, i * 8 + dt, j:j + 1]

        def load_x(b):
            with ExitStack() as ph:
                stage = [ph.enter_context(sbt("stg%d" % i, [128, 1024], F32)) for i in range(2)]
                for tt_ in range(18):
                    src = dr["ctx"][b, tt_ * 128:(tt_ + 1) * 128, :] if tt_ < 2 else dr["x"][b, (tt_ - 2) * 128:(tt_ - 1) * 128, :]
                    st = stage[tt_ % 2]
                    sk = "stg%d" % (tt_ % 2)
                    dma("sp", st[:], src, w=[sk])
                    bi = blk_of(tt_ * 128)
                    for half in range(2):
                        pi = psn()
                        for q in range(4):
                            dt = half * 4 + q
                            tr(ps[pi][:, q * 128:(q + 1) * 128], st[:, dt * 128:(dt + 1) * 128], identf[:], r=[sk, "identf"], w=[pk(pi)])
                        cp("dve" if half == 0 else "act", x_sb[:, half * 4:(half + 1) * 4, tt_ * 128:(tt_ + 1) * 128],
                           ps[pi][:].rearrange("p (q t) -> p q t", t=128), r=[pk(pi)],
                           pw=[("x", half * 4 + q, bi) for q in range(4)])
                S.barrier()

        def layer_scalars(l, b):
            for kind, (mi, nt, nk) in enumerate([(1, nmix, "nmix"), (1, nmix, "nmix"), (4, nffn, "nffn"), (4, nffn, "nffn")]):
                j = b if kind % 2 == 0 else nb
                stt("dve", s1t[:, kind, :], m_all[:, l, mi * 8:mi * 8 + 8, j], 1.0, nt[:, l, :], ALU.add, ALU.mult,
                    r=["m_all", nk], w=[("s1", kind)])

        def norm_to_h(l, b, which, blocks):
            for bi in blocks:
                t0, n = BLK[bi]
                j = nb if bi == 0 else b
                kind = which * 2 + (1 if bi == 0 else 0)
                pst = psn()
                for dt in range(8):
                    sq = g[dt % 2]
                    act(sq[:, :n], x_sb[:, dt, t0:t0 + n], AF.Square, r=[("x", dt, bi)], w=[gk[dt % 2]])
                    mm(ps[pst][:, :n], onesf[:], sq[:, :n], dt == 0, dt == 7, r=["onesf", gk[dt % 2]], w=[pk(pst)])
                act(g[2][:, :n], ps[pst][:, :n], AF.Sqrt, r=[pk(pst), "epsc"], w=[gk[2]], scale=1.0 / D, bias=epsc[:, 0:1])
                recip(g[2][:, :n], g[2][:, :n], r=[gk[2]], w=[gk[2]])
                for dt in range(8):
                    tmp = g[3 + dt % 2]
                    stt("dve", tmp[:, :n], x_sb[:, dt, t0:t0 + n], s1t[:, kind, dt:dt + 1], g[2][:, :n], ALU.mult, ALU.mult,
                        r=[("x", dt, bi), ("s1", kind), gk[2]], w=[gk[3 + dt % 2]])
                    act(h_sb[:, dt, t0:t0 + n], tmp[:, :n], AF.Identity, r=[gk[3 + dt % 2], "m_all"], w=[("h", dt, bi)],
                        bias=mod(l, 3 * which, dt, j), scale=1.0)

        def proj_fm(wv, wkey, bi, ncols=128):
            t0, n = BLK[bi]
            pi = psn()
            for kt in range(8):
                mm(ps[pi][0:ncols, :n], wv[:, kt, :], h_sb[:, kt, t0:t0 + n], kt == 0, kt == 7,
                   r=[wkey, ("h", kt, bi)], w=[pk(pi)])
            return pi

        def hgrn(l, b, last):
            CS = 32
            with ExitStack() as ph:
                def psb_(name, shape, dt):
                    return ph.enter_context(sbt(name, list(shape), dt))
                buf1 = gen[:, 0:T]
                buf2t = psb_("hb2", [128, T], F32)
                buf3t = psb_("hb3", [128, T], F32)
                buf2 = buf2t[:]
                buf3 = buf3t[:]
                qs = psb_("hqs", [128, T], BF16)
                qp = psb_("hqp", [128, T], BF16)
                kp = psb_("hkp", [128, T], BF16)
                vt = psb_("hvt", [64, 36, 128], BF16)
                ystg = [psb_("hys%d" % i_, [128, 512], BF16) for i_ in range(2)]
                ysc_ctr = [0]
                hf = Carver(psb_("hmf", [128, 2048], F32), 2048)
                hb_ = Carver(psb_("hmb", [128, 576], BF16), 576)
                sc = hf([128, 3, NCH])
                esc = hf([128, 3, NCH])
                Sst = hf([128, 128])
                Sbf2 = [hb_([128, 128]), hb_([128, 128])]
                scm = hb_([64, 32])
                kpT = hb_([64, 128])
                osum = hf([32, 128])
                junk = hf([32, 128])
                ssq = hf([32, 1])
                rst = hf([32, 1])
                sg = hf([32, 128])
                yv = hb_([32, 128])
                obw = [hf([32, 128]) for _ in range(4)]
                obr = [hf([32, 128]) for _ in range(4)]
                obc = [0, 0]
                b2_3 = buf2.rearrange("p (c i) -> p c i", i=CS)
                b3_3 = buf3.rearrange("p (c i) -> p c i", i=CS)
                MID, LST, REFB = 15, 31, 16

                for h in range(4):
                    sA = wslot()
                    Wq_ = wload(sA, 0, w_in[l][:, CQ + h * 128:CQ + (h + 1) * 128], 8, 128)
                    Wf_ = wload(sA, 1024, w_in[l][:, CF + h * 128:CF + (h + 1) * 128], 8, 128)
                    sB = wslot()
                    Wb_ = wload(sB, 0, w_in[l][:, CB + h * 128:CB + (h + 1) * 128], 8, 128)
                    Wi_ = wload(sB, 1024, w_in[l][:, CI + h * 128:CI + (h + 1) * 128], 8, 128)
                    sC = wslot()
                    Wg_ = wload(sC, 0, w_in[l][:, CG + h * 128:CG + (h + 1) * 128], 8, 128)
                    for bi in range(5):
                        t0, n = BLK[bi]
                        pi = proj_fm(Wq_, wk(sA), bi)
                        act(qs[:, t0:t0 + n], ps[pi][:, :n], AF.Silu, r=[pk(pi)], pw=["hqs"])
                    for cpair in range(18):
                        pi = psn()
                        for cc in range(2):
                            c = cpair * 2 + cc
                            bi = blk_of(c * 64)
                            for kt in range(8):
                                mm(ps[pi][0:64, cc * 128:(cc + 1) * 128], h_sb[:, kt, c * 64:(c + 1) * 64], Wi_[:, kt, :], kt == 0, kt == 7,
                                   r=[wk(sB), ("h", kt, bi)], w=[pk(pi)])
                        cp("act", vt[0:64, cpair * 2:cpair * 2 + 2, :], ps[pi][0:64, 0:256].rearrange("p (c e) -> p c e", e=128),
                           r=[pk(pi)], pw=["hvt"])

                    for dirn in (1, 0):
                        Wz, wzk = (Wf_, wk(sA)) if dirn == 0 else (Wb_, wk(sB))
                        for bi in range(5):
                            t0, n = BLK[bi]
                            pi = proj_fm(Wz, wzk, bi)
                            act(buf1[:, t0:t0 + n], ps[pi][:, :n], AF.Sigmoid, r=[pk(pi)], pw=["hb1"])
                        ts("dve", buf1, buf1, omlt[:, dirn, h, l:l + 1], lbt[:, dirn, h, l:l + 1], ALU.mult, ALU.add,
                           r=["hb1", "lb", "oml"], w=["hb1"])
                        act(buf2, buf1, AF.Ln, r=["hb1"], w=["hb2"])
                        A("dve", lambda e: e.tensor_tensor_scan(out=buf3, data0=onec[:, 0:1].to_broadcast([128, T]), data1=buf2,
                                                                 initial=0.0, op0=ALU.mult, op1=ALU.add),
                          ["hb2", "onec"], ["hb3"])
                        if dirn == 0:
                            cp("dve", sc[:, 0, 0:1], b3_3[:, 0, MID:MID + 1], r=["hb3"], w=["hsc0a"])
                            tt("dve", sc[:, 0, 1:NCH], b3_3[:, 1:NCH, MID], b3_3[:, 0:NCH - 1, LST], ALU.subtract, r=["hb3"], w=["hsc0b"])
                            cp("dve", sc[:, 1, 0:1], b3_3[:, 0, LST:LST + 1], r=["hb3"], w=["hsc1a"])
                            tt("dve", sc[:, 1, 1:NCH], b3_3[:, 1:NCH, LST], b3_3[:, 0:NCH - 1, LST], ALU.subtract, r=["hb3"], w=["hsc1b"])
                            tt("dve", sc[:, 2, :], b3_3[:, :, LST], b3_3[:, :, MID], ALU.subtract, r=["hb3"], w=["hsc2"])
                            act(esc[:], sc[:], AF.Exp, r=["hsc0a", "hsc0b", "hsc1a", "hsc1b", "hsc2"], w=["hesc"])
                            tt("dve", b2_3, b3_3, b3_3[:, :, MID:MID + 1].to_broadcast([128, NCH, CS]), ALU.subtract, r=["hb3", "hb2"], w=["hb2"])
                            dsrc, dk, ebuf, ek = buf2, "hb2", buf3, "hb3"
                        else:
                            tt("dve", buf2, buf2, buf3, ALU.subtract, r=["hb2", "hb3"], w=["hb2"])
                            tt("dve", sc[:, 0, :], b3_3[:, :, LST], b2_3[:, :, REFB], ALU.add, r=["hb3", "hb2"], w=["hsc0a"])
                            tt("dve", sc[:, 1, :], b3_3[:, :, LST], b2_3[:, :, 0], ALU.add, r=["hb3", "hb2"], w=["hsc1a"])
                            tt("dve", sc[:, 2, :], b2_3[:, :, 0], b2_3[:, :, REFB], ALU.subtract, r=["hb2"], w=["hsc2"])
                            act(esc[:], sc[:], AF.Exp, r=["hsc0a", "hsc1a", "hsc2"], w=["hesc"])
                            tt("dve", b3_3, b2_3, b2_3[:, :, REFB:REFB + 1].to_broadcast([128, NCH, CS]), ALU.subtract, r=["hb2", "hb3"], w=["hb3"])
                            dsrc, dk, ebuf, ek = buf3, "hb3", buf2, "hb2"
                        act(ebuf, dsrc, AF.Exp, r=[dk], w=[ek])
                        tt("dve", qp[:], qs[:], ebuf, ALU.mult, r=["hqs", ek], w=["hqp"])
                        act(ebuf, dsrc, AF.Exp, r=[dk, "hqp"], w=[ek], scale=-1.0)
                        ts("dve", buf1, buf1, -1.0, 1.0, ALU.mult, ALU.add, r=["hb1"], w=["hb1"])
                        tt("dve", kp[:], buf1, ebuf, ALU.mult, r=["hb1", ek], w=["hkp"])

                        order = list(range(7, -1, -1)) + list(range(NCH - 1, 7, -1)) if dirn == 1 else list(range(NCH))
                        final = dirn == 0
                        first = True
                        for idx, c in enumerate(order):
                            if HSTAGE == 1:
                                break
                            par = c % 2
                            w_ = c // 2
                            csl = slice(c * CS, (c + 1) * CS)
                            wsl_ = slice(w_ * 64, (w_ + 1) * 64)
                            prow = slice(par * 32, par * 32 + 32)
                            bi = blk_of(c * CS)
                            pi = psn()
                            sck = "hscm%d" % par
                            if par == 0:
                                mm(ps[pi][0:32, 0:32], kp[:, csl], qp[:, csl], True, True, r=["hkp", "hqp"], w=[pk(pi)])
                            else:
                                mm(ps[pi][0:64, 0:32], kp[:, wsl_], qp[:, csl], True, True, r=["hkp", "hqp"], w=[pk(pi)])
                            tt("dve", scm[prow, :], ps[pi][prow, 0:32], tri[prow, dirn * 32:(dirn + 1) * 32], ALU.mult, r=[pk(pi), "tri"], w=[sck])
                            if idx < len(order) - 1:
                                pt = psn()
                                tr(psb(pt)[0:64, 0:128], kp[:, wsl_], identb[:], r=["hkp", "identb"], w=[pk(pt)])
                                cp("act", kpT[prow, :], psb(pt)[prow, 0:128], r=[pk(pt)], w=["hkpT%d" % par])
                                pd = psn()
                                mm(ps[pd][:, 0:128], kpT[prow, :], vt[prow, w_, :], True, True, r=["hkpT%d" % par, "hvt"], w=[pk(pd)])
                                if first:
                                    ts1("dve", Sst, ps[pd][:, 0:128], esc[:, 2, c:c + 1], ALU.mult, r=[pk(pd), "hesc"], w=["hS"])
                                else:
                                    ts1("dve", Sst, Sst, esc[:, 1, c:c + 1], ALU.mult, r=["hS", "hesc"], w=["hS"])
                                    stt("dve", Sst, ps[pd][:, 0:128], esc[:, 2, c:c + 1], Sst, ALU.mult, ALU.add,
                                        r=[pk(pd), "hesc", "hS"], w=["hS"])
                                cn = order[idx + 1]
                                act(Sbf2[(idx + 1) % 2], Sst, AF.Identity, r=["hS", "hesc", "zeroc"], w=["hSb%d" % ((idx + 1) % 2)], scale=esc[:, 0, cn:cn + 1], bias=zeroc[:, 0:1])
                            po = psn()
                            mm(ps[po][0:32, 0:128], scm[prow, :], vt[prow, w_, :], True, first, r=[sck, "hvt"], w=[pk(po)])
                            if not first:
                                mm(ps[po][0:32, 0:128], qp[:, csl], Sbf2[idx % 2], False, True, r=["hqp", "hSb%d" % (idx % 2)], w=[pk(po)])
                            if not final:
                                k_ = obc[0] % 4
                                obc[0] += 1
                                cp("act", obw[k_], ps[po][0:32, 0:128], r=[pk(po)], w=[("obw", k_)])
                                if HSTAGE != 2:
                                    dma("sp", ob_d[c], obw[k_], r=[("obw", k_)], w=[("obd", c)])
                            elif HSTAGE not in (2, 3) and not (last and c < 8):
                                k_ = obc[1] % 4
                                obc[1] += 1
                                dma("sp", obr[k_], ob_d[c], r=[("obd", c)], w=[("obr", k_)])
                                tt("dve", osum, ps[po][0:32, 0:128], obr[k_], ALU.add, r=[pk(po), ("obr", k_)], w=["hosum"])
                                act(junk, osum, AF.Square, r=["hosum"], w=["hjunk", "hssq"], accum_out=ssq[:, 0:1])
                                act(rst, ssq, AF.Sqrt, r=["hssq", "epsc"], w=["hrst"], scale=1.0 / 128, bias=epsc[0:32, 0:1])
                                recip(rst, rst, r=["hrst"], w=["hrst"])
                                pg = psn()
                                for kt in range(8):
                                    mm(ps[pg][0:32, 0:128], h_sb[:, kt, csl], Wg_[:, kt, :], kt == 0, kt == 7,
                                       r=[wk(sC), ("h", kt, bi)], w=[pk(pg)])
                                act(sg, ps[pg][0:32, 0:128], AF.Silu, r=[pk(pg)], w=["hsg"])
                                tt("pool", sg, sg, gnrep[0:32, l * 128:(l + 1) * 128], ALU.mult, r=["hsg", "gnrep"], w=["hsg"])
                                stt("dve", yv, osum, rst[:, 0:1], sg, ALU.mult, ALU.mult, r=["hosum", "hrst", "hsg"], w=["hyv"])
                                pt2 = psn()
                                tr(psb(pt2)[:, 0:32], yv, identb[0:32, 0:32], r=["hyv", "identb"], w=[pk(pt2)])
                                bt0, bn = BLK[bi]
                                sidx = ysc_ctr[0] % 2
                                cp("act", ystg[sidx][:, c * CS - bt0:c * CS - bt0 + CS], psb(pt2)[:, 0:32], r=[pk(pt2)], pw=["hys%d" % sidx])
                                if c * CS + CS == bt0 + bn and HSTAGE != 4:
                                    dma("sp", ysc_blk(2, h, bi), ystg[sidx][:].bitcast(F32)[:, 0:bn // 2], r=["hys%d" % sidx], w=[("ysc", 2, h, bi)])
                                    ysc_ctr[0] += 1
                            first = False
                S.barrier()

        def attn(l, b, br, last):
            qoff, koff, voff = (AQ, AK, AV) if br == 0 else (BQ, BK, BV)
            normed = br == 1
            with ExitStack() as ph:
                def psb_(name, shape, dt):
                    return ph.enter_context(sbt(name, list(shape), dt))
                KT = psb_("aKT", [128, T], BF16)
                Vt = psb_("aVt", [128, 18, 128], BF16)
                WA = psb_("aWA", [128, 8, 128], BF16)
                WB = psb_("aWB", [128, 8, 128], BF16)
                QTb = psb_("aQT", [128, 512], BF16)
                Pb = psb_("aPb", [128, 1280], BF16)
                PT = psb_("aPT", [128, 1280], BF16)
                af = Carver(psb_("amf", [128, 24], F32), 24)
                mx = af([128, 4])
                negm = af([128, 2])
                rs = af([128, 8])
                es_ = af([128, 2])
                rinv2 = af([128, 2])
                on = psb_("aon", [128, 128], BF16)
                ystg = [psb_("ays%d" % i_, [128, 512], BF16) for i_ in range(2)]
                ysc_ctr = [0]

                def build_perm(si, wnat):
                    nat = wnat.rearrange("p k (hh r two) -> p (k hh) two r", hh=2, r=32, two=2)
                    wa = WA[:].rearrange("p k (hh half r) -> p (k hh) half r", hh=2, half=2, r=32)
                    wb = WB[:].rearrange("p k (hh half r) -> p (k hh) half r", hh=2, half=2, r=32)
                    for half in range(2):
                        cp("dve", wa[:, :, half, :], nat[:, :, half, :], r=[wk(si)], pw=["aWA"])
                        cp("dve", wb[:, :, half, :], nat[:, :, 1 - half, :], r=[wk(si)], pw=["aWB"])

                def rope_block(pa, pb, bi, out_ap, okey, gidx):
                    t0, n = BLK[bi]
                    dma("sp", g[0][:, :n], DAP(dr["k_ropec"], 128 * t0, [[n, 128], [1, n]]), w=[gk[0]])
                    dma("sp", g[1][:, :n], DAP(dr["k_ropes"], 128 * t0, [[n, 128], [1, n]]), w=[gk[1]])
                    if not normed:
                        tt("dve", g[2][:, :n], ps[pa][:, :n], g[0][:, :n], ALU.mult, r=[pk(pa), gk[0]], w=[gk[2]])
                        tt("dve", g[3][:, :n], ps[pb][:, :n], g[1][:, :n], ALU.mult, r=[pk(pb), gk[1]], w=[gk[3]])
                        tt("dve", out_ap, g[2][:, :n], g[3][:, :n], ALU.add, r=[gk[2], gk[3]], pw=[okey])
                    else:
                        act(g[4][:, :n], ps[pa][:, :n], AF.Square, r=[pk(pa)], w=[gk[4]])
                        pq = psn()
                        mm(ps[pq][:, :n], bones[:], g[4][:, :n], True, True, r=["bones", gk[4]], w=[pk(pq)])
                        act(g[5][:, :n], ps[pq][:, :n], AF.Sqrt, r=[pk(pq), "epsc"], w=[gk[5]], scale=1.0 / 64, bias=epsc[:, 0:1])
                        recip(g[5][:, :n], g[5][:, :n], r=[gk[5]], w=[gk[5]])
                        stt("dve", g[2][:, :n], ps[pa][:, :n], gq[:, gidx, l:l + 1], g[0][:, :n], ALU.mult, ALU.mult,
                            r=[pk(pa), "gq", gk[0]], w=[gk[2]])
                        stt("dve", g[3][:, :n], ps[pb][:, :n], gq[:, gidx + 1, l:l + 1], g[1][:, :n], ALU.mult, ALU.mult,
                            r=[pk(pb), "gq", gk[1]], w=[gk[3]])
                        tt("dve", g[2][:, :n], g[2][:, :n], g[3][:, :n], ALU.add, r=[gk[2], gk[3]], w=[gk[2]])
                        tt("dve", out_ap, g[2][:, :n], g[5][:, :n], ALU.mult, r=[gk[2], gk[5]], pw=[okey])

                si = wslot()
                wn = wload(si, 0, w_in[l][:, koff:koff + 128], 8, 128)
                build_perm(si, wn)
                for bi in range(5):
                    t0, n = BLK[bi]
                    pa = proj_fm(WA[:], "aWA", bi)
                    pb = proj_fm(WB[:], "aWB", bi)
                    rope_block(pa, pb, bi, KT[:, t0:t0 + n], ("aKT", bi), 2)
                si = wslot()
                wv_ = wload(si, 0, w_in[l][:, voff:voff + 128], 8, 128)
                for tg in range(5):
                    tiles = list(range(tg * 4, min(tg * 4 + 4, 18)))
                    pi = psn()
                    for q, tt_ in enumerate(tiles):
                        bi = blk_of(tt_ * 128)
                        for kt in range(8):
                            mm(ps[pi][:, q * 128:(q + 1) * 128], h_sb[:, kt, tt_ * 128:(tt_ + 1) * 128], wv_[:, kt, :], kt == 0, kt == 7,
                               r=[wk(si), ("h", kt, bi)], w=[pk(pi)])
                    nt_ = len(tiles)
                    cp("act", Vt[:, tiles[0]:tiles[0] + nt_, :], ps[pi][:, 0:nt_ * 128].rearrange("p (q e) -> p q e", e=128),
                       r=[pk(pi)], pw=["aVt"])

                def attend(i, qt0, qc, ranges, band_lo, sink, use_max):
                    po = psacc()
                    if use_max:
                        st_ = {}

                        def sA1(hh):
                            head = i + 4 * hh
                            rows = slice(hh * 64, hh * 64 + 64)
                            lhsT = QTb[rows, qc:qc + 128]
                            pbo = hh * 640
                            pbk = ("aPbA", hh)
                            banks = []
                            for (k0, nk) in ranges:
                                pi = psn()
                                mm(ps[pi][:, 0:nk], lhsT, KT[rows, k0:k0 + nk], True, True,
                                   r=["aQT"] + [("aKT", kb) for kb in sorted({blk_of(k0), blk_of(k0 + nk - 1)})], w=[pk(pi)])
                                banks.append(pi)
                            for j, pi in enumerate(banks):
                                nk = ranges[j][1]
                                A("dve", lambda e, j=j, pi=pi, nk=nk, hh=hh: e.reduce_max(out=mx[:, 2 * hh + j:2 * hh + j + 1], in_=ps[pi][:, 0:nk], axis=AX.X),
                                  [pk(pi)], [("amx", hh, j)])
                            m0 = mx[:, 2 * hh:2 * hh + 1]
                            if len(banks) == 2:
                                tt("dve", m0, m0, mx[:, 2 * hh + 1:2 * hh + 2], ALU.max, r=[("amx", hh, 0), ("amx", hh, 1)], w=[("amx", hh, 0)])
                            ng = negm[:, hh:hh + 1]
                            if sink:
                                ts("dve", ng, m0, m0125[:, 0:1], negsink[:, l * 8 + head:l * 8 + head + 1], ALU.mult, ALU.min,
                                   r=[("amx", hh, 0), "negsink", "m0125"], w=[("anegm", hh)])
                            else:
                                ts1("dve", ng, m0, -0.125, ALU.mult, r=[("amx", hh, 0)], w=[("anegm", hh)])
                            col = 0
                            for j, pi in enumerate(banks):
                                nk = ranges[j][1]
                                act(Pb[:, pbo + col:pbo + col + nk], ps[pi][:, 0:nk], AF.Exp, r=[pk(pi), ("anegm", hh)], pw=[pbk], scale=0.125, bias=ng)
                                col += nk
                            if band_lo is not None:
                                nbk = ranges[1][1]
                                tt("dve", Pb[:, pbo + 256:pbo + 256 + nbk], Pb[:, pbo + 256:pbo + 256 + nbk], band01[:, band_lo:band_lo + nbk], ALU.mult,
                                   r=[pbk, "band01"], w=[pbk])
                            rsum = rs[:, 6 + hh:7 + hh]
                            A("dve", lambda e, col=col, pbo=pbo, rsum=rsum: e.reduce_sum(out=rsum, in_=Pb[:, pbo:pbo + col], axis=AX.X), [pbk], [("arsA", hh)])
                            if sink:
                                act(es_[:, hh:hh + 1], sinkrep[:, l * 8 + head:l * 8 + head + 1], AF.Exp, r=["sinkrep", ("anegm", hh)], w=[("aes", hh)], bias=ng, scale=1.0)
                                tt("dve", rsum, rsum, es_[:, hh:hh + 1], ALU.add, r=[("arsA", hh), ("aes", hh)], w=[("arsA", hh)])
                            recip(rinv2[:, hh:hh + 1], rsum, r=[("arsA", hh)], w=[("arinv", hh)])
                            st_[hh] = col

                        def sA2(hh):
                            col = st_[hh]
                            pbo = hh * 640
                            pbk = ("aPbA", hh)
                            ptk = ("aPTA", hh)
                            ntile = col // 128
                            pt = psn()
                            for kt in range(ntile):
                                tr(psb(pt)[:, kt * 128:(kt + 1) * 128], Pb[:, pbo + kt * 128:pbo + (kt + 1) * 128], identb[:], r=[pbk, "identb"], w=[pk(pt)])
                            cp("act" if hh == 0 else "dve", PT[:, pbo:pbo + ntile * 128], psb(pt)[:, 0:ntile * 128], r=[pk(pt)], w=[ptk])
                            ktile = []
                            for (k0, nk) in ranges:
                                ktile += [k0 // 128 + q for q in range(nk // 128)]
                            for kt in range(ntile):
                                mm(ps[po][:, hh * 64:(hh + 1) * 64], PT[:, pbo + kt * 128:pbo + (kt + 1) * 128], Vt[:, ktile[kt], hh * 64:(hh + 1) * 64],
                                   kt == 0, kt == ntile - 1, r=[ptk, "aVt"], w=[pk(po)])

                        sA1(0)
                        sA1(1)
                        sA2(0)
                        sA2(1)
                    for hh in (range(2) if not use_max else ()):
                        head = i + 4 * hh
                        rows = slice(hh * 64, hh * 64 + 64)
                        lhsT = QTb[rows, qc:qc + 128]
                        if True:
                            nr = len(ranges)

                            def stage1(j):
                                k0, nk = ranges[j]
                                pi = psn()
                                mm(ps[pi][:, 0:nk], lhsT, KT[rows, k0:k0 + nk], True, True, r=["aQT", ("aKT", blk_of(k0))], w=[pk(pi)])
                                hb = (j % 2) * 512
                                act(Pb[:, hb:hb + nk], ps[pi][:, 0:nk], AF.Exp, r=[pk(pi)], w=[("aPb", j % 2), ("ars", j)], scale=0.125,
                                    accum_out=rs[:, j:j + 1])

                            def stage2(j):
                                k0, nk = ranges[j]
                                hb = (j % 2) * 512
                                pbk = ("aPb", j % 2)
                                ntile = nk // 128
                                pt = psn()
                                for kt in range(ntile):
                                    tr(psb(pt)[:, kt * 128:(kt + 1) * 128], Pb[:, hb + kt * 128:hb + (kt + 1) * 128], identb[:], r=[pbk, "identb"], w=[pk(pt)])
                                ptk = ("aPT", j % 2)
                                cp("act" if j % 2 == 0 else "dve", PT[:, hb:hb + ntile * 128], psb(pt)[:, 0:ntile * 128], r=[pk(pt)], w=[ptk])
                                for kt in range(ntile):
                                    mm(ps[po][:, hh * 64:(hh + 1) * 64], PT[:, hb + kt * 128:hb + (kt + 1) * 128],
                                       Vt[:, k0 // 128 + kt, hh * 64:(hh + 1) * 64], j == 0 and kt == 0, j == nr - 1 and kt == ntile - 1,
                                       r=[ptk, "aVt"], w=[pk(po)])

                            stage1(0)
                            for j in range(nr):
                                if j + 1 < nr:
                                    stage1(j + 1)
                                stage2(j)
                            if nr > 1:
                                A("dve", lambda e, nr=nr: e.reduce_sum(out=rs[:, 0:1], in_=rs[:, 0:nr], axis=AX.X),
                                  [("ars", j) for j in range(nr)], [("ars", 0)])
                            recip(rinv2[:, hh:hh + 1], rs[:, 0:1], r=[("ars", 0)], w=[("arinv", hh)])
                    tt("dve", on[:].rearrange("p (a d) -> p a d", d=64), ps[po][:, 0:128].rearrange("p (a d) -> p a d", d=64),
                       rinv2[:].unsqueeze(2).to_broadcast([128, 2, 64]), ALU.mult, r=[pk(po), ("arinv", 0), ("arinv", 1)], w=["aon"])
                    pt = psn()
                    tr(psb(pt)[:, 0:128], on[:], identb[:], r=["aon", "identb"], w=[pk(pt)])
                    bi_ = blk_of(qt0)
                    bt0, bn = BLK[bi_]
                    sidx = ysc_ctr[0] % 2
                    cp("act", ystg[sidx][:, qt0 - bt0:qt0 - bt0 + 128], psb(pt)[:, 0:128], r=[pk(pt)], pw=["ays%d" % sidx])
                    if qt0 + 128 == bt0 + bn:
                        dma("sp", ysc_blk(br, i, bi_), ystg[sidx][:].bitcast(F32)[:, 0:bn // 2], r=["ays%d" % sidx], w=[("ysc", br, i, bi_)])
                        ysc_ctr[0] += 1

                for i in range(4):
                    si = wslot()
                    wn = wsl[si][:, 0:1024].rearrange("p (k n) -> p k n", n=128)
                    for hh in range(2):
                        hd = i + 4 * hh
                        dst = wn[:, :, hh * 64:(hh + 1) * 64]
                        src = w_in[l][:, qoff + hd * 64:qoff + (hd + 1) * 64].rearrange("(k p) n -> p k n", p=128)
                        dma("pool", dst, src, pw=[wk(si)])
                    build_perm(si, wn)
                    for bi in range(5):
                        if bi == 0 and last:
                            continue
                        t0, n = BLK[bi]
                        pa = proj_fm(WA[:], "aWA", bi)
                        pb = proj_fm(WB[:], "aWB", bi)
                        rope_block(pa, pb, bi, QTb[:, :n], "aQT", 0)
                        for qi in range(n // 128):
                            qt0 = t0 + qi * 128
                            if bi == 0:
                                attend(i, qt0, qi * 128, [(0, 256)], None, br == 0, br == 0)
                            elif br == 0:
                                nq = (qt0 - 256) // 128
                                lo = max(nq - 1, 0)
                                hi = min(nq + 2, 16)
                                attend(i, qt0, qi * 128, [(0, 256), (256 + lo * 128, (hi - lo) * 128)], 128 if nq == 0 else 0, True, True)
                            else:
                                attend(i, qt0, qi * 128, [(0, 512), (512, 512), (1024, 512), (1536, 512), (2048, 256)], None, False, False)
                S.barrier()

        def merge(l, b, last):
            with ExitStack() as ph:
                uT = ph.enter_context(sbt("uT", [128, 8, 1280], BF16))
                yS = [ph.enter_context(sbt("yS%d" % n_, [128, 4, 1280], BF16)) for n_ in range(3)]
                sbs = [[0, 1, 2], [3, 4]]
                if last:
                    sbs = [[1, 2], [3, 4]]
                for sbi, blocks in enumerate(sbs):
                    base = 0 if sbi == 0 else 1280
                    for n_ in range(3):
                        for bi in blocks:
                            t0, n = BLK[bi]
                            ysf = yS[n_][:].rearrange("p k t -> p (k t)").bitcast(F32).rearrange("p (k t) -> p k t", k=4)
                            for k_ in range(4):
                                dma("sp", ysf[:, k_, (t0 - base) // 2:(t0 - base + n) // 2], ysc_blk(n_, k_, bi),
                                    r=[("ysc", n_, k_, bi)], pw=[("yS", n_)])
                    for dt in range(8):
                        sG = wslot()
                        Wg01 = [wload(sG, n_ * 1024, w_in[l][:, GT + n_ * 1024 + dt * 128:GT + n_ * 1024 + (dt + 1) * 128], 8, 128) for n_ in range(2)]
                        sG2 = wslot()
                        Wg2 = wload(sG2, 0, w_in[l][:, GT + 2048 + dt * 128:GT + 2048 + (dt + 1) * 128], 8, 128)
                        sW = wslot()
                        wbv = wsl[sW][:, 0:1536].rearrange("p (n k c) -> p n k c", n=3, k=4)
                        wb_ = dr["w_branch"]
                        for n_ in range(2):
                            for hh in range(2):
                                src = wb_[l, n_, hh * 256:(hh + 1) * 256, dt * 128:(dt + 1) * 128].rearrange("(j r) c -> r j c", r=64)
                                dma("pool", wbv[hh * 64:(hh + 1) * 64, n_, :, :], src, pw=[wk(sW)])
                        dma("pool", wbv[:, 2, :, :], wb_[l, 2, :, dt * 128:(dt + 1) * 128].rearrange("(k p) c -> p k c", p=128), pw=[wk(sW)])
                        Wgs = [(Wg01[0], wk(sG)), (Wg01[1], wk(sG)), (Wg2, wk(sG2))]
                        for bi in blocks:
                            t0, n = BLK[bi]
                            for n_ in range(3):
                                pg = proj_fm(Wgs[n_][0], Wgs[n_][1], bi)
                                sgt = g[n_ % 2]
                                act(sgt[:, :n], ps[pg][:, :n], AF.Sigmoid, r=[pk(pg)], w=[gk[n_ % 2]])
                                pu = psn()
                                for kt in range(4):
                                    mm(ps[pu][:, :n], wbv[:, n_, kt, :], yS[n_][:, kt, t0 - base:t0 - base + n], kt == 0, kt == 3,
                                       r=[wk(sW), ("yS", n_)], w=[pk(pu)])
                                if n_ == 0:
                                    tt("dve", g[2][:, :n], sgt[:, :n], ps[pu][:, :n], ALU.mult, r=[gk[0], pk(pu)], w=[gk[2]])
                                else:
                                    tt("dve", g[3][:, :n], sgt[:, :n], ps[pu][:, :n], ALU.mult, r=[gk[n_ % 2], pk(pu)], w=[gk[3]])
                                    if n_ == 1:
                                        tt("dve", g[2][:, :n], g[2][:, :n], g[3][:, :n], ALU.add, r=[gk[2], gk[3]], w=[gk[2]])
                                    else:
                                        tt("dve", uT[:, dt, t0 - base:t0 - base + n], g[2][:, :n], g[3][:, :n], ALU.add,
                                           r=[gk[2], gk[3]], pw=[("uT", dt)])
                    for half in range(4):
                        sO = wslot()
                        Wo = wload(sO, 0, dr["w_out"][l][:, half * 256:(half + 1) * 256], 8, 256)
                        for q in range(2):
                            dt2 = half * 2 + q
                            for bi in blocks:
                                t0, n = BLK[bi]
                                j = nb if bi == 0 else b
                                pi = psn()
                                for kt in range(8):
                                    mm(ps[pi][:, :n], Wo[:, kt, q * 128:(q + 1) * 128], uT[:, kt, t0 - base:t0 - base + n], kt == 0, kt == 7,
                                       r=[wk(sO), ("uT", kt)], w=[pk(pi)])
                                stt("dve", x_sb[:, dt2, t0:t0 + n], ps[pi][:, :n], mod(l, 2, dt2, j), x_sb[:, dt2, t0:t0 + n], ALU.mult, ALU.add,
                                    r=[pk(pi), "m_all", ("x", dt2, bi)], w=[("x", dt2, bi)])
                S.barrier()

        def ffn(l, b, last):
            blocks = [1, 2, 3, 4] if last else [0, 1, 2, 3, 4]
            wsrc = dr["w_dw"][l].rearrange("k (ct p) -> (k ct) p", p=128)
            wdf = wdw.rearrange("p k c -> p (k c)")
            for (r0_, rn_) in ((0, 64), (64, 64), (128, 4)):
                loadT(wdf[:, r0_:r0_ + rn_], wsrc[r0_:r0_ + rn_, :], rn_, "wdw")
            bsrc_ = dr["b_dw"][l].rearrange("(ct p) -> ct p", p=128)
            for (r0_, rn_) in ((0, 32), (32, 8), (40, 4)):
                loadT(bdw[:, r0_:r0_ + rn_], bsrc_[r0_:r0_ + rn_, :], rn_, "bdw")
            norm_to_h(l, b, 1, blocks)
            seqs = [(256, T)] if last else [(0, 256), (256, T)]
            with ExitStack() as ph:
                ua = ph.enter_context(sbt("fua", [128, T], F32))
                ug = ph.enter_context(sbt("fug", [128, T], F32))
                ca = ph.enter_context(sbt("fca", [128, T], F32))
                cg_ = ph.enter_context(sbt("fcg", [128, T], F32))
                actb = ph.enter_context(sbt("fact", [128, 2, T], BF16))
                lo_t = seqs[0][0]
                for grp in range(11):
                    sA = wslot()
                    Wa = wload(sA, 0, dr["w_up"][l][:, grp * 256:(grp + 1) * 256], 8, 256)
                    sG = wslot()
                    Wg = wload(sG, 0, dr["w_up"][l][:, 2816 + grp * 256:2816 + (grp + 1) * 256], 8, 256)
                    sD = wslot()
                    Wd = wload(sD, 0, dr["w_down"][l][grp * 256:(grp + 1) * 256, :], 2, 1024)
                    for ci in range(2):
                        cta = grp * 2 + ci
                        ctg = 22 + cta
                        for (Wx, wkx, ct, uu, cc, un, cn) in ((Wa, wk(sA), cta, ua, ca, "fua", "fca"), (Wg, wk(sG), ctg, ug, cg_, "fug", "fcg")):
                            for bi in blocks:
                                t0, n = BLK[bi]
                                pi = proj_fm(Wx[:, :, ci * 128:(ci + 1) * 128], wkx, bi)
                                act(cc[:, t0:t0 + n], ps[pi][:, :n], AF.Identity, r=[pk(pi), "wdw", "bdw"], pw=[cn],
                                    scale=wdw[:, 1, ct:ct + 1], bias=bdw[:, ct:ct + 1])
                                cp("dve", uu[:, t0:t0 + n], ps[pi][:, :n], r=[pk(pi)], pw=[un])
                            for (s0, s1_) in seqs:
                                stt("dve", cc[:, s0 + 1:s1_], uu[:, s0:s1_ - 1], wdw[:, 0, ct:ct + 1], cc[:, s0 + 1:s1_], ALU.mult, ALU.add,
                                    r=[un, cn, "wdw"], w=[cn])
                                stt("dve", cc[:, s0:s1_ - 1], uu[:, s0 + 1:s1_], wdw[:, 2, ct:ct + 1], cc[:, s0:s1_ - 1], ALU.mult, ALU.add,
                                    r=[un, cn, "wdw"], w=[cn])
                        act(cg_[:, lo_t:T], cg_[:, lo_t:T], AF.Silu, r=["fcg"], w=["fcg"])
                        tt("dve", actb[:, ci, lo_t:T], ca[:, lo_t:T], cg_[:, lo_t:T], ALU.mult, r=["fca", "fcg"], w=[("fact", ci)])
                    for dt in range(8):
                        for bi in blocks:
                            t0, n = BLK[bi]
                            j = nb if bi == 0 else b
                            pi = psn()
                            for ci in range(2):
                                mm(ps[pi][:, :n], Wd[:, ci, dt * 128:(dt + 1) * 128], actb[:, ci, t0:t0 + n], ci == 0, ci == 1,
                                   r=[wk(sD), ("fact", ci)], w=[pk(pi)])
                            stt("dve", x_sb[:, dt, t0:t0 + n], ps[pi][:, :n], mod(l, 5, dt, j), x_sb[:, dt, t0:t0 + n], ALU.mult, ALU.add,
                                r=[pk(pi), "m_all", ("x", dt, bi)], w=[("x", dt, bi)])
                S.barrier()

        def final_store(b):
            with ExitStack() as ph:
                stage = [ph.enter_context(sbt("ostg%d" % i, [128, 1024], F32)) for i in range(2)]
                xn = ph.enter_context(sbt("xnf", [128, 8, 512], F32))
                cnt = 0
                for bi in range(1, 5):
                    t0, n = BLK[bi]
                    pst = psn()
                    for dt in range(8):
                        sq = g[dt % 2]
                        act(sq[:, :n], x_sb[:, dt, t0:t0 + n], AF.Square, r=[("x", dt, bi)], w=[gk[dt % 2]])
                        mm(ps[pst][:, :n], onesf[:], sq[:, :n], dt == 0, dt == 7, r=["onesf", gk[dt % 2]], w=[pk(pst)])
                    act(g[2][:, :n], ps[pst][:, :n], AF.Sqrt, r=[pk(pst), "epsc"], w=[gk[2]], scale=1.0 / D, bias=epsc[:, 0:1])
                    recip(g[2][:, :n], g[2][:, :n], r=[gk[2]], w=[gk[2]])
                    for dt in range(8):
                        stt("dve", xn[:, dt, :], x_sb[:, dt, t0:t0 + n], nfin[:, dt:dt + 1], g[2][:, :n], ALU.mult, ALU.mult,
                            r=[("x", dt, bi), "nfin", gk[2]], w=[("xnf", dt)])
                    for qi in range(4):
                        st = stage[cnt % 2]
                        sk = "ostg%d" % (cnt % 2)
                        cnt += 1
                        for half in range(2):
                            pi = psn()
                            for q in range(4):
                                dt = half * 4 + q
                                tr(ps[pi][:, q * 128:(q + 1) * 128], xn[:, dt, qi * 128:(qi + 1) * 128], identf[:], r=[("xnf", dt), "identf"], w=[pk(pi)])
                            cp("dve" if half == 0 else "act", st[:, half * 512:(half + 1) * 512], ps[pi][:], r=[pk(pi)], pw=[sk])
                        tok = t0 - 256 + qi * 128
                        dma("sp", y_d[b, tok:tok + 128, :], st[:], r=[sk])
                S.barrier()

        for b in range(nb):
            load_x(b)
            for l in range(depth):
                last = l == LYR - 1
                S.epoch += 1
                layer_scalars(l, b)
                norm_to_h(l, b, 0, [0, 1, 2, 3, 4])
                if "h" in dump_names and b == 0 and l == 0:
                    dumpx("h", h_sb[:], [128, 8, T], [("h", dt, bi) for dt in range(8) for bi in range(5)])
                if "hgrn" in phases:
                    hgrn(l, b, last)
                if "attnA" in phases:
                    attn(l, b, 0, last)
                if "attnB" in phases:
                    attn(l, b, 1, last)
                if b == 0 and l == 0 and "ysc" in dump_names:
                    dumpx("ysc", ysc, [3 * 4 * 128 * (T // 2)], [("ysc", n_, k_, bi) for n_ in range(3) for k_ in range(4) for bi in range(5)])
                if "merge" in phases:
                    merge(l, b, last)
                if b == 0 and l == 0:
                    dumpx("xmix", x_sb[:], [128, 8, T], [("x", dt, bi) for dt in range(8) for bi in range(5)])
                if "ffn" in phases:
                    ffn(l, b, last)
                if b == 0 and l == 0:
                    dumpx("xffn", x_sb[:], [128, 8, T], [("x", dt, bi) for dt in range(8) for bi in range(5)])
            final_store(b)
        S.emit(nc)
    return nc, dump_d


_CONSTS = None


def run(inputs, nb=2, depth=4, dump=(), ncores=8, trace=False, phases=("hgrn", "attnA", "attnB", "merge", "ffn")):
    global _CONSTS
    if _CONSTS is None:
        _CONSTS = make_consts()
    nc, dump_d = build(nb=nb, depth=depth, dump=dump, phases=phases)
    shared = {}
    for name, _ in WNAMES:
        shared[name] = np.ascontiguousarray(np.asarray(inputs[name], dtype=np.float32))
    shared["c_ctx"] = np.ascontiguousarray(np.asarray(inputs["c_ctx"], dtype=np.float32))
    shared.update(_CONSTS)
    x = np.asarray(inputs["x"], dtype=np.float32)
    c = np.asarray(inputs["c"], dtype=np.float32)
    ctx = np.asarray(inputs["ctx"], dtype=np.float32)
    in_maps = []
    for core in range(ncores):
        m = dict(shared)
        m["x"] = np.ascontiguousarray(x[core * nb:(core + 1) * nb])
        m["c"] = np.ascontiguousarray(c[core * nb:(core + 1) * nb])
        m["ctx"] = np.ascontiguousarray(ctx[core * nb:(core + 1) * nb])
        in_maps.append(m)
    res = run_bass_kernel_spmd(nc, in_maps, core_ids=list(range(ncores)), trace=trace)
    return res


def kernel(**inputs):
    res = run(inputs, nb=2, depth=4, ncores=8)
    out = np.concatenate([np.asarray(r["y"]) for r in res.results], axis=0)
    return out.astype(np.float32)
```

```python
import os
import numpy as np
import concourse.bass as bass
import concourse.mybir as mybir
from concourse.bass_utils import run_bass_kernel_spmd
from contextlib import ExitStack

F32 = mybir.dt.float32
BF16 = mybir.dt.bfloat16
AF = mybir.ActivationFunctionType
ALU = mybir.AluOpType
AX = mybir.AxisListType

D = 1024
SL = 2048
NCX = 256
T = 2304
LYR = 4
NCH = 72
BLK = [(0, 256), (256, 512), (768, 512), (1280, 512), (1792, 512)]
EPS = 1e-6
HSTAGE = int(os.environ.get("HSTAGE", "0"))
AQ, AK, AV, BQ, BK, BV, CQ, CF, CB, CI, CG, GT = 0, 512, 640, 768, 1280, 1408, 1536, 2048, 2560, 3072, 3584, 4096


def blk_of(t):
    return 0 if t < 256 else 1 + (t - 256) // 512


ENGS = ("pe", "dve", "act", "pool", "sp")
NSLOT = {"sp": 8, "act": 4, "pool": 8}


class Op:
    __slots__ = ("eng", "fn", "cs", "deps", "needs_inc", "value", "is_dma", "idx")


class Sched:
    def __init__(self):
        self.ops = []
        self.lastw = {}
        self.readers = {}
        self.dma_rr = {"sp": 0, "act": 0, "pool": 0}
        self.slot_last = {}
        self.pending_bar = {}
        self.last_of_cs = {}
        self.genbase = {}
        self.epoch = 0

    def barrier(self):
        deps = list(self.last_of_cs.values())
        for e in ENGS:
            self.pending_bar[e] = set(deps) | self.pending_bar.get(e, set())

    def add(self, eng, fn, reads=(), writes=(), dma=False, pwrites=()):
        op = Op()
        op.eng = eng
        op.fn = fn
        op.is_dma = dma
        op.idx = len(self.ops)
        op.needs_inc = dma
        op.value = None
        if dma:
            s = self.dma_rr[eng]
            self.dma_rr[eng] = (s + 1) % NSLOT[eng]
            op.cs = ("dma", eng, s)
        else:
            op.cs = (eng, self.epoch)
        deps = set()
        raww = set()
        for k in reads:
            for w in self.lastw.get(k, ()):
                deps.add(w)
                raww.add(w)
            if isinstance(k, str) and k.startswith("ps") and len(k) == 3:
                for cs, i in self.readers.get(k, {}).items():
                    if cs != op.cs:
                        deps.add(i)
        for k in writes:
            for w in self.lastw.get(k, ()):
                deps.add(w)
                raww.add(w)
            for cs, i in self.readers.get(k, {}).items():
                deps.add(i)
        for k in pwrites:
            rd = self.readers.get(k)
            if rd:
                self.genbase[k] = set(rd.values()) | set(self.lastw.get(k, ()))
            for w in self.genbase.get(k, ()):
                deps.add(w)
                raww.add(w)
        if dma:
            pl = self.slot_last.get(op.cs)
            if pl is not None:
                deps.add(pl)
            self.slot_last[op.cs] = op.idx
        bar = self.pending_bar.pop(eng, None)
        if bar:
            deps |= bar
        final = []
        for d in deps:
            dop = self.ops[d]
            if dop.cs == op.cs and not dma:
                if op.eng == "pe":
                    continue
            final.append(d)
        op.deps = final
        for k in reads:
            self.readers.setdefault(k, {})[op.cs] = op.idx
        for k in writes:
            self.lastw[k] = [op.idx]
            self.readers[k] = {}
            self.genbase[k] = {op.idx}
        for k in pwrites:
            if self.readers.get(k):
                self.lastw[k] = [op.idx]
                self.readers[k] = {}
            else:
                self.lastw.setdefault(k, []).append(op.idx)
        self.last_of_cs[op.cs] = op.idx
        self.ops.append(op)
        return op

    def emit(self, nc, final_wait_eng="sp"):
        ops = self.ops
        for op in ops:
            for d in op.deps:
                ops[d].needs_inc = True
        cnt = {}
        for op in ops:
            if op.needs_inc:
                inc = 16 if op.is_dma else 1
                cnt[op.cs] = cnt.get(op.cs, 0) + inc
                op.value = cnt[op.cs]
        cs_list = sorted(cnt.keys(), key=str)
        with ExitStack() as es:
            sems = {}
            for cs in cs_list:
                nm = "s_" + ("_".join(str(c) for c in cs) if isinstance(cs, tuple) else cs)
                nm = nm.replace(" ", "")
                sems[cs] = es.enter_context(nc.semaphore(nm))
            block = es.enter_context(nc.Block())
            streams = {e: [] for e in ENGS}
            for op in ops:
                streams[op.eng].append(op)

            def run(eng_name, eng):
                seen = {}
                for op in streams[eng_name]:
                    need = {}
                    for d in op.deps:
                        dop = ops[d]
                        v = dop.value
                        if seen.get(dop.cs, 0) < v and need.get(dop.cs, 0) < v:
                            need[dop.cs] = v
                    for cs, v in need.items():
                        eng.wait_ge(sems[cs], v)
                        seen[cs] = v
                    ins = op.fn(eng)
                    if op.needs_inc:
                        ins.then_inc(sems[op.cs], 16 if op.is_dma else 1)
                if eng_name == final_wait_eng:
                    for cs, v in cnt.items():
                        if seen.get(cs, 0) < v:
                            eng.wait_ge(sems[cs], v)

            @block.tensor
            def _(e):
                run("pe", e)

            @block.vector
            def _(e):
                run("dve", e)

            @block.scalar
            def _(e):
                run("act", e)

            @block.gpsimd
            def _(e):
                run("pool", e)

            @block.sync
            def _(e):
                run("sp", e)
        return cnt


def make_consts():
    c = {}
    c["k_ident"] = np.eye(128, dtype=np.float32)
    bo = np.zeros((128, 128), np.float32)
    bo[:64, :64] = 1.0
    bo[64:, 64:] = 1.0
    c["k_bones"] = bo
    ii = np.arange(128)[:, None]
    jj = np.arange(128)[None, :]
    band = np.ones((128, 384), np.float32)
    band[:, 0:128] = (jj >= ii)
    band[:, 256:384] = (jj <= ii)
    c["k_band"] = band
    j6 = (np.arange(64) % 32)[:, None]
    t6 = np.arange(32)[None, :]
    tri = np.zeros((64, 64), np.float32)
    tri[:, 0:32] = (j6 <= t6)
    tri[:, 32:64] = (j6 >= t6)
    c["k_tri"] = tri
    pos = np.arange(SL)
    row = (pos // 64).astype(np.float32)
    col = (pos % 64).astype(np.float32)
    inv = (np.float32(10000.0) ** (-np.arange(0, 32, 2, dtype=np.float32) / np.float32(32))).astype(np.float32)
    ang = np.concatenate([row[:, None] * inv, col[:, None] * inv], axis=-1).astype(np.float32)
    cs_ = np.cos(ang).astype(np.float32)
    sn_ = np.sin(ang).astype(np.float32)
    C = np.ones((128, T), np.float32)
    Sg = np.zeros((128, T), np.float32)
    for p in range(128):
        r = p % 64
        pair = r % 32
        C[p, NCX:] = cs_[:, pair]
        Sg[p, NCX:] = sn_[:, pair] * (-1.0 if r < 32 else 1.0)
    def blocked(a):
        return np.concatenate([np.ascontiguousarray(a[:, t0:t0 + n]).reshape(-1) for (t0, n) in BLK]).astype(np.float32)
    c["k_ropec"] = blocked(C)
    c["k_ropes"] = blocked(Sg)
    return c


WNAMES = [("w_mod", (4, 1024, 6144)), ("b_mod", (4, 6144)), ("norm_mix", (4, 1024)), ("norm_ffn", (4, 1024)),
          ("w_in", (4, 1024, 7168)), ("sink_a", (4, 8)), ("qn_b", (4, 64)), ("kn_b", (4, 64)),
          ("lb_fwd", (4, 512)), ("lb_bwd", (4, 512)), ("gn_c", (4, 128)), ("w_branch", (4, 3, 512, 1024)),
          ("w_out", (4, 1024, 1024)), ("w_up", (4, 1024, 5632)), ("w_dw", (4, 3, 5632)), ("b_dw", (4, 5632)),
          ("w_down", (4, 2816, 1024)), ("norm_final", (1024,))]
CNAMES = [("k_ident", (128, 128)), ("k_bones", (128, 128)), ("k_band", (128, 384)), ("k_tri", (64, 64)),
          ("k_ropec", (128 * T,)), ("k_ropes", (128 * T,))]


def build(nb=2, depth=4, dump=(), phases=("hgrn", "attnA", "attnB", "merge", "ffn")):
    nc = bass.Bass("TRN2", target_bir_lowering=False)
    S = Sched()
    dr = {}
    for name, shape in [("x", (nb, SL, D)), ("c", (nb, D)), ("ctx", (nb, NCX, D)), ("c_ctx", (D,))] + WNAMES + CNAMES:
        dr[name] = nc.dram_tensor(name, list(shape), F32, kind="ExternalInput").ap()
    y_d = nc.dram_tensor("y", [nb, SL, D], F32, kind="ExternalOutput").ap()
    ysc = nc.dram_tensor("ysc", [3 * 4 * 128 * (T // 2)], F32).ap()

    def ysc_blk(n_, k_, bi_):
        t0_, bn_ = BLK[bi_]
        off = (n_ * 4 + k_) * 128 * (T // 2) + 128 * (t0_ // 2)
        return bass.AP(ysc.tensor, off, [[bn_ // 2, 128], [1, bn_ // 2]])

    def ysc_blk4(n_, bi_):
        t0_, bn_ = BLK[bi_]
        off = (n_ * 4) * 128 * (T // 2) + 128 * (t0_ // 2)
        return bass.AP(ysc.tensor, off, [[bn_ // 2, 128], [128 * (T // 2), 4], [1, bn_ // 2]])
    ob_d = nc.dram_tensor("ob_d", [NCH, 32, 128], F32).ap()
    dump_d = {}
    dump_names = set(dump)
    w_in = dr["w_in"]

    def DAP(base, offset, pat):
        return bass.AP(base.tensor, offset, pat)

    es = ExitStack()
    with es:
        uid = [0]

        def sbt(name, shape, dt):
            uid[0] += 1
            return nc.sbuf_tensor("%s_%d" % (name, uid[0]), list(shape), dt)

        def sb(name, shape, dt):
            return es.enter_context(sbt(name, list(shape), dt))

        def A(eng, fn, r=(), w=(), dma=False, pw=()):
            return S.add(eng, fn, r, w, dma, pw)

        def dma(eng, out, in_, r=(), w=(), pw=(), slow=False):
            if slow:
                A(eng, lambda e: e.dma_start(out=out, in_=in_, allow_slow_non_contiguous=True), r, w, True, pw)
            else:
                A(eng, lambda e: e.dma_start(out=out, in_=in_), r, w, True, pw)

        def mm(out, lhsT, rhs, st, sp, r, w):
            A("pe", lambda e: e.matmul(out, lhsT=lhsT, rhs=rhs, start=st, stop=sp), r, w)

        def tr(out, in_, ident, r, w):
            A("pe", lambda e: e.transpose(out=out, in_=in_, identity=ident), r, w)

        def act(out, in_, func, r=(), w=(), pw=(), **kw):
            A("act", lambda e: e.activation(out=out, in_=in_, func=func, **kw), r, w, False, pw)

        def tt(eng, out, in0, in1, op, r=(), w=(), pw=()):
            A(eng, lambda e: e.tensor_tensor(out=out, in0=in0, in1=in1, op=op), r, w, False, pw)

        def ts(eng, out, in0, s1, s2, op0, op1, r=(), w=(), pw=()):
            A(eng, lambda e: e.tensor_scalar(out=out, in0=in0, scalar1=s1, scalar2=s2, op0=op0, op1=op1), r, w, False, pw)

        def ts1(eng, out, in0, s1, op0, r=(), w=(), pw=()):
            np_ = out.shape[0]
            if isinstance(s1, (int, float)):
                A(eng, lambda e: e.tensor_scalar(out=out, in0=in0, scalar1=float(s1), scalar2=0.0, op0=op0, op1=ALU.add), r, w, False, pw)
            else:
                A(eng, lambda e: e.tensor_scalar(out=out, in0=in0, scalar1=s1, scalar2=zeroc[0:np_, 0:1], op0=op0, op1=ALU.add),
                  list(r) + ["zeroc"], w, False, pw)

        def stt(eng, out, in0, sc, in1, op0, op1, r=(), w=(), pw=()):
            A(eng, lambda e: e.scalar_tensor_tensor(out=out, in0=in0, scalar=sc, in1=in1, op0=op0, op1=op1), r, w, False, pw)

        def cp(eng, out, in_, r=(), w=(), pw=()):
            if eng == "act":
                A(eng, lambda e: e.activation(out=out, in_=in_, func=AF.Copy), r, w, False, pw)
            else:
                A(eng, lambda e: e.tensor_copy(out=out, in_=in_), r, w, False, pw)

        def recip(out, in_, r=(), w=()):
            A("dve", lambda e: e.reciprocal(out=out, in_=in_), r, w)

        def memset(eng, ap, val, w=()):
            A(eng, lambda e: e.memset(ap, val), (), w)

        def dumpx(name, ap, shape, keys):
            if name in dump_names:
                d_ = nc.dram_tensor("dbg_" + name, list(shape), F32 if ap.dtype == F32 else BF16, kind="ExternalOutput").ap()
                dump_d[name] = d_
                dma("sp", d_, ap, r=keys)

        ps = [es.enter_context(nc.psum_tensor("ps%d" % i, [128, 512], F32)) for i in range(8)]
        pctr = [0]

        def psn():
            i = pctr[0] % 6
            pctr[0] += 1
            return i

        pactr = [0]

        def psacc():
            i = 6 + pactr[0] % 2
            pactr[0] += 1
            return i

        def pk(i):
            return "ps%d" % i

        def psb(i):
            return ps[i][:].bitcast(BF16)

        NWS = 5
        wsl = [sb("ws%d" % i, [128, 2048], BF16) for i in range(NWS)]
        wctr = [0]

        def wslot():
            i = wctr[0] % NWS
            wctr[0] += 1
            return i

        def wk(i):
            return "ws%d" % i

        def wload(si, off, src2d, kt, n):
            dst = wsl[si][:, off:off + kt * n].rearrange("p (k n) -> p k n", n=n)
            src = src2d.rearrange("(k p) n -> p k n", p=128)
            dma("pool", dst, src, pw=[wk(si)])
            return dst

        x_sb = sb("x_sb", [128, 8, T], F32)
        h_sb = sb("h_sb", [128, 8, T], BF16)
        gen = sb("gen", [128, 3072], F32)
        g = [gen[:, i * 512:(i + 1) * 512] for i in range(6)]
        gk = ["g%d" % i for i in range(6)]
        class Carver:
            def __init__(self, tile, n):
                self.t, self.n, self.off = tile, n, 0

            def __call__(self, shape):
                m = 1
                for d_ in shape[1:]:
                    m *= d_
                assert self.off + m <= self.n, (self.off, m, self.n)
                ap = self.t[0:shape[0], self.off:self.off + m]
                self.off += m
                if len(shape) == 3:
                    ap = ap.rearrange("p (a b) -> p a b", b=shape[2])
                elif len(shape) == 4:
                    ap = ap.rearrange("p (a b c) -> p a b c", b=shape[2], c=shape[3])
                return ap

        cf = Carver(sb("miscf", [128, 2600], F32), 2600)
        cb = Carver(sb("miscb", [128, 560], BF16), 560)
        identf = cf([128, 128])
        identb = cb([128, 128])
        onesf = cf([128, 128])
        bones = cf([128, 128])
        band01 = cb([128, 384])
        tri = cf([64, 64])
        m_all = cf([128, LYR, 48, 4])
        bmt = cf([128, LYR, 48])
        nmix = cf([128, LYR, 8])
        nffn = cf([128, LYR, 8])
        nfin = cf([128, 8])
        cT = cf([128, 4, 8])
        csb = cb([128, 4, 8])
        s1t = cf([128, 4, 8])
        lraw = cf([128, 2, 4, LYR])
        lbt = cf([128, 2, 4, LYR])
        omlt = cf([128, 2, 4, LYR])
        lsum = cf([128, 8])
        sinkrep = cf([128, LYR * 8])
        negsink = cf([128, LYR * 8])
        gq = cf([128, 4, LYR])
        gnrep = cf([64, LYR * 128])
        wdw = cf([128, 3, 44])
        bdw = cf([128, 44])
        epsc = cf([128, 1])
        onec = cf([128, 1])
        zeroc = cf([128, 1])
        m0125 = cf([128, 1])
        dma("sp", identf[:], dr["k_ident"], w=["identf"])
        dma("pool", identb[:], dr["k_ident"], w=["identb"])
        dma("sp", bones[:], dr["k_bones"], w=["bones"])
        dma("pool", band01[:], dr["k_band"], w=["band01"])
        dma("sp", tri[:], dr["k_tri"], w=["tri"])
        memset("dve", onesf[:], 1.0, w=["onesf"])
        memset("dve", epsc[:], EPS, w=["epsc"])
        memset("dve", onec[:], 1.0, w=["onec"])
        memset("dve", zeroc[:], 0.0, w=["zeroc"])
        memset("dve", m0125[:], -0.125, w=["m0125"])
        def loadT(dst_ap, src_rows, R, dkey, in_view=None):
            dma("sp", g[5][0:R, 0:128], src_rows, w=[gk[5]])
            pi = psn()
            tr(ps[pi][:, 0:R], g[5][0:R, 0:128], identf[0:R, 0:R], r=[gk[5], "identf"], w=[pk(pi)])
            src = ps[pi][:, 0:R] if in_view is None else in_view(ps[pi][:, 0:R])
            cp("dve", dst_ap, src, r=[pk(pi)], pw=[dkey])

        loadT(nmix.rearrange("p l d -> p (l d)"), dr["norm_mix"].rearrange("l (dt p) -> (l dt) p", p=128), 32, "nmix")
        loadT(nffn.rearrange("p l d -> p (l d)"), dr["norm_ffn"].rearrange("l (dt p) -> (l dt) p", p=128), 32, "nffn")
        loadT(nfin, dr["norm_final"].rearrange("(dt p) -> dt p", p=128), 8, "nfin")
        bmf = bmt.rearrange("p l c -> p (l c)")
        bsrc = dr["b_mod"].rearrange("l (ct p) -> (l ct) p", p=128)
        for hf_ in range(2):
            loadT(bmf[:, hf_ * 96:(hf_ + 1) * 96], bsrc[hf_ * 96:(hf_ + 1) * 96, :], 96, "bmt")
        for a_, nm_ in enumerate(["lb_fwd", "lb_bwd"]):
            loadT(lraw[:, a_].rearrange("p h l -> p l h"), dr[nm_].rearrange("l (h p) -> (l h) p", p=128), 16, "lraw%d" % a_,
                  in_view=lambda ap: ap.rearrange("p (l h) -> p l h", h=4))
        dma("sp", sinkrep[:], DAP(dr["sink_a"], 0, [[0, 128], [1, LYR * 8]]), w=["sinkrep"], slow=True)
        dma("sp", gnrep[:], DAP(dr["gn_c"], 0, [[0, 64], [1, LYR * 128]]), w=["gnrep"], slow=True)
        for gi, (nm, flip) in enumerate([("qn_b", 0), ("qn_b", 1), ("kn_b", 0), ("kn_b", 1)]):
            dma("sp", g[4][0:4, 0:64], dr[nm], w=[gk[4]])
            srcv = g[4][0:4, 0:64].rearrange("l (r two) -> l two r", two=2)
            for hp in range(2):
                for half in range(2):
                    c0 = 128 + hp * 64 + half * 32
                    cp("dve", g[4][0:4, c0:c0 + 32], srcv[:, half ^ flip, :], r=[gk[4]], pw=["g4perm"])
            pi = psn()
            tr(ps[pi][:, 0:4], g[4][0:4, 128:256], identf[0:4, 0:4], r=["g4perm", "identf"], w=[pk(pi)])
            cp("dve", gq[:, gi, :], ps[pi][:, 0:4], r=[pk(pi)], pw=["gq"])
        ts1("dve", negsink[:], sinkrep[:], -1.0, ALU.mult, r=["sinkrep"], w=["negsink"])

        lr = lraw[:].rearrange("p a h l -> p (a h) l")
        lb3 = lbt[:].rearrange("p a h l -> p (a h) l")
        om3 = omlt[:].rearrange("p a h l -> p (a h) l")
        act(lr, lr, AF.Exp, r=["lraw0", "lraw1"], w=["lraw"])
        A("dve", lambda e: e.reduce_sum(out=lsum[:], in_=lr, axis=AX.X), ["lraw"], ["lsum"])
        recip(lsum[:], lsum[:], r=["lsum"], w=["lsum"])
        tt("dve", lr, lr, lsum[:].unsqueeze(2).to_broadcast([128, 8, LYR]), ALU.mult, r=["lraw", "lsum"], w=["lraw"])
        memset("dve", lb3[:, :, 0:1], 0.0, w=["lb0"])
        cp("dve", lb3[:, :, 1:2], lr[:, :, 1:2], r=["lraw", "lb0"], w=["lb1"])
        tt("dve", lb3[:, :, 2:3], lb3[:, :, 1:2], lr[:, :, 2:3], ALU.add, r=["lraw", "lb1"], w=["lb2"])
        tt("dve", lb3[:, :, 3:4], lb3[:, :, 2:3], lr[:, :, 3:4], ALU.add, r=["lraw", "lb2"], w=["lb3"])
        ts("dve", lb3, lb3, 0.0, 1.0, ALU.max, ALU.min, r=["lb0", "lb1", "lb2", "lb3"], w=["lb"])
        ts("dve", om3, lb3, -1.0, 1.0, ALU.mult, ALU.add, r=["lb"], w=["oml"])

        NS = nb + 1
        memset("dve", cT[:], 0.0, w=["cT"])
        loadT(cT.rearrange("p j k -> p (j k)")[:, 0:nb * 8], dr["c"].rearrange("j (kt p) -> (j kt) p", p=128), nb * 8, "cT")
        loadT(cT[:, nb, :], dr["c_ctx"].rearrange("(kt p) -> kt p", p=128), 8, "cT")
        act(csb[:], cT[:], AF.Silu, r=["cT"], w=["csb"])
        for l in range(depth):
            for gi in range(24):
                si = wslot()
                wv = wload(si, 0, dr["w_mod"][l][:, gi * 256:(gi + 1) * 256], 8, 256)
                pi = psn()
                for ci in range(2):
                    for kt in range(8):
                        mm(ps[pi][:, ci * 4:ci * 4 + 4], wv[:, kt, ci * 128:(ci + 1) * 128], csb[:, :, kt], kt == 0, kt == 7,
                           r=[wk(si), "csb"], w=[pk(pi)])
                tt("dve", m_all[:, l, gi * 2:gi * 2 + 2, :], ps[pi][:, 0:8].rearrange("p (c j) -> p c j", j=4),
                   bmt[:, l, gi * 2:gi * 2 + 2].unsqueeze(2).to_broadcast([128, 2, 4]), ALU.add,
                   r=[pk(pi), "bmt"], pw=["m_all"])

        def mod(l, i, dt, j):
            return m_all[:, l, i * 8 + dt, j:j + 1]

        def load_x(b):
            with ExitStack() as ph:
                stage = [ph.enter_context(sbt("stg%d" % i, [128, 1024], F32)) for i in range(2)]
                for tt_ in range(18):
                    src = dr["ctx"][b, tt_ * 128:(tt_ + 1) * 128, :] if tt_ < 2 else dr["x"][b, (tt_ - 2) * 128:(tt_ - 1) * 128, :]
                    st = stage[tt_ % 2]
                    sk = "stg%d" % (tt_ % 2)
                    dma("sp", st[:], src, w=[sk])
                    bi = blk_of(tt_ * 128)
                    for half in range(2):
                        pi = psn()
                        for q in range(4):
                            dt = half * 4 + q
                            tr(ps[pi][:, q * 128:(q + 1) * 128], st[:, dt * 128:(dt + 1) * 128], identf[:], r=[sk, "identf"], w=[pk(pi)])
                        cp("dve" if half == 0 else "act", x_sb[:, half * 4:(half + 1) * 4, tt_ * 128:(tt_ + 1) * 128],
                           ps[pi][:].rearrange("p (q t) -> p q t", t=128), r=[pk(pi)],
                           pw=[("x", half * 4 + q, bi) for q in range(4)])
                S.barrier()

        def layer_scalars(l, b):
            for kind, (mi, nt, nk) in enumerate([(1, nmix, "nmix"), (1, nmix, "nmix"), (4, nffn, "nffn"), (4, nffn, "nffn")]):
                j = b if kind % 2 == 0 else nb
                stt("dve", s1t[:, kind, :], m_all[:, l, mi * 8:mi * 8 + 8, j], 1.0, nt[:, l, :], ALU.add, ALU.mult,
                    r=["m_all", nk], w=[("s1", kind)])

        def norm_to_h(l, b, which, blocks):
            for bi in blocks:
                t0, n = BLK[bi]
                j = nb if bi == 0 else b
                kind = which * 2 + (1 if bi == 0 else 0)
                pst = psn()
                for dt in range(8):
                    sq = g[dt % 2]
                    act(sq[:, :n], x_sb[:, dt, t0:t0 + n], AF.Square, r=[("x", dt, bi)], w=[gk[dt % 2]])
                    mm(ps[pst][:, :n], onesf[:], sq[:, :n], dt == 0, dt == 7, r=["onesf", gk[dt % 2]], w=[pk(pst)])
                act(g[2][:, :n], ps[pst][:, :n], AF.Sqrt, r=[pk(pst), "epsc"], w=[gk[2]], scale=1.0 / D, bias=epsc[:, 0:1])
                recip(g[2][:, :n], g[2][:, :n], r=[gk[2]], w=[gk[2]])
                for dt in range(8):
                    tmp = g[3 + dt % 2]
                    stt("dve", tmp[:, :n], x_sb[:, dt, t0:t0 + n], s1t[:, kind, dt:dt + 1], g[2][:, :n], ALU.mult, ALU.mult,
                        r=[("x", dt, bi), ("s1", kind), gk[2]], w=[gk[3 + dt % 2]])
                    act(h_sb[:, dt, t0:t0 + n], tmp[:, :n], AF.Identity, r=[gk[3 + dt % 2], "m_all"], w=[("h", dt, bi)],
                        bias=mod(l, 3 * which, dt, j), scale=1.0)

        def proj_fm(wv, wkey, bi, ncols=128):
            t0, n = BLK[bi]
            pi = psn()
            for kt in range(8):
                mm(ps[pi][0:ncols, :n], wv[:, kt, :], h_sb[:, kt, t0:t0 + n], kt == 0, kt == 7,
                   r=[wkey, ("h", kt, bi)], w=[pk(pi)])
            return pi

        def hgrn(l, b, last):
            CS = 32
            with ExitStack() as ph:
                def psb_(name, shape, dt):
                    return ph.enter_context(sbt(name, list(shape), dt))
                buf1 = gen[:, 0:T]
                buf2t = psb_("hb2", [128, T], F32)
                buf3t = psb_("hb3", [128, T], F32)
                buf2 = buf2t[:]
                buf3 = buf3t[:]
                qs = psb_("hqs", [128, T], BF16)
                qp = psb_("hqp", [128, T], BF16)
                kp = psb_("hkp", [128, T], BF16)
                vt = psb_("hvt", [64, 36, 128], BF16)
                ystg = [psb_("hys%d" % i_, [128, 512], BF16) for i_ in range(2)]
                ysc_ctr = [0]
                hf = Carver(psb_("hmf", [128, 2048], F32), 2048)
                hb_ = Carver(psb_("hmb", [128, 576], BF16), 576)
                sc = hf([128, 3, NCH])
                esc = hf([128, 3, NCH])
                Sst = hf([128, 128])
                Sbf2 = [hb_([128, 128]), hb_([128, 128])]
                scm = hb_([64, 32])
                kpT = hb_([64, 128])
                osum = hf([32, 128])
                junk = hf([32, 128])
                ssq = hf([32, 1])
                rst = hf([32, 1])
                sg = hf([32, 128])
                yv = hb_([32, 128])
                obw = [hf([32, 128]) for _ in range(4)]
                obr = [hf([32, 128]) for _ in range(4)]
                obc = [0, 0]
                b2_3 = buf2.rearrange("p (c i) -> p c i", i=CS)
                b3_3 = buf3.rearrange("p (c i) -> p c i", i=CS)
                MID, LST, REFB = 15, 31, 16

                for h in range(4):
                    sA = wslot()
                    Wq_ = wload(sA, 0, w_in[l][:, CQ + h * 128:CQ + (h + 1) * 128], 8, 128)
                    Wf_ = wload(sA, 1024, w_in[l][:, CF + h * 128:CF + (h + 1) * 128], 8, 128)
                    sB = wslot()
                    Wb_ = wload(sB, 0, w_in[l][:, CB + h * 128:CB + (h + 1) * 128], 8, 128)
                    Wi_ = wload(sB, 1024, w_in[l][:, CI + h * 128:CI + (h + 1) * 128], 8, 128)
                    sC = wslot()
                    Wg_ = wload(sC, 0, w_in[l][:, CG + h * 128:CG + (h + 1) * 128], 8, 128)
                    for bi in range(5):
                        t0, n = BLK[bi]
                        pi = proj_fm(Wq_, wk(sA), bi)
                        act(qs[:, t0:t0 + n], ps[pi][:, :n], AF.Silu, r=[pk(pi)], pw=["hqs"])
                    for cpair in range(18):
                        pi = psn()
                        for cc in range(2):
                            c = cpair * 2 + cc
                            bi = blk_of(c * 64)
                            for kt in range(8):
                                mm(ps[pi][0:64, cc * 128:(cc + 1) * 128], h_sb[:, kt, c * 64:(c + 1) * 64], Wi_[:, kt, :], kt == 0, kt == 7,
                                   r=[wk(sB), ("h", kt, bi)], w=[pk(pi)])
                        cp("act", vt[0:64, cpair * 2:cpair * 2 + 2, :], ps[pi][0:64, 0:256].rearrange("p (c e) -> p c e", e=128),
                           r=[pk(pi)], pw=["hvt"])

                    for dirn in (1, 0):
                        Wz, wzk = (Wf_, wk(sA)) if dirn == 0 else (Wb_, wk(sB))
                        for bi in range(5):
                            t0, n = BLK[bi]
                            pi = proj_fm(Wz, wzk, bi)
                            act(buf1[:, t0:t0 + n], ps[pi][:, :n], AF.Sigmoid, r=[pk(pi)], pw=["hb1"])
                        ts("dve", buf1, buf1, omlt[:, dirn, h, l:l + 1], lbt[:, dirn, h, l:l + 1], ALU.mult, ALU.add,
                           r=["hb1", "lb", "oml"], w=["hb1"])
                        act(buf2, buf1, AF.Ln, r=["hb1"], w=["hb2"])
                        A("dve", lambda e: e.tensor_tensor_scan(out=buf3, data0=onec[:, 0:1].to_broadcast([128, T]), data1=buf2,
                                                                 initial=0.0, op0=ALU.mult, op1=ALU.add),
                          ["hb2", "onec"], ["hb3"])
                        if dirn == 0:
                            cp("dve", sc[:, 0, 0:1], b3_3[:, 0, MID:MID + 1], r=["hb3"], w=["hsc0a"])
                            tt("dve", sc[:, 0, 1:NCH], b3_3[:, 1:NCH, MID], b3_3[:, 0:NCH - 1, LST], ALU.subtract, r=["hb3"], w=["hsc0b"])
                            cp("dve", sc[:, 1, 0:1], b3_3[:, 0, LST:LST + 1], r=["hb3"], w=["hsc1a"])
                            tt("dve", sc[:, 1, 1:NCH], b3_3[:, 1:NCH, LST], b3_3[:, 0:NCH - 1, LST], ALU.subtract, r=["hb3"], w=["hsc1b"])
                            tt("dve", sc[:, 2, :], b3_3[:, :, LST], b3_3[:, :, MID], ALU.subtract, r=["hb3"], w=["hsc2"])
                            act(esc[:], sc[:], AF.Exp, r=["hsc0a", "hsc0b", "hsc1a", "hsc1b", "hsc2"], w=["hesc"])
                            tt("dve", b2_3, b3_3, b3_3[:, :, MID:MID + 1].to_broadcast([128, NCH, CS]), ALU.subtract, r=["hb3", "hb2"], w=["hb2"])
                            dsrc, dk, ebuf, ek = buf2, "hb2", buf3, "hb3"
                        else:
                            tt("dve", buf2, buf2, buf3, ALU.subtract, r=["hb2", "hb3"], w=["hb2"])
                            tt("dve", sc[:, 0, :], b3_3[:, :, LST], b2_3[:, :, REFB], ALU.add, r=["hb3", "hb2"], w=["hsc0a"])
                            tt("dve", sc[:, 1, :], b3_3[:, :, LST], b2_3[:, :, 0], ALU.add, r=["hb3", "hb2"], w=["hsc1a"])
                            tt("dve", sc[:, 2, :], b2_3[:, :, 0], b2_3[:, :, REFB], ALU.subtract, r=["hb2"], w=["hsc2"])
                            act(esc[:], sc[:], AF.Exp, r=["hsc0a", "hsc1a", "hsc2"], w=["hesc"])
                            tt("dve", b3_3, b2_3, b2_3[:, :, REFB:REFB + 1].to_broadcast([128, NCH, CS]), ALU.subtract, r=["hb2", "hb3"], w=["hb3"])
                            dsrc, dk, ebuf, ek = buf3, "hb3", buf2, "hb2"
                        act(ebuf, dsrc, AF.Exp, r=[dk], w=[ek])
                        tt("dve", qp[:], qs[:], ebuf, ALU.mult, r=["hqs", ek], w=["hqp"])
                        act(ebuf, dsrc, AF.Exp, r=[dk, "hqp"], w=[ek], scale=-1.0)
                        ts("dve", buf1, buf1, -1.0, 1.0, ALU.mult, ALU.add, r=["hb1"], w=["hb1"])
                        tt("dve", kp[:], buf1, ebuf, ALU.mult, r=["hb1", ek], w=["hkp"])

                        order = list(range(7, -1, -1)) + list(range(NCH - 1, 7, -1)) if dirn == 1 else list(range(NCH))
                        final = dirn == 0
                        first = True
                        for idx, c in enumerate(order):
                            if HSTAGE == 1:
                                break
                            par = c % 2
                            w_ = c // 2
                            csl = slice(c * CS, (c + 1) * CS)
                            wsl_ = slice(w_ * 64, (w_ + 1) * 64)
                            prow = slice(par * 32, par * 32 + 32)
                            bi = blk_of(c * CS)
                            pi = psn()
                            sck = "hscm%d" % par
                            if par == 0:
                                mm(ps[pi][0:32, 0:32], kp[:, csl], qp[:, csl], True, True, r=["hkp", "hqp"], w=[pk(pi)])
                            else:
                                mm(ps[pi][0:64, 0:32], kp[:, wsl_], qp[:, csl], True, True, r=["hkp", "hqp"], w=[pk(pi)])
                            tt("dve", scm[prow, :], ps[pi][prow, 0:32], tri[prow, dirn * 32:(dirn + 1) * 32], ALU.mult, r=[pk(pi), "tri"], w=[sck])
                            if idx < len(order) - 1:
                                pt = psn()
                                tr(psb(pt)[0:64, 0:128], kp[:, wsl_], identb[:], r=["hkp", "identb"], w=[pk(pt)])
                                cp("act", kpT[prow, :], psb(pt)[prow, 0:128], r=[pk(pt)], w=["hkpT%d" % par])
                                pd = psn()
                                mm(ps[pd][:, 0:128], kpT[prow, :], vt[prow, w_, :], True, True, r=["hkpT%d" % par, "hvt"], w=[pk(pd)])
                                if first:
                                    ts1("dve", Sst, ps[pd][:, 0:128], esc[:, 2, c:c + 1], ALU.mult, r=[pk(pd), "hesc"], w=["hS"])
                                else:
                                    ts1("dve", Sst, Sst, esc[:, 1, c:c + 1], ALU.mult, r=["hS", "hesc"], w=["hS"])
                                    stt("dve", Sst, ps[pd][:, 0:128], esc[:, 2, c:c + 1], Sst, ALU.mult, ALU.add,
                                        r=[pk(pd), "hesc", "hS"], w=["hS"])
                                cn = order[idx + 1]
                                act(Sbf2[(idx + 1) % 2], Sst, AF.Identity, r=["hS", "hesc", "zeroc"], w=["hSb%d" % ((idx + 1) % 2)], scale=esc[:, 0, cn:cn + 1], bias=zeroc[:, 0:1])
                            po = psn()
                            mm(ps[po][0:32, 0:128], scm[prow, :], vt[prow, w_, :], True, first, r=[sck, "hvt"], w=[pk(po)])
                            if not first:
                                mm(ps[po][0:32, 0:128], qp[:, csl], Sbf2[idx % 2], False, True, r=["hqp", "hSb%d" % (idx % 2)], w=[pk(po)])
                            if not final:
                                k_ = obc[0] % 4
                                obc[0] += 1
                                cp("act", obw[k_], ps[po][0:32, 0:128], r=[pk(po)], w=[("obw", k_)])
                                if HSTAGE != 2:
                                    dma("sp", ob_d[c], obw[k_], r=[("obw", k_)], w=[("obd", c)])
                            elif HSTAGE not in (2, 3) and not (last and c < 8):
                                k_ = obc[1] % 4
                                obc[1] += 1
                                dma("sp", obr[k_], ob_d[c], r=[("obd", c)], w=[("obr", k_)])
                                tt("dve", osum, ps[po][0:32, 0:128], obr[k_], ALU.add, r=[pk(po), ("obr", k_)], w=["hosum"])
                                act(junk, osum, AF.Square, r=["hosum"], w=["hjunk", "hssq"], accum_out=ssq[:, 0:1])
                                act(rst, ssq, AF.Sqrt, r=["hssq", "epsc"], w=["hrst"], scale=1.0 / 128, bias=epsc[0:32, 0:1])
                                recip(rst, rst, r=["hrst"], w=["hrst"])
                                pg = psn()
                                for kt in range(8):
                                    mm(ps[pg][0:32, 0:128], h_sb[:, kt, csl], Wg_[:, kt, :], kt == 0, kt == 7,
                                       r=[wk(sC), ("h", kt, bi)], w=[pk(pg)])
                                act(sg, ps[pg][0:32, 0:128], AF.Silu, r=[pk(pg)], w=["hsg"])
                                tt("pool", sg, sg, gnrep[0:32, l * 128:(l + 1) * 128], ALU.mult, r=["hsg", "gnrep"], w=["hsg"])
                                stt("dve", yv, osum, rst[:, 0:1], sg, ALU.mult, ALU.mult, r=["hosum", "hrst", "hsg"], w=["hyv"])
                                pt2 = psn()
                                tr(psb(pt2)[:, 0:32], yv, identb[0:32, 0:32], r=["hyv", "identb"], w=[pk(pt2)])
                                bt0, bn = BLK[bi]
                                sidx = ysc_ctr[0] % 2
                                cp("act", ystg[sidx][:, c * CS - bt0:c * CS - bt0 + CS], psb(pt2)[:, 0:32], r=[pk(pt2)], pw=["hys%d" % sidx])
                                if c * CS + CS == bt0 + bn and HSTAGE != 4:
                                    dma("sp", ysc_blk(2, h, bi), ystg[sidx][:].bitcast(F32)[:, 0:bn // 2], r=["hys%d" % sidx], w=[("ysc", 2, h, bi)])
                                    ysc_ctr[0] += 1
                            first = False
                S.barrier()

        def attn(l, b, br, last):
            qoff, koff, voff = (AQ, AK, AV) if br == 0 else (BQ, BK, BV)
            normed = br == 1
            with ExitStack() as ph:
                def psb_(name, shape, dt):
                    return ph.enter_context(sbt(name, list(shape), dt))
                KT = psb_("aKT", [128, T], BF16)
                Vt = psb_("aVt", [128, 18, 128], BF16)
                WA = psb_("aWA", [128, 8, 128], BF16)
                WB = psb_("aWB", [128, 8, 128], BF16)
                QTb = psb_("aQT", [128, 512], BF16)
                Pb = psb_("aPb", [128, 1280], BF16)
                PT = psb_("aPT", [128, 1280], BF16)
                af = Carver(psb_("amf", [128, 24], F32), 24)
                mx = af([128, 4])
                negm = af([128, 2])
                rs = af([128, 8])
                es_ = af([128, 2])
                rinv2 = af([128, 2])
                on = psb_("aon", [128, 128], BF16)
                ystg = [psb_("ays%d" % i_, [128, 512], BF16) for i_ in range(2)]
                ysc_ctr = [0]

                def build_perm(si, wnat):
                    nat = wnat.rearrange("p k (hh r two) -> p (k hh) two r", hh=2, r=32, two=2)
                    wa = WA[:].rearrange("p k (hh half r) -> p (k hh) half r", hh=2, half=2, r=32)
                    wb = WB[:].rearrange("p k (hh half r) -> p (k hh) half r", hh=2, half=2, r=32)
                    for half in range(2):
                        cp("dve", wa[:, :, half, :], nat[:, :, half, :], r=[wk(si)], pw=["aWA"])
                        cp("dve", wb[:, :, half, :], nat[:, :, 1 - half, :], r=[wk(si)], pw=["aWB"])

                def rope_block(pa, pb, bi, out_ap, okey, gidx):
                    t0, n = BLK[bi]
                    dma("sp", g[0][:, :n], DAP(dr["k_ropec"], 128 * t0, [[n, 128], [1, n]]), w=[gk[0]])
                    dma("sp", g[1][:, :n], DAP(dr["k_ropes"], 128 * t0, [[n, 128], [1, n]]), w=[gk[1]])
                    if not normed:
                        tt("dve", g[2][:, :n], ps[pa][:, :n], g[0][:, :n], ALU.mult, r=[pk(pa), gk[0]], w=[gk[2]])
                        tt("dve", g[3][:, :n], ps[pb][:, :n], g[1][:, :n], ALU.mult, r=[pk(pb), gk[1]], w=[gk[3]])
                        tt("dve", out_ap, g[2][:, :n], g[3][:, :n], ALU.add, r=[gk[2], gk[3]], pw=[okey])
                    else:
                        act(g[4][:, :n], ps[pa][:, :n], AF.Square, r=[pk(pa)], w=[gk[4]])
                        pq = psn()
                        mm(ps[pq][:, :n], bones[:], g[4][:, :n], True, True, r=["bones", gk[4]], w=[pk(pq)])
                        act(g[5][:, :n], ps[pq][:, :n], AF.Sqrt, r=[pk(pq), "epsc"], w=[gk[5]], scale=1.0 / 64, bias=epsc[:, 0:1])
                        recip(g[5][:, :n], g[5][:, :n], r=[gk[5]], w=[gk[5]])
                        stt("dve", g[2][:, :n], ps[pa][:, :n], gq[:, gidx, l:l + 1], g[0][:, :n], ALU.mult, ALU.mult,
                            r=[pk(pa), "gq", gk[0]], w=[gk[2]])
                        stt("dve", g[3][:, :n], ps[pb][:, :n], gq[:, gidx + 1, l:l + 1], g[1][:, :n], ALU.mult, ALU.mult,
                            r=[pk(pb), "gq", gk[1]], w=[gk[3]])
                        tt("dve", g[2][:, :n], g[2][:, :n], g[3][:, :n], ALU.add, r=[gk[2], gk[3]], w=[gk[2]])
                        tt("dve", out_ap, g[2][:, :n], g[5][:, :n], ALU.mult, r=[gk[2], gk[5]], pw=[okey])

                si = wslot()
                wn = wload(si, 0, w_in[l][:, koff:koff + 128], 8, 128)
                build_perm(si, wn)
                for bi in range(5):
                    t0, n = BLK[bi]
                    pa = proj_fm(WA[:], "aWA", bi)
                    pb = proj_fm(WB[:], "aWB", bi)
                    rope_block(pa, pb, bi, KT[:, t0:t0 + n], ("aKT", bi), 2)
                si = wslot()
                wv_ = wload(si, 0, w_in[l][:, voff:voff + 128], 8, 128)
                for tg in range(5):
                    tiles = list(range(tg * 4, min(tg * 4 + 4, 18)))
                    pi = psn()
                    for q, tt_ in enumerate(tiles):
                        bi = blk_of(tt_ * 128)
                        for kt in range(8):
                            mm(ps[pi][:, q * 128:(q + 1) * 128], h_sb[:, kt, tt_ * 128:(tt_ + 1) * 128], wv_[:, kt, :], kt == 0, kt == 7,
                               r=[wk(si), ("h", kt, bi)], w=[pk(pi)])
                    nt_ = len(tiles)
                    cp("act", Vt[:, tiles[0]:tiles[0] + nt_, :], ps[pi][:, 0:nt_ * 128].rearrange("p (q e) -> p q e", e=128),
                       r=[pk(pi)], pw=["aVt"])

                def attend(i, qt0, qc, ranges, band_lo, sink, use_max):
                    po = psacc()
                    if use_max:
                        st_ = {}

                        def sA1(hh):
                            head = i + 4 * hh
                            rows = slice(hh * 64, hh * 64 + 64)
                            lhsT = QTb[rows, qc:qc + 128]
                            pbo = hh * 640
                            pbk = ("aPbA", hh)
                            banks = []
                            for (k0, nk) in ranges:
                                pi = psn()
                                mm(ps[pi][:, 0:nk], lhsT, KT[rows, k0:k0 + nk], True, True,
                                   r=["aQT"] + [("aKT", kb) for kb in sorted({blk_of(k0), blk_of(k0 + nk - 1)})], w=[pk(pi)])
                                banks.append(pi)
                            for j, pi in enumerate(banks):
                                nk = ranges[j][1]
                                A("dve", lambda e, j=j, pi=pi, nk=nk, hh=hh: e.reduce_max(out=mx[:, 2 * hh + j:2 * hh + j + 1], in_=ps[pi][:, 0:nk], axis=AX.X),
                                  [pk(pi)], [("amx", hh, j)])
                            m0 = mx[:, 2 * hh:2 * hh + 1]
                            if len(banks) == 2:
                                tt("dve", m0, m0, mx[:, 2 * hh + 1:2 * hh + 2], ALU.max, r=[("amx", hh, 0), ("amx", hh, 1)], w=[("amx", hh, 0)])
                            ng = negm[:, hh:hh + 1]
                            if sink:
                                ts("dve", ng, m0, m0125[:, 0:1], negsink[:, l * 8 + head:l * 8 + head + 1], ALU.mult, ALU.min,
                                   r=[("amx", hh, 0), "negsink", "m0125"], w=[("anegm", hh)])
                            else:
                                ts1("dve", ng, m0, -0.125, ALU.mult, r=[("amx", hh, 0)], w=[("anegm", hh)])
                            col = 0
                            for j, pi in enumerate(banks):
                                nk = ranges[j][1]
                                act(Pb[:, pbo + col:pbo + col + nk], ps[pi][:, 0:nk], AF.Exp, r=[pk(pi), ("anegm", hh)], pw=[pbk], scale=0.125, bias=ng)
                                col += nk
                            if band_lo is not None:
                                nbk = ranges[1][1]
                                tt("dve", Pb[:, pbo + 256:pbo + 256 + nbk], Pb[:, pbo + 256:pbo + 256 + nbk], band01[:, band_lo:band_lo + nbk], ALU.mult,
                                   r=[pbk, "band01"], w=[pbk])
                            rsum = rs[:, 6 + hh:7 + hh]
                            A("dve", lambda e, col=col, pbo=pbo, rsum=rsum: e.reduce_sum(out=rsum, in_=Pb[:, pbo:pbo + col], axis=AX.X), [pbk], [("arsA", hh)])
                            if sink:
                                act(es_[:, hh:hh + 1], sinkrep[:, l * 8 + head:l * 8 + head + 1], AF.Exp, r=["sinkrep", ("anegm", hh)], w=[("aes", hh)], bias=ng, scale=1.0)
                                tt("dve", rsum, rsum, es_[:, hh:hh + 1], ALU.add, r=[("arsA", hh), ("aes", hh)], w=[("arsA", hh)])
                            recip(rinv2[:, hh:hh + 1], rsum, r=[("arsA", hh)], w=[("arinv", hh)])
                            st_[hh] = col

                        def sA2(hh):
                            col = st_[hh]
                            pbo = hh * 640
                            pbk = ("aPbA", hh)
                            ptk = ("aPTA", hh)
                            ntile = col // 128
                            pt = psn()
                            for kt in range(ntile):
                                tr(psb(pt)[:, kt * 128:(kt + 1) * 128], Pb[:, pbo + kt * 128:pbo + (kt + 1) * 128], identb[:], r=[pbk, "identb"], w=[pk(pt)])
                            cp("act" if hh == 0 else "dve", PT[:, pbo:pbo + ntile * 128], psb(pt)[:, 0:ntile * 128], r=[pk(pt)], w=[ptk])
                            ktile = []
                            for (k0, nk) in ranges:
                                ktile += [k0 // 128 + q for q in range(nk // 128)]
                            for kt in range(ntile):
                                mm(ps[po][:, hh * 64:(hh + 1) * 64], PT[:, pbo + kt * 128:pbo + (kt + 1) * 128], Vt[:, ktile[kt], hh * 64:(hh + 1) * 64],
                                   kt == 0, kt == ntile - 1, r=[ptk, "aVt"], w=[pk(po)])

                        sA1(0)
                        sA1(1)
                        sA2(0)
                        sA2(1)
                    for hh in (range(2) if not use_max else ()):
                        head = i + 4 * hh
                        rows = slice(hh * 64, hh * 64 + 64)
                        lhsT = QTb[rows, qc:qc + 128]
                        if True:
                            nr = len(ranges)

                            def stage1(j):
                                k0, nk = ranges[j]
                                pi = psn()
                                mm(ps[pi][:, 0:nk], lhsT, KT[rows, k0:k0 + nk], True, True, r=["aQT", ("aKT", blk_of(k0))], w=[pk(pi)])
                                hb = (j % 2) * 512
                                act(Pb[:, hb:hb + nk], ps[pi][:, 0:nk], AF.Exp, r=[pk(pi)], w=[("aPb", j % 2), ("ars", j)], scale=0.125,
                                    accum_out=rs[:, j:j + 1])

                            def stage2(j):
                                k0, nk = ranges[j]
                                hb = (j % 2) * 512
                                pbk = ("aPb", j % 2)
                                ntile = nk // 128
                                pt = psn()
                                for kt in range(ntile):
                                    tr(psb(pt)[:, kt * 128:(kt + 1) * 128], Pb[:, hb + kt * 128:hb + (kt + 1) * 128], identb[:], r=[pbk, "identb"], w=[pk(pt)])
                                ptk = ("aPT", j % 2)
                                cp("act" if j % 2 == 0 else "dve", PT[:, hb:hb + ntile * 128], psb(pt)[:, 0:ntile * 128], r=[pk(pt)], w=[ptk])
                                for kt in range(ntile):
                                    mm(ps[po][:, hh * 64:(hh + 1) * 64], PT[:, hb + kt * 128:hb + (kt + 1) * 128],
                                       Vt[:, k0 // 128 + kt, hh * 64:(hh + 1) * 64], j == 0 and kt == 0, j == nr - 1 and kt == ntile - 1,
                                       r=[ptk, "aVt"], w=[pk(po)])

                            stage1(0)
                            for j in range(nr):
                                if j + 1 < nr:
                                    stage1(j + 1)
                                stage2(j)
                            if nr > 1:
                                A("dve", lambda e, nr=nr: e.reduce_sum(out=rs[:, 0:1], in_=rs[:, 0:nr], axis=AX.X),
                                  [("ars", j) for j in range(nr)], [("ars", 0)])
                            recip(rinv2[:, hh:hh + 1], rs[:, 0:1], r=[("ars", 0)], w=[("arinv", hh)])
                    tt("dve", on[:].rearrange("p (a d) -> p a d", d=64), ps[po][:, 0:128].rearrange("p (a d) -> p a d", d=64),
                       rinv2[:].unsqueeze(2).to_broadcast([128, 2, 64]), ALU.mult, r=[pk(po), ("arinv", 0), ("arinv", 1)], w=["aon"])
                    pt = psn()
                    tr(psb(pt)[:, 0:128], on[:], identb[:], r=["aon", "identb"], w=[pk(pt)])
                    bi_ = blk_of(qt0)
                    bt0, bn = BLK[bi_]
                    sidx = ysc_ctr[0] % 2
                    cp("act", ystg[sidx][:, qt0 - bt0:qt0 - bt0 + 128], psb(pt)[:, 0:128], r=[pk(pt)], pw=["ays%d" % sidx])
                    if qt0 + 128 == bt0 + bn:
                        dma("sp", ysc_blk(br, i, bi_), ystg[sidx][:].bitcast(F32)[:, 0:bn // 2], r=["ays%d" % sidx], w=[("ysc", br, i, bi_)])
                        ysc_ctr[0] += 1

                for i in range(4):
                    si = wslot()
                    wn = wsl[si][:, 0:1024].rearrange("p (k n) -> p k n", n=128)
                    for hh in range(2):
                        hd = i + 4 * hh
                        dst = wn[:, :, hh * 64:(hh + 1) * 64]
                        src = w_in[l][:, qoff + hd * 64:qoff + (hd + 1) * 64].rearrange("(k p) n -> p k n", p=128)
                        dma("pool", dst, src, pw=[wk(si)])
                    build_perm(si, wn)
                    for bi in range(5):
                        if bi == 0 and last:
                            continue
                        t0, n = BLK[bi]
                        pa = proj_fm(WA[:], "aWA", bi)
                        pb = proj_fm(WB[:], "aWB", bi)
                        rope_block(pa, pb, bi, QTb[:, :n], "aQT", 0)
                        for qi in range(n // 128):
                            qt0 = t0 + qi * 128
                            if bi == 0:
                                attend(i, qt0, qi * 128, [(0, 256)], None, br == 0, br == 0)
                            elif br == 0:
                                nq = (qt0 - 256) // 128
                                lo = max(nq - 1, 0)
                                hi = min(nq + 2, 16)
                                attend(i, qt0, qi * 128, [(0, 256), (256 + lo * 128, (hi - lo) * 128)], 128 if nq == 0 else 0, True, True)
                            else:
                                attend(i, qt0, qi * 128, [(0, 512), (512, 512), (1024, 512), (1536, 512), (2048, 256)], None, False, False)
                S.barrier()

        def merge(l, b, last):
            with ExitStack() as ph:
                uT = ph.enter_context(sbt("uT", [128, 8, 1280], BF16))
                yS = [ph.enter_context(sbt("yS%d" % n_, [128, 4, 1280], BF16)) for n_ in range(3)]
                sbs = [[0, 1, 2], [3, 4]]
                if last:
                    sbs = [[1, 2], [3, 4]]
                for sbi, blocks in enumerate(sbs):
                    base = 0 if sbi == 0 else 1280
                    for n_ in range(3):
                        for bi in blocks:
                            t0, n = BLK[bi]
                            ysf = yS[n_][:].rearrange("p k t -> p (k t)").bitcast(F32).rearrange("p (k t) -> p k t", k=4)
                            for k_ in range(4):
                                dma("sp", ysf[:, k_, (t0 - base) // 2:(t0 - base + n) // 2], ysc_blk(n_, k_, bi),
                                    r=[("ysc", n_, k_, bi)], pw=[("yS", n_)])
                    for dt in range(8):
                        sG = wslot()
                        Wg01 = [wload(sG, n_ * 1024, w_in[l][:, GT + n_ * 1024 + dt * 128:GT + n_ * 1024 + (dt + 1) * 128], 8, 128) for n_ in range(2)]
                        sG2 = wslot()
                        Wg2 = wload(sG2, 0, w_in[l][:, GT + 2048 + dt * 128:GT + 2048 + (dt + 1) * 128], 8, 128)
                        sW = wslot()
                        wbv = wsl[sW][:, 0:1536].rearrange("p (n k c) -> p n k c", n=3, k=4)
                        wb_ = dr["w_branch"]
                        for n_ in range(2):
                            for hh in range(2):
                                src = wb_[l, n_, hh * 256:(hh + 1) * 256, dt * 128:(dt + 1) * 128].rearrange("(j r) c -> r j c", r=64)
                                dma("pool", wbv[hh * 64:(hh + 1) * 64, n_, :, :], src, pw=[wk(sW)])
                        dma("pool", wbv[:, 2, :, :], wb_[l, 2, :, dt * 128:(dt + 1) * 128].rearrange("(k p) c -> p k c", p=128), pw=[wk(sW)])
                        Wgs = [(Wg01[0], wk(sG)), (Wg01[1], wk(sG)), (Wg2, wk(sG2))]
                        for bi in blocks:
                            t0, n = BLK[bi]
                            for n_ in range(3):
                                pg = proj_fm(Wgs[n_][0], Wgs[n_][1], bi)
                                sgt = g[n_ % 2]
                                act(sgt[:, :n], ps[pg][:, :n], AF.Sigmoid, r=[pk(pg)], w=[gk[n_ % 2]])
                                pu = psn()
                                for kt in range(4):
                                    mm(ps[pu][:, :n], wbv[:, n_, kt, :], yS[n_][:, kt, t0 - base:t0 - base + n], kt == 0, kt == 3,
                                       r=[wk(sW), ("yS", n_)], w=[pk(pu)])
                                if n_ == 0:
                                    tt("dve", g[2][:, :n], sgt[:, :n], ps[pu][:, :n], ALU.mult, r=[gk[0], pk(pu)], w=[gk[2]])
                                else:
                                    tt("dve", g[3][:, :n], sgt[:, :n], ps[pu][:, :n], ALU.mult, r=[gk[n_ % 2], pk(pu)], w=[gk[3]])
                                    if n_ == 1:
                                        tt("dve", g[2][:, :n], g[2][:, :n], g[3][:, :n], ALU.add, r=[gk[2], gk[3]], w=[gk[2]])
                                    else:
                                        tt("dve", uT[:, dt, t0 - base:t0 - base + n], g[2][:, :n], g[3][:, :n], ALU.add,
                                           r=[gk[2], gk[3]], pw=[("uT", dt)])
                    for half in range(4):
                        sO = wslot()
                        Wo = wload(sO, 0, dr["w_out"][l][:, half * 256:(half + 1) * 256], 8, 256)
                        for q in range(2):
                            dt2 = half * 2 + q
                            for bi in blocks:
                                t0, n = BLK[bi]
                                j = nb if bi == 0 else b
                                pi = psn()
                                for kt in range(8):
                                    mm(ps[pi][:, :n], Wo[:, kt, q * 128:(q + 1) * 128], uT[:, kt, t0 - base:t0 - base + n], kt == 0, kt == 7,
                                       r=[wk(sO), ("uT", kt)], w=[pk(pi)])
                                stt("dve", x_sb[:, dt2, t0:t0 + n], ps[pi][:, :n], mod(l, 2, dt2, j), x_sb[:, dt2, t0:t0 + n], ALU.mult, ALU.add,
                                    r=[pk(pi), "m_all", ("x", dt2, bi)], w=[("x", dt2, bi)])
                S.barrier()

        def ffn(l, b, last):
            blocks = [1, 2, 3, 4] if last else [0, 1, 2, 3, 4]
            wsrc = dr["w_dw"][l].rearrange("k (ct p) -> (k ct) p", p=128)
            wdf = wdw.rearrange("p k c -> p (k c)")
            for (r0_, rn_) in ((0, 64), (64, 64), (128, 4)):
                loadT(wdf[:, r0_:r0_ + rn_], wsrc[r0_:r0_ + rn_, :], rn_, "wdw")
            bsrc_ = dr["b_dw"][l].rearrange("(ct p) -> ct p", p=128)
            for (r0_, rn_) in ((0, 32), (32, 8), (40, 4)):
                loadT(bdw[:, r0_:r0_ + rn_], bsrc_[r0_:r0_ + rn_, :], rn_, "bdw")
            norm_to_h(l, b, 1, blocks)
            seqs = [(256, T)] if last else [(0, 256), (256, T)]
            with ExitStack() as ph:
                ua = ph.enter_context(sbt("fua", [128, T], F32))
                ug = ph.enter_context(sbt("fug", [128, T], F32))
                ca = ph.enter_context(sbt("fca", [128, T], F32))
                cg_ = ph.enter_context(sbt("fcg", [128, T], F32))
                actb = ph.enter_context(sbt("fact", [128, 2, T], BF16))
                lo_t = seqs[0][0]
                for grp in range(11):
                    sA = wslot()
                    Wa = wload(sA, 0, dr["w_up"][l][:, grp * 256:(grp + 1) * 256], 8, 256)
                    sG = wslot()
                    Wg = wload(sG, 0, dr["w_up"][l][:, 2816 + grp * 256:2816 + (grp + 1) * 256], 8, 256)
                    sD = wslot()
                    Wd = wload(sD, 0, dr["w_down"][l][grp * 256:(grp + 1) * 256, :], 2, 1024)
                    for ci in range(2):
                        cta = grp * 2 + ci
                        ctg = 22 + cta
                        for (Wx, wkx, ct, uu, cc, un, cn) in ((Wa, wk(sA), cta, ua, ca, "fua", "fca"), (Wg, wk(sG), ctg, ug, cg_, "fug", "fcg")):
                            for bi in blocks:
                                t0, n = BLK[bi]
                                pi = proj_fm(Wx[:, :, ci * 128:(ci + 1) * 128], wkx, bi)
                                act(cc[:, t0:t0 + n], ps[pi][:, :n], AF.Identity, r=[pk(pi), "wdw", "bdw"], pw=[cn],
                                    scale=wdw[:, 1, ct:ct + 1], bias=bdw[:, ct:ct + 1])
                                cp("dve", uu[:, t0:t0 + n], ps[pi][:, :n], r=[pk(pi)], pw=[un])
                            for (s0, s1_) in seqs:
                                stt("dve", cc[:, s0 + 1:s1_], uu[:, s0:s1_ - 1], wdw[:, 0, ct:ct + 1], cc[:, s0 + 1:s1_], ALU.mult, ALU.add,
                                    r=[un, cn, "wdw"], w=[cn])
                                stt("dve", cc[:, s0:s1_ - 1], uu[:, s0 + 1:s1_], wdw[:, 2, ct:ct + 1], cc[:, s0:s1_ - 1], ALU.mult, ALU.add,
                                    r=[un, cn, "wdw"], w=[cn])
                        act(cg_[:, lo_t:T], cg_[:, lo_t:T], AF.Silu, r=["fcg"], w=["fcg"])
                        tt("dve", actb[:, ci, lo_t:T], ca[:, lo_t:T], cg_[:, lo_t:T], ALU.mult, r=["fca", "fcg"], w=[("fact", ci)])
                    for dt in range(8):
                        for bi in blocks:
                            t0, n = BLK[bi]
                            j = nb if bi == 0 else b
                            pi = psn()
                            for ci in range(2):
                                mm(ps[pi][:, :n], Wd[:, ci, dt * 128:(dt + 1) * 128], actb[:, ci, t0:t0 + n], ci == 0, ci == 1,
                                   r=[wk(sD), ("fact", ci)], w=[pk(pi)])
                            stt("dve", x_sb[:, dt, t0:t0 + n], ps[pi][:, :n], mod(l, 5, dt, j), x_sb[:, dt, t0:t0 + n], ALU.mult, ALU.add,
                                r=[pk(pi), "m_all", ("x", dt, bi)], w=[("x", dt, bi)])
                S.barrier()

        def final_store(b):
            with ExitStack() as ph:
                stage = [ph.enter_context(sbt("ostg%d" % i, [128, 1024], F32)) for i in range(2)]
                xn = ph.enter_context(sbt("xnf", [128, 8, 512], F32))
                cnt = 0
                for bi in range(1, 5):
                    t0, n = BLK[bi]
                    pst = psn()
                    for dt in range(8):
                        sq = g[dt % 2]
                        act(sq[:, :n], x_sb[:, dt, t0:t0 + n], AF.Square, r=[("x", dt, bi)], w=[gk[dt % 2]])
                        mm(ps[pst][:, :n], onesf[:], sq[:, :n], dt == 0, dt == 7, r=["onesf", gk[dt % 2]], w=[pk(pst)])
                    act(g[2][:, :n], ps[pst][:, :n], AF.Sqrt, r=[pk(pst), "epsc"], w=[gk[2]], scale=1.0 / D, bias=epsc[:, 0:1])
                    recip(g[2][:, :n], g[2][:, :n], r=[gk[2]], w=[gk[2]])
                    for dt in range(8):
                        stt("dve", xn[:, dt, :], x_sb[:, dt, t0:t0 + n], nfin[:, dt:dt + 1], g[2][:, :n], ALU.mult, ALU.mult,
                            r=[("x", dt, bi), "nfin", gk[2]], w=[("xnf", dt)])
                    for qi in range(4):
                        st = stage[cnt % 2]
                        sk = "ostg%d" % (cnt % 2)
                        cnt += 1
                        for half in range(2):
                            pi = psn()
                            for q in range(4):
                                dt = half * 4 + q
                                tr(ps[pi][:, q * 128:(q + 1) * 128], xn[:, dt, qi * 128:(qi + 1) * 128], identf[:], r=[("xnf", dt), "identf"], w=[pk(pi)])
                            cp("dve" if half == 0 else "act", st[:, half * 512:(half + 1) * 512], ps[pi][:], r=[pk(pi)], pw=[sk])
                        tok = t0 - 256 + qi * 128
                        dma("sp", y_d[b, tok:tok + 128, :], st[:], r=[sk])
                S.barrier()

        for b in range(nb):
            load_x(b)
            for l in range(depth):
                last = l == LYR - 1
                S.epoch += 1
                layer_scalars(l, b)
                norm_to_h(l, b, 0, [0, 1, 2, 3, 4])
                if "h" in dump_names and b == 0 and l == 0:
                    dumpx("h", h_sb[:], [128, 8, T], [("h", dt, bi) for dt in range(8) for bi in range(5)])
                if "hgrn" in phases:
                    hgrn(l, b, last)
                if "attnA" in phases:
                    attn(l, b, 0, last)
                if "attnB" in phases:
                    attn(l, b, 1, last)
                if b == 0 and l == 0 and "ysc" in dump_names:
                    dumpx("ysc", ysc, [3 * 4 * 128 * (T // 2)], [("ysc", n_, k_, bi) for n_ in range(3) for k_ in range(4) for bi in range(5)])
                if "merge" in phases:
                    merge(l, b, last)
                if b == 0 and l == 0:
                    dumpx("xmix", x_sb[:], [128, 8, T], [("x", dt, bi) for dt in range(8) for bi in range(5)])
                if "ffn" in phases:
                    ffn(l, b, last)
                if b == 0 and l == 0:
                    dumpx("xffn", x_sb[:], [128, 8, T], [("x", dt, bi) for dt in range(8) for bi in range(5)])
            final_store(b)
        S.emit(nc)
    return nc, dump_d


_CONSTS = None


def run(inputs, nb=2, depth=4, dump=(), ncores=8, trace=False, phases=("hgrn", "attnA", "attnB", "merge", "ffn")):
    global _CONSTS
    if _CONSTS is None:
        _CONSTS = make_consts()
    nc, dump_d = build(nb=nb, depth=depth, dump=dump, phases=phases)
    shared = {}
    for name, _ in WNAMES:
        shared[name] = np.ascontiguousarray(np.asarray(inputs[name], dtype=np.float32))
    shared["c_ctx"] = np.ascontiguousarray(np.asarray(inputs["c_ctx"], dtype=np.float32))
    shared.update(_CONSTS)
    x = np.asarray(inputs["x"], dtype=np.float32)
    c = np.asarray(inputs["c"], dtype=np.float32)
    ctx = np.asarray(inputs["ctx"], dtype=np.float32)
    in_maps = []
    for core in range(ncores):
        m = dict(shared)
        m["x"] = np.ascontiguousarray(x[core * nb:(core + 1) * nb])
        m["c"] = np.ascontiguousarray(c[core * nb:(core + 1) * nb])
        m["ctx"] = np.ascontiguousarray(ctx[core * nb:(core + 1) * nb])
        in_maps.append(m)
    res = run_bass_kernel_spmd(nc, in_maps, core_ids=list(range(ncores)), trace=trace)
    return res


def kernel(**inputs):
    res = run(inputs, nb=2, depth=4, ncores=8)
    out = np.concatenate([np.asarray(r["y"]) for r in res.results], axis=0)
    return out.astype(np.float32)
```
